# Optimizing a Trainium2 kernel written in Bass

```python
import math
import jax, jax.numpy as jnp
from jax import lax
import numpy as np

D_MODEL = 1024
BATCH = 4
SEQ = 4096
DEPTH = 4
DEC_BATCH = 32
DEC_SEQ = 8
PAST_LEN = 8192
PAGE_SIZE = 128

N_A_LAYERS = DEPTH // 2
N_B_LAYERS = DEPTH - N_A_LAYERS
SSM_WIDTH = D_MODEL
SSM_GROUP = 16
N_GROUPS = SSM_WIDTH // SSM_GROUP
SSM_STATE = 64
N_HEADS = 16
HEAD_DIM = 64
ATT_WIDTH = N_HEADS * HEAD_DIM
Q_BLOCK = 128
RMS_EPS = 1e-6
DT_MIN = 0.001
DT_MAX = 0.1
SB_BIAS_INIT = -12.0

kernel_name = 'yoco_s5_stickbreaking_step'


def _rms_norm(x, g):
    x32 = x.astype(jnp.float32)
    y = x32 * lax.rsqrt(jnp.mean(x32 * x32, axis=-1, keepdims=True) + RMS_EPS)
    return (y * g.astype(jnp.float32)).astype(x.dtype)


def _ssm_combine(e1, e2):
    a1r, a1i, b1r, b1i = e1
    a2r, a2i, b2r, b2i = e2
    return (a2r * a1r - a2i * a1i,
            a2r * a1i + a2i * a1r,
            a2r * b1r - a2i * b1i + b2r,
            a2r * b1i + a2i * b1r + b2i)


def _s5_scan(u, h0_re, h0_im, log_dt, a_re, a_im, b_re, b_im, c_re, c_im, d_skip):
    f32 = jnp.float32
    n, t, _ = u.shape
    u32 = u.astype(f32).reshape(n, t, N_GROUPS, SSM_GROUP)
    a_re = a_re.astype(f32)
    a_im = a_im.astype(f32)
    dt = jnp.exp(log_dt.astype(f32))[:, None]
    mag = jnp.exp(dt * a_re)
    lam_re = mag * jnp.cos(dt * a_im)
    lam_im = mag * jnp.sin(dt * a_im)
    den = a_re * a_re + a_im * a_im
    w_re = ((lam_re - 1.0) * a_re + lam_im * a_im) / den
    w_im = (lam_im * a_re - (lam_re - 1.0) * a_im) / den
    r = jnp.einsum('ntgc,gpc->ntgp', u32, b_re.astype(f32))
    i = jnp.einsum('ntgc,gpc->ntgp', u32, b_im.astype(f32))
    bu_re = w_re * r - w_im * i
    bu_im = w_re * i + w_im * r
    if h0_re is not None:
        h0_re = h0_re.astype(f32)
        h0_im = h0_im.astype(f32)
        bu_re = bu_re.at[:, 0].add(lam_re * h0_re - lam_im * h0_im)
        bu_im = bu_im.at[:, 0].add(lam_re * h0_im + lam_im * h0_re)
    lr = jnp.broadcast_to(lam_re, bu_re.shape)
    li = jnp.broadcast_to(lam_im, bu_im.shape)
    _, _, h_re, h_im = lax.associative_scan(_ssm_combine, (lr, li, bu_re, bu_im), axis=1)
    y = (jnp.einsum('ntgp,gcp->ntgc', h_re, c_re.astype(f32))
         - jnp.einsum('ntgp,gcp->ntgc', h_im, c_im.astype(f32)))
    y = y.reshape(n, t, SSM_WIDTH) + d_skip.astype(f32) * u32.reshape(n, t, SSM_WIDTH)
    return y.astype(u.dtype), h_re[:, -1], h_im[:, -1]


def _s5_layer(x, h0_re, h0_im, g_pre, g_post, w_in, log_dt, a_re, a_im, b_re, b_im,
              c_re, c_im, d_skip, w_glu, b_glu, w_out):
    h = _rms_norm(x, g_pre)
    uz = h @ w_in
    u, z = uz[..., :SSM_WIDTH], uz[..., SSM_WIDTH:]
    y, hr, hi = _s5_scan(u, h0_re, h0_im, log_dt, a_re, a_im, b_re, b_im, c_re, c_im, d_skip)
    y = jax.nn.gelu(y)
    y = y * jax.nn.sigmoid(y @ w_glu + b_glu)
    y = y * jax.nn.silu(z)
    return x + _rms_norm(y @ w_out, g_post), hr, hi


def _stick_break_weights(z, mask):
    log_fail = jnp.where(mask, jax.nn.log_sigmoid(-z), 0.0)
    later = lax.cumsum(log_fail, axis=z.ndim - 1, reverse=True) - log_fail
    return jnp.where(mask, jnp.exp(jax.nn.log_sigmoid(z) + later), 0.0)


def _stick_break_prompt(q, k, v, bias):
    b, s, h, dh = q.shape
    nb = s // Q_BLOCK
    scale = HEAD_DIM ** -0.5
    bias32 = bias.astype(jnp.float32)[None, :, None, None]
    k_pos = jnp.arange(s)
    q_blocks = q.reshape(b, nb, Q_BLOCK, h, dh).transpose(1, 0, 2, 3, 4)
    starts = jnp.arange(nb) * Q_BLOCK

    def block(args):
        qb, start = args
        z = jnp.einsum('bqhd,bkhd->bhqk', qb, k, preferred_element_type=jnp.float32) * scale + bias32
        q_pos = start + jnp.arange(Q_BLOCK)
        mask = k_pos[None, :] < q_pos[:, None]
        a = _stick_break_weights(z, mask)
        return jnp.einsum('bhqk,bkhd->bqhd', a.astype(v.dtype), v)

    o = lax.map(block, (q_blocks, starts))
    return o.transpose(1, 0, 2, 3, 4).reshape(b, s, h * dh)


def _stick_break_sample(q, k_past, v_past, k_new, v_new, bias):
    n, t, h, dh = q.shape
    past = k_past.shape[1]
    scale = HEAD_DIM ** -0.5
    bias32 = bias.astype(jnp.float32)[None, :, None, None]
    z = jnp.concatenate([
        jnp.einsum('bqhd,bkhd->bhqk', q, k_past, preferred_element_type=jnp.float32),
        jnp.einsum('bqhd,bkhd->bhqk', q, k_new, preferred_element_type=jnp.float32)], axis=-1) * scale + bias32
    q_pos = past + jnp.arange(t)
    k_pos = jnp.arange(past + t)
    mask = k_pos[None, :] < q_pos[:, None]
    a = _stick_break_weights(z, mask)
    o = (jnp.einsum('bhqk,bkhd->bqhd', a[..., :past].astype(v_past.dtype), v_past)
         + jnp.einsum('bhqk,bkhd->bqhd', a[..., past:].astype(v_new.dtype), v_new))
    return o.reshape(n, t, h * dh)


def _shared_kv(x, g, w_kv):
    h = _rms_norm(x, g)
    kv = h @ w_kv
    n, t, _ = x.shape
    k = kv[..., :ATT_WIDTH].reshape(n, t, N_HEADS, HEAD_DIM)
    v = kv[..., ATT_WIDTH:].reshape(n, t, N_HEADS, HEAD_DIM)
    return k, v


def _sb_layer(x, attend, g_pre, g_post, w_in, w_out):
    h = _rms_norm(x, g_pre)
    qg = h @ w_in
    n, t, _ = x.shape
    q = qg[..., :ATT_WIDTH].reshape(n, t, N_HEADS, HEAD_DIM)
    g = qg[..., ATT_WIDTH:]
    o = attend(q)
    return x + _rms_norm((o * jax.nn.silu(g)) @ w_out, g_post)


def setup_inputs(seed: int = 0) -> dict:
    key = jax.random.key(seed)
    ks = jax.random.split(key, 28)
    f32 = jnp.float32
    n_pages = PAST_LEN // PAGE_SIZE
    n_used = DEC_BATCH * n_pages
    n_phys = n_used + max(1, n_used // 4)
    page_table = jax.random.permutation(ks[6], n_phys)[:n_used].reshape(DEC_BATCH, n_pages).astype(jnp.int32)
    na, nbl = N_A_LAYERS, N_B_LAYERS

    def nrm(k, shape, scale):
        return jax.random.normal(k, shape, f32) * scale

    a_im0 = jnp.pi * jnp.arange(SSM_STATE, dtype=f32)
    return {
        'x_prompt': nrm(ks[0], (BATCH, SEQ, D_MODEL), 1.0),
        'x_sample': nrm(ks[1], (DEC_BATCH, DEC_SEQ, D_MODEL), 1.0),
        'state_ssm_re': nrm(ks[2], (na, DEC_BATCH, N_GROUPS, SSM_STATE), 0.5),
        'state_ssm_im': nrm(ks[3], (na, DEC_BATCH, N_GROUPS, SSM_STATE), 0.5),
        'cache_k': nrm(ks[4], (n_phys, PAGE_SIZE, N_HEADS, HEAD_DIM), 1.0),
        'cache_v': nrm(ks[5], (n_phys, PAGE_SIZE, N_HEADS, HEAD_DIM), 1.0),
        'page_table': page_table,
        'a_norm_pre': 1.0 + nrm(ks[7], (na, D_MODEL), 0.01),
        'a_norm_post': 1.0 + nrm(ks[8], (na, D_MODEL), 0.01),
        'a_w_in': nrm(ks[9], (na, D_MODEL, 2 * SSM_WIDTH), D_MODEL ** -0.5),
        'a_log_dt': jax.random.uniform(ks[10], (na, N_GROUPS), f32, math.log(DT_MIN), math.log(DT_MAX)),
        'a_A_re': -0.5 + nrm(ks[11], (na, N_GROUPS, SSM_STATE), 0.01),
        'a_A_im': a_im0 + nrm(ks[12], (na, N_GROUPS, SSM_STATE), 0.01),
        'a_B_re': nrm(ks[13], (na, N_GROUPS, SSM_STATE, SSM_GROUP), (2 * SSM_GROUP) ** -0.5),
        'a_B_im': nrm(ks[14], (na, N_GROUPS, SSM_STATE, SSM_GROUP), (2 * SSM_GROUP) ** -0.5),
        'a_C_re': nrm(ks[15], (na, N_GROUPS, SSM_GROUP, SSM_STATE), (2 * SSM_STATE) ** -0.5),
        'a_C_im': nrm(ks[16], (na, N_GROUPS, SSM_GROUP, SSM_STATE), (2 * SSM_STATE) ** -0.5),
        'a_D': nrm(ks[17], (na, SSM_WIDTH), 1.0),
        'a_w_glu': nrm(ks[18], (na, SSM_WIDTH, SSM_WIDTH), SSM_WIDTH ** -0.5),
        'a_b_glu': nrm(ks[19], (na, SSM_WIDTH), 0.01),
        'a_w_out': nrm(ks[20], (na, SSM_WIDTH, D_MODEL), SSM_WIDTH ** -0.5),
        'kv_norm': 1.0 + nrm(ks[21], (D_MODEL,), 0.01),
        'w_kv': nrm(ks[22], (D_MODEL, 2 * ATT_WIDTH), D_MODEL ** -0.5),
        'b_norm_pre': 1.0 + nrm(ks[23], (nbl, D_MODEL), 0.01),
        'b_norm_post': 1.0 + nrm(ks[24], (nbl, D_MODEL), 0.01),
        'b_w_in': nrm(ks[25], (nbl, D_MODEL, 2 * ATT_WIDTH), D_MODEL ** -0.5),
        'b_logit_bias': SB_BIAS_INIT + nrm(ks[27], (nbl, N_HEADS), 0.5),
        'b_w_out': nrm(ks[26], (nbl, ATT_WIDTH, D_MODEL), ATT_WIDTH ** -0.5),
    }


def reference(x_prompt, x_sample, state_ssm_re, state_ssm_im, cache_k, cache_v, page_table,
              a_norm_pre, a_norm_post, a_w_in, a_log_dt, a_A_re, a_A_im, a_B_re, a_B_im,
              a_C_re, a_C_im, a_D, a_w_glu, a_b_glu, a_w_out, kv_norm, w_kv,
              b_norm_pre, b_norm_post, b_w_in, b_logit_bias, b_w_out):
    xp, xs = x_prompt, x_sample
    p_re, p_im, s_re, s_im = [], [], [], []
    for layer in range(DEPTH):
        if layer < N_A_LAYERS:
            i = layer
            prm = (a_norm_pre[i], a_norm_post[i], a_w_in[i], a_log_dt[i], a_A_re[i], a_A_im[i],
                   a_B_re[i], a_B_im[i], a_C_re[i], a_C_im[i], a_D[i], a_w_glu[i], a_b_glu[i], a_w_out[i])
            xp, hr, hi = _s5_layer(xp, None, None, *prm)
            xs, sr, si = _s5_layer(xs, state_ssm_re[i], state_ssm_im[i], *prm)
            p_re.append(hr)
            p_im.append(hi)
            s_re.append(sr)
            s_im.append(si)
        else:
            if layer == N_A_LAYERS:
                k_p, v_p = _shared_kv(xp, kv_norm, w_kv)
                k_s, v_s = _shared_kv(xs, kv_norm, w_kv)
                n_seq = page_table.shape[0]
                k_past = cache_k[page_table].reshape(n_seq, -1, N_HEADS, HEAD_DIM)
                v_past = cache_v[page_table].reshape(n_seq, -1, N_HEADS, HEAD_DIM)
            j = layer - N_A_LAYERS
            bias_j = b_logit_bias[j]
            xp = _sb_layer(xp, lambda q: _stick_break_prompt(q, k_p, v_p, bias_j),
                           b_norm_pre[j], b_norm_post[j], b_w_in[j], b_w_out[j])
            xs = _sb_layer(xs, lambda q: _stick_break_sample(q, k_past, v_past, k_s, v_s, bias_j),
                           b_norm_pre[j], b_norm_post[j], b_w_in[j], b_w_out[j])
    return (xp, xs, jnp.stack(p_re), jnp.stack(p_im), k_p, v_p,
            jnp.stack(s_re), jnp.stack(s_im), k_s, v_s)
```

```python
import math
import contextlib
import numpy as np
import ml_dtypes
import concourse.bass as bass
import concourse.mybir as mybir
from concourse.bass_utils import run_bass_kernel_spmd

F32 = mybir.dt.float32
BF16 = mybir.dt.bfloat16
I32 = mybir.dt.int32
AF = mybir.ActivationFunctionType
ALU = mybir.AluOpType

NDMA_LANES = 6
DBG = {}
TWO_PI = 2.0 * math.pi
NE = 17


import types


def _freeze(fn):
    if fn is None or fn.__closure__ is None:
        return fn
    cells = []
    for c in fn.__closure__:
        try:
            cells.append(types.CellType(c.cell_contents))
        except ValueError:
            cells.append(c)
    return types.FunctionType(fn.__code__, fn.__globals__, fn.__name__, fn.__defaults__, tuple(cells))


class Sched:
    PHYS = ("pe", "act", "dve", "pool", "sp")

    def __init__(self, nc):
        self.nc = nc
        self.ops = {p: [] for p in self.PHYS}
        self.lanes = {}
        self.seen = {p: {} for p in self.PHYS}
        self.lastw = {}
        self.readers = {}
        self.dma_i = {p: 0 for p in self.PHYS}
        self.nops = 0

    def op(self, phys, fn, reads=(), writes=(), dma=False):
        fn = _freeze(fn)
        if dma:
            lane = "%s_q%d" % (phys, self.dma_i[phys] % NDMA_LANES)
            self.dma_i[phys] += 1
        else:
            lane = phys
        if lane not in self.lanes:
            self.lanes[lane] = 0
        waits = {}
        seen = self.seen[phys]

        def need(l2, v):
            if seen.get(l2, 0) < v and waits.get(l2, 0) < v:
                waits[l2] = v

        if dma and self.lanes[lane] > 0:
            need(lane, self.lanes[lane])
        for k in reads:
            if k in self.lastw:
                need(*self.lastw[k])
        for k in writes:
            if k in self.lastw:
                need(*self.lastw[k])
            for l2, v in self.readers.get(k, {}).items():
                need(l2, v)
        for l2, v in waits.items():
            seen[l2] = v
        inc = 16 if dma else 1
        self.lanes[lane] += inc
        val = self.lanes[lane]
        self.ops[phys].append((sorted(waits.items()), fn, lane, inc))
        for k in writes:
            self.lastw[k] = (lane, val)
            self.readers[k] = {}
        for k in reads:
            self.readers.setdefault(k, {})[lane] = val
        self.nops += 1
        return val

    def barrier(self):
        cur = sorted(self.lanes.items())
        for p in self.PHYS:
            w = [(l, v) for l, v in cur if self.seen[p].get(l, 0) < v]
            for l, v in w:
                self.seen[p][l] = v
            if w:
                self.ops[p].append((w, None, None, 0))
        self.lastw = {}
        self.readers = {}

    def finish(self):
        self.barrier()

    def emit(self):
        nc = self.nc
        with contextlib.ExitStack() as st:
            sems = {}
            for lane in self.lanes:
                sems[lane] = st.enter_context(nc.semaphore("s_" + lane))
            block = st.enter_context(nc.Block())

            def replay(phys, eng):
                for waits, fn, lane, inc in self.ops[phys]:
                    for l2, v in waits:
                        if v > 0:
                            eng.wait_ge(sems[l2], v)
                    if fn is not None:
                        ins = fn(eng)
                        ins.then_inc(sems[lane], inc)

            @block.tensor
            def _(e):
                replay("pe", e)

            @block.scalar
            def _(e):
                replay("act", e)

            @block.vector
            def _(e):
                replay("dve", e)

            @block.gpsimd
            def _(e):
                replay("pool", e)

            @block.sync
            def _(e):
                replay("sp", e)


class Arena:
    def __init__(self, t, nwords):
        self.t = t
        self.n = nwords
        self.off = 0
        self.peak = 0

    def mark(self):
        return self.off

    def release(self, m):
        self.off = m

    def f32(self, shape):
        n = int(np.prod(shape))
        assert self.off + n <= self.n, ("arena overflow", self.off, n, self.n)
        ap = self.t[:, self.off:self.off + n]
        self.off += n
        self.peak = max(self.peak, self.off)
        return self._shape(ap, shape)

    def i32(self, shape):
        n = int(np.prod(shape))
        assert self.off + n <= self.n, ("arena overflow", self.off, n, self.n)
        ap = self.t[:, self.off:self.off + n].bitcast(I32)
        self.off += n
        self.peak = max(self.peak, self.off)
        return self._shape(ap, shape)

    def bf(self, shape):
        n = int(np.prod(shape))
        w = (n + 1) // 2
        assert self.off + w <= self.n, ("arena overflow", self.off, w, self.n)
        ap = self.t[:, self.off:self.off + w].bitcast(BF16)
        if 2 * w != n:
            ap = ap[:, 0:n]
        self.off += w
        self.peak = max(self.peak, self.off)
        return self._shape(ap, shape)

    @staticmethod
    def _shape(ap, shape):
        if len(shape) == 1:
            return ap
        if len(shape) == 2:
            return ap.rearrange("p (a b) -> p a b", a=shape[0])
        if len(shape) == 3:
            return ap.rearrange("p (a b c) -> p a b c", a=shape[0], b=shape[1])
        if len(shape) == 4:
            return ap.rearrange("p (a b c d) -> p a b c d", a=shape[0], b=shape[1], c=shape[2])
        raise ValueError(shape)


def flat(ap):
    nd = len(ap.shape)
    if nd == 2:
        return ap
    if nd == 3:
        return ap.rearrange("p a b -> p (a b)")
    if nd == 4:
        return ap.rearrange("p a b c -> p (a b c)")
    if nd == 5:
        return ap.rearrange("p a b c d -> p (a b c d)")
    raise ValueError


class Cfg:
    def __init__(self, SEQ=4096, NPG=64, stages=("s5", "kv", "sb"), n_s5=2, n_sb=2, NPHYS=2560, QG=4, sample=True, sample_attn=True):
        self.NPHYS = NPHYS
        self.QG = QG
        self.sample = sample
        self.sample_attn = sample_attn
        self.SEQ = SEQ
        self.NT = SEQ // 128
        self.NCHP = SEQ // 8
        self.NB = self.NCHP // 128
        self.NPG = NPG
        self.NSLOT = self.NT // 2
        self.stages = stages
        self.n_s5 = n_s5
        self.n_sb = n_sb
        assert self.NCHP % 128 == 0 and self.NCHP <= 512


def build(cfg):
    nc = bass.Bass("TRN2", target_bir_lowering=False)
    SEQ, NT, NCHP, NB = cfg.SEQ, cfg.NT, cfg.NCHP, cfg.NB

    def din(name, shape, dt=F32):
        return nc.dram_tensor(name, list(shape), dt, kind="ExternalInput").ap()

    def dout(name, shape, dt=F32):
        return nc.dram_tensor(name, list(shape), dt, kind="ExternalOutput").ap()

    def dscr(name, shape, dt=F32):
        return nc.dram_tensor(name, list(shape), dt, kind="Internal").ap()

    xp = din("xp", [SEQ, 1024])
    xs = din("xs", [128, 1024])
    s5pA = din("s5pA", [2, 128, 3, 4096])
    s5pB = din("s5pB", [2, 128, 2, 4096])
    s5pAT = din("s5pAT", [2, 128, 3, 64])
    s5pBT = din("s5pBT", [2, 128, 2, 64 * 16])
    s5pCT = din("s5pCT", [2, 128, 2, 64 * 16])
    s5cols = din("s5cols", [2, 128, 24])
    s5gpost = din("s5gpost", [2, 128, 1024])
    s5h0 = din("s5h0", [2, 128, 2, 256])
    a_w_in = din("a_w_in", [2, 1024, 2048])
    a_w_glu = din("a_w_glu", [2, 1024, 1024])
    a_w_out = din("a_w_out", [2, 1024, 1024])
    c_ident = din("c_ident", [128, 128], BF16)
    c_mask4 = din("c_mask4", [128, 512])
    c_pswap = din("c_pswap", [128, 128])
    c_cols = din("c_cols", [128, 4])
    c_iota = din("c_iota", [128, 512])
    c_evals = din("c_evals", [128, NE * 64])

    w_kv = din("w_kv", [1024, 2048])
    b_w_in = din("b_w_in", [2, 1024, 2048])
    b_w_out = din("b_w_out", [2, 1024, 1024])
    sbcols = din("sbcols", [128, 24])
    sbgpost = din("sbgpost", [2, 128, 1024])
    sbbias = din("sbbias", [2, 128, 16])
    sbbiasrow = din("sbbiasrow", [2, 128, 128])
    c_core = din("c_core", [128, 4])
    c_attn = din("c_attn", [128, 6, 128], BF16)
    c_identf = din("c_identf", [128, 128])
    c_onesrow = din("c_onesrow", [1, 128])
    NPHYS = cfg.NPHYS
    if cfg.sample_attn:
        cache_k = din("cache_k", [NPHYS * 128, 1024])
        cache_v = din("cache_v", [NPHYS * 128, 1024])
        ptab = din("ptab", [128, 4 * cfg.NPG], I32)
    k_p = dout("k_p", [SEQ, 1024])
    v_p = dout("v_p", [SEQ, 1024])
    k_s = dout("k_s", [128, 1024])
    v_s = dout("v_s", [128, 1024])
    yq_p = dout("yq_p", [cfg.NSLOT * 128, 1024])
    xq = dscr("xq", [cfg.NSLOT * 128 + 128, 1024])
    y_p = dout("y_p", [SEQ, 1024]) if cfg.stages == ("s5",) else None
    y_s = dout("y_s", [128, 1024])
    fin_p = dout("fin_p", [2, 128, 64])
    fin_s = dout("fin_s", [2, 128, 256])

    xa = dscr("xa", [SEQ + 128, 1024])
    xb = dscr("xb", [SEQ + 128, 1024])
    scrU = dscr("scrU", [8, 16, 8, NCHP], BF16)
    scrY = dscr("scrY", [8, 16, 8, NCHP], BF16)
    scrUs = dscr("scrUs", [8, 16, 8, 4], BF16)
    scrYs = dscr("scrYs", [8, 16, 8, 4], BF16)

    with contextlib.ExitStack() as st:
        NW = 52600
        idx_t = st.enter_context(nc.sbuf_tensor("idx_t", [128, 4 * cfg.NPG], I32))
        arena_t = st.enter_context(nc.sbuf_tensor("arena", [128, NW], F32))
        A = Arena(arena_t, NW)
        pbank = [st.enter_context(nc.psum_tensor("pb%d" % i, [128, 512], F32)) for i in range(7)]
        ptr = st.enter_context(nc.psum_tensor("ptr", [128, 8, 128], BF16))
        S = Sched(nc)

        def dma(out, in_, reads=(), writes=()):
            S.op("sp", lambda e: e.dma_start(out=out, in_=in_), reads=reads, writes=writes, dma=True)

        def tt(eng, out, a, b, op, reads, writes):
            S.op(eng, lambda e: e.tensor_tensor(out=out, in0=a, in1=b, op=op), reads=reads, writes=writes)

        def ts(eng, out, a, s1, s2, op0, op1, reads, writes):
            if s2 is None:
                S.op(eng, lambda e: e.tensor_scalar(out=out, in0=a, scalar1=s1, scalar2=None, op0=op0), reads=reads, writes=writes)
            else:
                S.op(eng, lambda e: e.tensor_scalar(out=out, in0=a, scalar1=s1, scalar2=s2, op0=op0, op1=op1), reads=reads, writes=writes)

        def stt(eng, out, a, sc, b, op0, op1, reads, writes):
            S.op(eng, lambda e: e.scalar_tensor_tensor(out=out, in0=a, scalar=sc, in1=b, op0=op0, op1=op1), reads=reads, writes=writes)

        def cp(eng, out, a, reads, writes):
            S.op(eng, lambda e: e.tensor_copy(out=out, in_=a), reads=reads, writes=writes)

        def act(out, a, func, reads, writes, **kw):
            S.op("act", lambda e: e.activation(out=out, in_=a, func=func, **kw), reads=reads, writes=writes)

        def mm(out, pairs, reads, writes):
            def fn(e):
                n = len(pairs)
                ins = None
                for i, (l, r) in enumerate(pairs):
                    ins = e.matmul(out, lhsT=l, rhs=r, start=(i == 0), stop=(i == n - 1))
                return ins
            S.op("pe", fn, reads=reads, writes=writes)

        def mm_multi(groups, reads, writes):
            def fn(e):
                ins = None
                for out, pairs in groups:
                    n = len(pairs)
                    for i, (l, r) in enumerate(pairs):
                        ins = e.matmul(out, lhsT=l, rhs=r, start=(i == 0), stop=(i == n - 1))
                return ins
            S.op("pe", fn, reads=reads, writes=writes)

        ident = A.bf([128])
        mask4 = A.f32([512])
        pswap = A.f32([128])
        ccols = A.f32([4])
        iota = A.f32([512])
        dma(ident, c_ident[:, :], writes=["ident"])
        dma(mask4, c_mask4[:, :], writes=["mask4"])
        dma(pswap, c_pswap[:, :], writes=["pswap"])
        dma(ccols, c_cols[:, :], writes=["ccols"])
        dma(iota, c_iota[:, :], writes=["iota"])
        esc = ccols[:, 0:1]
        sgnA = ccols[:, 1:2]
        nsgnA = ccols[:, 2:3]
        CONST_KEYS = ["ident", "mask4", "pswap", "ccols", "iota"]

        def phase_barrier():
            S.barrier()
            for k in CONST_KEYS:
                S.lastw.pop(k, None)

        def norm_T(src_ap, xt, hb, hT, tag, sscol, junk):
            if src_ap is not None:
                dma(xt, src_ap, writes=["xt" + tag])
            act(junk, xt, AF.Square, reads=["xt" + tag], writes=["junk", "ss" + tag], accum_out=sscol[:, 0:1])
            ts("dve", sscol[:, 1:2], sscol[:, 0:1], 1.0 / 1024.0, 1e-6, ALU.mult, ALU.add, ["ss" + tag], ["ss1" + tag])
            act(sscol[:, 2:3], sscol[:, 1:2], AF.Sqrt, reads=["ss1" + tag], writes=["ss2" + tag])
            S.op("dve", lambda e: e.reciprocal(out=sscol[:, 3:4], in_=sscol[:, 2:3]), reads=["ss2" + tag], writes=["ss3" + tag])
            S.op("act", lambda e: e.mul(out=hb, in_=xt, mul=sscol[:, 3:4]), reads=["xt" + tag, "ss3" + tag], writes=["hb" + tag])

            def tp(e):
                ins = None
                for k in range(8):
                    ins = e.transpose(ptr[:, k, :], hb[:, k * 128:(k + 1) * 128], ident)
                return ins
            S.op("pe", tp, reads=["hb" + tag, "ident"], writes=["ptr"])
            cp("dve", hT, ptr[:, :, :], ["ptr"], ["hT" + tag])

        def load_weight(dst_bf, w_dram, ncols, gcol, stage, key):
            for k in range(8):
                q = k % 2
                dma(stage[q][:, 0:ncols], w_dram[k * 128:(k + 1) * 128, :], writes=["wst%d" % q])
                if gcol is None:
                    cp("pool", dst_bf[:, k, :], stage[q][:, 0:ncols], ["wst%d" % q], [key])
                else:
                    ts("pool", dst_bf[:, k, :], stage[q][:, 0:ncols], gcol[:, k:k + 1], None, ALU.mult, None, ["wst%d" % q, "s5cols"], [key])

        def s5_layer(l, src_p, src_s, dst_p, dst_s):
            m0 = A.mark()
            UY = A.bf([8, 8, NCHP])
            UYs = A.bf([8, 8, 4])
            cols = A.f32([24])
            dma(cols, s5cols[l], writes=["s5cols"])
            gpre, Dcol, bglu = cols[:, 0:8], cols[:, 8:16], cols[:, 16:24]
            fin = A.f32([64])
            fins = A.f32([64, 4])
            h0 = A.f32([2, 256])
            h0b = A.bf([64, 4])
            dma(h0, s5h0[l], writes=["h0"])
            cp("pool", flat(h0b), h0[:, 0, :], ["h0"], ["h0b"])
            m1 = A.mark()
            xt = [A.f32([1024]) for _ in range(2)]
            hb = [A.bf([1024]) for _ in range(2)]
            hT = [A.bf([8, 128]) for _ in range(2)]
            sscol = [A.f32([4]) for _ in range(2)]
            junk = A.bf([1024])

            wA = A.bf([8, 1024])
            stage = [A.f32([1024]) for _ in range(2)]
            load_weight(wA, a_w_in[l][:, 0:1024], 1024, gpre, stage, "wA")
            for i in range(NT + 1):
                q = i % 2
                tag = str(q)
                src = src_p[i * 128:(i + 1) * 128, :] if i < NT else src_s
                norm_T(src, xt[q], hb[q], hT[q], tag, sscol[q], junk)
                for half in range(2):
                    pb = pbank[half * 2 + q]
                    groups = []
                    for jj in range(4):
                        j = half * 4 + jj
                        groups.append((pb[:, jj * 128:(jj + 1) * 128],
                                       [(wA[:, k, j * 128:(j + 1) * 128], hT[q][:, k, :]) for k in range(8)]))
                    mm_multi(groups, ["wA", "hT" + tag], ["pb%d" % (half * 2 + q)])
                    pv = pb[:, :].rearrange("p (j m) -> p j m", j=4)
                    if i < NT:
                        outv = UY[:, half * 4:half * 4 + 4, :, i * 16:(i + 1) * 16]
                        inv = pv.rearrange("p j (n s) -> p j s n", s=8)
                        S.op("act", lambda e, outv=outv, inv=inv: e.copy(out=outv, in_=inv),
                             reads=["pb%d" % (half * 2 + q)], writes=["UY"])
                    else:
                        outv = UYs[:, half * 4:half * 4 + 4, :, :]
                        inv = pv.rearrange("p j (i r) -> p j r i", r=32)[:, :, 0:8, :]
                        S.op("act", lambda e, outv=outv, inv=inv: e.copy(out=outv, in_=inv),
                             reads=["pb%d" % (half * 2 + q)], writes=["UYs"])
            A.release(m1)
            phase_barrier()

            m2 = A.mark()
            pAT = A.f32([3, 64])
            pBT = A.f32([2, 64, 16])
            pCT = A.f32([2, 64, 16])
            dma(pAT, s5pAT[l], writes=["pAT"])
            dma(flat(pBT), s5pBT[l].rearrange("p a b -> p (a b)"), writes=["pBT"])
            dma(flat(pCT), s5pCT[l].rearrange("p a b -> p (a b)"), writes=["pCT"])
            PEr = A.f32([NE, 64])
            PEi = A.f32([NE, 64])
            magE = A.f32([NE, 64])
            sm = [A.f32([64]) for _ in range(10)]
            Ewr = A.f32([8, 64])
            Ewi = A.f32([8, 64])
            Prs = A.f32([9, 64])
            f8 = A.f32([64])
            mT = A.mark()
            evals = A.f32([NE, 64])
            dma(flat(evals), c_evals[:, :], writes=["evals"])
            tE = [A.f32([NE, 64]) for _ in range(3)]
            tEi = A.i32([NE, 64])
            Ewt = A.f32([8, 64])
            ArT, AiT, LdT = pAT[:, 0, :], pAT[:, 1, :], pAT[:, 2, :]

            def bcE(v):
                return v.unsqueeze(1).broadcast_to([128, NE, 64])

            act(sm[0], LdT, AF.Exp, ["pAT"], ["sm0"])
            tt("dve", sm[1], sm[0], ArT, ALU.mult, ["sm0", "pAT"], ["sm1"])
            stt("dve", sm[2], sm[0], 1.0 / TWO_PI, AiT, ALU.mult, ALU.mult, ["sm0", "pAT"], ["sm2"])
            tt("dve", tE[0], evals, bcE(sm[1]), ALU.mult, ["evals", "sm1"], ["tE0"])
            act(magE, tE[0], AF.Exp, ["tE0"], ["magE"])
            tt("dve", tE[0], evals, bcE(sm[2]), ALU.mult, ["evals", "sm2", "magE"], ["tE0"])
            cp("dve", tEi, tE[0], ["tE0"], ["tEi"])
            tt("dve", tE[1], tE[0], tEi, ALU.subtract, ["tE0", "tEi"], ["tE1"])
            ts("dve", tE[0], tE[0], 0.25, None, ALU.add, None, ["tE0", "tE1"], ["tE0"])
            cp("dve", tEi, tE[0], ["tE0", "tE1"], ["tEi"])
            tt("dve", tE[2], tE[0], tEi, ALU.subtract, ["tE0", "tEi"], ["tE2"])
            act(tE[1], tE[1], AF.Sin, ["tE1"], ["tE1"], scale=TWO_PI)
            act(tE[2], tE[2], AF.Sin, ["tE2"], ["tE2"], scale=TWO_PI)
            tt("dve", PEr, magE, tE[2], ALU.mult, ["magE", "tE2"], ["PEr"])
            tt("dve", PEi, magE, tE[1], ALU.mult, ["magE", "tE1"], ["PEi"])
            PK = ["PEr", "PEi"]
            lr1, li1 = PEr[:, 9, :], PEi[:, 9, :]
            ts("dve", sm[3], lr1, -1.0, None, ALU.add, None, PK, ["sm3"])
            tt("dve", sm[4], ArT, ArT, ALU.mult, ["pAT"], ["sm4"])
            tt("dve", sm[5], AiT, AiT, ALU.mult, ["pAT"], ["sm5"])
            tt("dve", sm[4], sm[4], sm[5], ALU.add, ["sm4", "sm5"], ["sm4"])
            S.op("dve", lambda e: e.reciprocal(out=sm[4], in_=sm[4]), reads=["sm4"], writes=["sm4"])
            tt("dve", sm[5], sm[3], ArT, ALU.mult, ["sm3", "pAT"], ["sm5"])
            tt("dve", sm[6], li1, AiT, ALU.mult, PK + ["pAT"], ["sm6"])
            tt("dve", sm[5], sm[5], sm[6], ALU.add, ["sm5", "sm6"], ["sm5"])
            tt("dve", sm[7], sm[5], sm[4], ALU.mult, ["sm5", "sm4"], ["sm7"])
            tt("dve", sm[5], li1, ArT, ALU.mult, PK + ["pAT", "sm7"], ["sm5"])
            tt("dve", sm[6], sm[3], AiT, ALU.mult, ["sm3", "pAT", "sm5"], ["sm6"])
            tt("dve", sm[5], sm[5], sm[6], ALU.subtract, ["sm5", "sm6"], ["sm5"])
            tt("dve", sm[8], sm[5], sm[4], ALU.mult, ["sm5", "sm4"], ["sm8"])
            wre, wim = sm[7], sm[8]

            def bc8(v):
                return v.unsqueeze(1).broadcast_to([128, 8, 64])
            tt("dve", Ewr, PEr[:, 0:8, :], bc8(wre), ALU.mult, PK + ["sm7"], ["Ewr"])
            tt("dve", Ewt, PEi[:, 0:8, :], bc8(wim), ALU.mult, PK + ["sm8"], ["Ewt"])
            tt("dve", Ewr, Ewr, Ewt, ALU.subtract, ["Ewr", "Ewt"], ["Ewr"])
            tt("dve", Ewi, PEr[:, 0:8, :], bc8(wim), ALU.mult, PK + ["sm8"], ["Ewi"])
            tt("dve", Ewt, PEi[:, 0:8, :], bc8(wre), ALU.mult, PK + ["sm7", "Ewr"], ["Ewt"])
            tt("dve", Ewi, Ewi, Ewt, ALU.add, ["Ewi", "Ewt"], ["Ewi"])
            ts("dve", flat(Ewi), flat(Ewi), nsgnA, None, ALU.mult, None, ["Ewi", "ccols"], ["Ewi"])
            ts("dve", flat(Prs), flat(PEr[:, 8:17, :]), sgnA, None, ALU.mult, None, PK + ["ccols"], ["Prs"])
            ts("dve", f8, sm[2], 8.0, None, ALU.mult, None, ["sm2"], ["f8"])
            rho8 = magE[:, 16, :]
            a8, b8 = PEr[:, 16, :], PEi[:, 16, :]

            phase_barrier()
            A.release(mT)
            LBs = A.bf([8, 8, 16])
            RCs = A.bf([8, 9, 16])
            Tt = A.bf([8, 128])
            Wt = A.bf([8, 2, 64])
            tl = [A.f32([8, 9, 16]) for _ in range(2)]
            pAs = A.f32([3, 512])
            pBs = A.f32([2, 512])
            tw = [A.f32([512]) for _ in range(8)]
            twi = A.i32([512])
            U = A.bf([8, NCHP])
            Us = A.bf([8, 4])
            Yg = A.bf([8, NCHP])
            Ygs = A.bf([8, 4])
            yT = A.bf([8, NCHP])
            yTs = A.bf([8, 4])
            ga0 = [A.f32([NCHP]) for _ in range(5)]
            ga = [ga0, ga0]
            gi0 = A.i32([NCHP])
            gi = [gi0, gi0]
            Hp = [A.bf([NCHP]) for _ in range(2)]
            gb = [A.f32([NCHP]) for _ in range(2)]
            for q in range(2):
                S.op("pool", lambda e, q=q: e.memset(Hp[q][:, 0:1], 0.0), writes=["Hp%d" % q])
            pM = pbank[6]

            for j in range(8):
                g0 = 8 * j
                in0 = Ewr[:, :, g0:g0 + 8].rearrange("p s g -> p g s").unsqueeze(3).broadcast_to([128, 8, 8, 16])
                in1 = pBT[:, 0, g0:g0 + 8, :].unsqueeze(2).broadcast_to([128, 8, 8, 16])
                tl0 = tl[0][:, :, 0:8, :]
                tl1 = tl[1][:, :, 0:8, :]
                tt("dve", tl0, in0, in1, ALU.mult, ["Ewr", "pBT"], ["tl0"])
                in0 = Ewi[:, :, g0:g0 + 8].rearrange("p s g -> p g s").unsqueeze(3).broadcast_to([128, 8, 8, 16])
                in1 = pBT[:, 1, g0:g0 + 8, :].unsqueeze(2).broadcast_to([128, 8, 8, 16])
                tt("dve", tl1, in0, in1, ALU.mult, ["Ewi", "pBT"], ["tl1"])
                tt("dve", LBs, tl0, tl1, ALU.add, ["tl0", "tl1"], ["LBs"])
                in0 = Prs[:, :, g0:g0 + 8].rearrange("p t g -> p g t").unsqueeze(3).broadcast_to([128, 8, 9, 16])
                in1 = pCT[:, 0, g0:g0 + 8, :].unsqueeze(2).broadcast_to([128, 8, 9, 16])
                tt("dve", tl[0], in0, in1, ALU.mult, ["Prs", "pCT", "tl0"], ["tl0"])
                in0 = PEi[:, 8:17, g0:g0 + 8].rearrange("p t g -> p g t").unsqueeze(3).broadcast_to([128, 8, 9, 16])
                in1 = pCT[:, 1, g0:g0 + 8, :].unsqueeze(2).broadcast_to([128, 8, 9, 16])
                tt("dve", tl[1], in0, in1, ALU.mult, PK + ["pCT", "tl1"], ["tl1"])
                tt("dve", RCs, tl[0], tl[1], ALU.subtract, ["tl0", "tl1"], ["RCs"])
                for hh in range(2):
                    groups = []
                    for gg in range(4):
                        gl = hh * 4 + gg
                        groups.append((pbank[4][:, gg * 128:(gg + 1) * 128],
                                       [(LBs[:, gl, :, :].rearrange("p s c -> p (s c)"),
                                         RCs[:, gl, 0:8, :].rearrange("p t c -> p (t c)"))]))
                    mm_multi(groups, ["LBs", "RCs"], ["pb4"])
                    tt("dve", flat(Tt[:, hh * 4:hh * 4 + 4, :]), pbank[4][:, :], mask4, ALU.mult, ["pb4", "mask4"], ["Tt"])
                dma(pAs, s5pA[l][:, :, g0 * 64:(g0 + 8) * 64], writes=["pAs"])
                dma(pBs, s5pB[l][:, :, g0 * 64:(g0 + 8) * 64], writes=["pBs"])
                AR, AI, LD = pAs[:, 0, :], pAs[:, 1, :], pAs[:, 2, :]
                BR, BI = pBs[:, 0, :], pBs[:, 1, :]
                t = tw
                K = lambda *ix: ["tw%d" % i for i in ix]
                act(t[0], LD, AF.Exp, ["pAs"], K(0))
                tt("dve", t[1], t[0], AR, ALU.mult, K(0) + ["pAs"], K(1))
                stt("dve", t[2], t[0], 1.0 / TWO_PI, AI, ALU.mult, ALU.mult, K(0) + ["pAs"], K(2))
                act(t[3], t[1], AF.Exp, K(1), K(3))
                cp("dve", twi, t[2], K(2), ["twi"])
                tt("dve", t[4], t[2], twi, ALU.subtract, K(2) + ["twi"], K(4))
                ts("dve", t[5], t[2], 0.25, None, ALU.add, None, K(2), K(5))
                cp("dve", twi, t[5], K(5, 4), ["twi"])
                tt("dve", t[5], t[5], twi, ALU.subtract, K(5) + ["twi"], K(5))
                act(t[4], t[4], AF.Sin, K(4), K(4), scale=TWO_PI)
                act(t[5], t[5], AF.Sin, K(5), K(5), scale=TWO_PI)
                tt("dve", t[5], t[3], t[5], ALU.mult, K(3, 5), K(5))
                tt("dve", t[4], t[3], t[4], ALU.mult, K(3, 4), K(4))
                ts("dve", t[5], t[5], -1.0, None, ALU.add, None, K(5), K(5))
                tt("dve", t[0], AR, AR, ALU.mult, ["pAs"] + K(0, 1, 2), K(0))
                tt("dve", t[3], AI, AI, ALU.mult, ["pAs"] + K(3, 4, 5), K(3))
                tt("dve", t[0], t[0], t[3], ALU.add, K(0, 3), K(0))
                S.op("dve", lambda e, a=t[0]: e.reciprocal(out=a, in_=a), reads=K(0), writes=K(0))
                tt("dve", t[3], t[5], AR, ALU.mult, K(5) + ["pAs"], K(3))
                tt("dve", t[6], t[4], AI, ALU.mult, K(4) + ["pAs"], K(6))
                tt("dve", t[3], t[3], t[6], ALU.add, K(3, 6), K(3))
                tt("dve", t[3], t[3], t[0], ALU.mult, K(3, 0), K(3))
                tt("dve", t[6], t[4], AR, ALU.mult, K(4, 3) + ["pAs"], K(6))
                tt("dve", t[7], t[5], AI, ALU.mult, K(5) + ["pAs"], K(7))
                tt("dve", t[6], t[6], t[7], ALU.subtract, K(6, 7), K(6))
                tt("dve", t[6], t[6], t[0], ALU.mult, K(6, 0), K(6))
                tt("dve", t[0], t[3], BR, ALU.mult, K(3, 0) + ["pBs"], K(0))
                tt("dve", t[4], t[6], BI, ALU.mult, K(6, 4) + ["pBs"], K(4))
                tt("dve", t[0], t[0], t[4], ALU.subtract, K(0, 4), K(0))
                tt("dve", t[4], t[3], BI, ALU.mult, K(3, 4) + ["pBs"], K(4))
                tt("dve", t[5], t[6], BR, ALU.mult, K(6, 5) + ["pBs"], K(5))
                tt("dve", t[4], t[4], t[5], ALU.add, K(4, 5), K(4))
                act(t[3], t[1], AF.Exp, K(1, 3), K(3), scale=esc)
                ts("dve", t[5], t[2], esc, None, ALU.mult, None, K(2, 5) + ["ccols"], K(5))
                cp("dve", twi, t[5], K(5), ["twi"])
                tt("dve", t[6], t[5], twi, ALU.subtract, K(5, 6) + ["twi"], K(6))
                ts("dve", t[5], t[5], 0.25, None, ALU.add, None, K(5, 6), K(5))
                cp("dve", twi, t[5], K(5, 6), ["twi"])
                tt("dve", t[5], t[5], twi, ALU.subtract, K(5) + ["twi"], K(5))
                act(t[6], t[6], AF.Sin, K(6), K(6), scale=TWO_PI)
                act(t[5], t[5], AF.Sin, K(5), K(5), scale=TWO_PI)
                tt("dve", t[5], t[3], t[5], ALU.mult, K(3, 5), K(5))
                tt("dve", t[6], t[3], t[6], ALU.mult, K(3, 6), K(6))
                v3 = lambda a: a.rearrange("p (g q) -> p g q", g=8)
                tt("dve", t[3], t[5], t[0], ALU.mult, K(5, 0, 3), K(3))
                tt("dve", t[7], t[6], t[4], ALU.mult, K(6, 4, 7), K(7))
                tt("dve", Wt[:, :, 0, :], v3(t[3]), v3(t[7]), ALU.subtract, K(3, 7), ["Wt"])
                tt("dve", t[3], t[5], t[4], ALU.mult, K(5, 4, 3), K(3))
                tt("dve", t[7], t[6], t[0], ALU.mult, K(6, 0, 7), K(7))
                tt("dve", Wt[:, :, 1, :], v3(t[3]), v3(t[7]), ALU.add, K(3, 7), ["Wt"])
                dma(scrU.rearrange("g c s n -> (g c) s n"), UY[:, j, :, :], reads=["UY"], writes=["scrU"])
                for s_ in range(8):
                    dma(U[16 * s_:16 * s_ + 16, :, :], scrU[:, :, s_, :].rearrange("g c n -> c g n"),
                        reads=["scrU"], writes=["U"])
                dma(scrUs.rearrange("g c s n -> (g c) s n"), UYs[:, j, :, :], reads=["UYs"], writes=["scrUs"])
                for s_ in range(8):
                    dma(Us[16 * s_:16 * s_ + 16, :, :], scrUs[:, :, s_, :].rearrange("g c n -> c g n"),
                        reads=["scrUs"], writes=["Us"])
                for gl in range(8):
                    g = g0 + gl
                    q = gl % 2
                    a = ga[q]
                    KA = lambda *ix: ["ga_%d" % i for i in ix]
                    pS, pSw, pG, pY = pbank[q], pbank[2 + q], pbank[4], pbank[5]
                    Wg = Wt[:, gl, :, :]
                    Ug = U[:, gl, :]
                    mm_multi([(pS[:, 0:NCHP], [(Wg.rearrange("p r q -> p (r q)"), Ug)]),
                              (pSw[64:128, 0:NCHP], [(Wg[:, 0, :], Ug)]),
                              (pSw[0:64, 0:NCHP], [(Wg[:, 1, :], Ug)])],
                             ["Wt", "U"], ["pb%d" % q, "pb%d" % (2 + q)])
                    ts("dve", a[0], iota[:, 0:NCHP], f8[:, g:g + 1], None, ALU.mult, None, ["iota", "f8"] + KA(0), KA(0))
                    cp("dve", gi[q], a[0], KA(0), ["gi"])
                    tt("dve", a[1], a[0], gi[q], ALU.subtract, KA(0, 1) + ["gi"], KA(1))
                    ts("dve", a[0], a[0], 0.25, None, ALU.add, None, KA(0, 1), KA(0))
                    cp("dve", gi[q], a[0], KA(0, 1), ["gi"])
                    tt("dve", a[0], a[0], gi[q], ALU.subtract, KA(0) + ["gi"], KA(0))
                    act(a[1], a[1], AF.Sin, KA(1), KA(1), scale=TWO_PI)
                    act(a[0], a[0], AF.Sin, KA(0), KA(0), scale=TWO_PI)
                    tt("dve", a[2], a[0], pS[:, 0:NCHP], ALU.mult, KA(0, 2) + ["pb%d" % q], KA(2))
                    stt("dve", a[3], a[1], sgnA, pSw[:, 0:NCHP], ALU.mult, ALU.mult, KA(1, 3) + ["ccols", "pb%d" % (2 + q)], KA(3))
                    tt("dve", a[2], a[2], a[3], ALU.add, KA(2, 3), KA(2))
                    S.op("dve", lambda e, o=a[3], d1=a[2], g=g: e.tensor_tensor_scan(
                        out=o, data0=rho8[:, g:g + 1].to_broadcast([128, NCHP]), data1=d1, initial=0.0,
                        op0=ALU.mult, op1=ALU.add), reads=KA(2, 3) + ["magE"], writes=KA(3))
                    mm(pG[:, 0:NCHP], [(pswap, a[3])], ["pswap"] + KA(3), ["pb4"])
                    tt("dve", a[2], a[0], a[3], ALU.mult, KA(0, 3, 2), KA(2))
                    stt("dve", a[4], a[1], nsgnA, pG[:, 0:NCHP], ALU.mult, ALU.mult, KA(1, 4) + ["ccols", "pb4"], KA(4))
                    tt("dve", Hp[q][:, 1:NCHP], a[2][:, 0:NCHP - 1], a[4][:, 0:NCHP - 1], ALU.add, KA(2, 4), ["Hp%d" % q])
                    tt("dve", fin[:, g:g + 1], a[2][:, NCHP - 1:NCHP], a[4][:, NCHP - 1:NCHP], ALU.add, KA(2, 4), ["fin"])
                    Tg = Tt[:, gl, :]
                    Vg = RCs[:, gl, 1:9, :].rearrange("p t c -> p (t c)")
                    mm(pY[:, 0:NCHP], [(Tg, Ug), (Vg, Hp[q])], ["Tt", "RCs", "U", "Hp%d" % q], ["pb5"])
                    S.op("act", lambda e, o=Yg[:, gl, :], i_=pY[:, 0:NCHP]: e.copy(out=o, in_=i_), reads=["pb5"], writes=["Yg"])
                    mm_multi([(pM[:, g * 4:g * 4 + 4], [(Wg.rearrange("p r q -> p (r q)"), Us[:, gl, :])]),
                              (pM[:, 256 + gl * 4:256 + gl * 4 + 4], [(Tg, Us[:, gl, :]), (Vg, h0b[:, g, :])])],
                             ["Wt", "Tt", "RCs", "Us", "h0b"], ["pb6"])
                S.op("act", lambda e: e.copy(out=flat(Ygs), in_=pM[:, 256:288]), reads=["pb6"], writes=["Ygs"])
                dma(scrY.rearrange("t c g n -> (t c) g n"), Yg, reads=["Yg"], writes=["scrY"])
                for g_ in range(8):
                    dma(yT[16 * g_:16 * g_ + 16, :, :], scrY[:, :, g_, :].rearrange("t c n -> c t n"),
                        reads=["scrY"], writes=["yT"])
                dma(scrYs.rearrange("t c g n -> (t c) g n"), Ygs, reads=["Ygs"], writes=["scrYs"])
                for g_ in range(8):
                    dma(yTs[16 * g_:16 * g_ + 16, :, :], scrYs[:, :, g_, :].rearrange("t c n -> c t n"),
                        reads=["scrYs"], writes=["yTs"])
                def gelu_block(uview, yview, n, q):
                    b0 = gb[0][:, 0:n]
                    b1 = gb[1][:, 0:n]
                    stt("dve", b0, uview, Dcol[:, j:j + 1], yview, ALU.mult, ALU.add, ["UY", "UYs", "yT", "yTs", "s5cols", "gb0"], ["gb0"])
                    act(b1, b0, AF.Square, ["gb0", "gb1"], ["gb1"])
                    ts("dve", b1, b1, 0.044715, 1.0, ALU.mult, ALU.add, ["gb1"], ["gb1"])
                    tt("dve", b1, b1, b0, ALU.mult, ["gb1", "gb0"], ["gb1"])
                    act(b1, b1, AF.Sigmoid, ["gb1"], ["gb1"], scale=1.5957691216057308)
                    tt("dve", uview, b0, b1, ALU.mult, ["gb0", "gb1"], ["UY", "UYs"])
                for s_ in range(8):
                    gelu_block(UY[:, j, s_, :], yT[:, s_, :], NCHP, 0)
                gelu_block(flat(UYs[:, j, :, :]), flat(yTs), 32, 0)
            dma(fin_p[l], fin, reads=["fin"])
            h0v = h0[:, 0, :].rearrange("p (g i) -> p g i", i=4)
            h0s = h0[:, 1, :].rearrange("p (g i) -> p g i", i=4)
            bc4 = lambda v: v.unsqueeze(2).broadcast_to([128, 64, 4])
            fs2 = A.f32([64, 4])
            b8n = A.f32([64])
            ts("dve", b8n, b8, nsgnA, None, ALU.mult, None, PK + ["ccols"], ["b8n"])
            tt("dve", fins, h0v, bc4(a8), ALU.mult, ["h0"] + PK, ["fins"])
            tt("dve", fs2, h0s, bc4(b8n), ALU.mult, ["h0", "b8n"], ["fs2"])
            tt("dve", fins, fins, fs2, ALU.add, ["fins", "fs2"], ["fins"])
            tt("dve", flat(fins), flat(fins), pM[:, 0:256], ALU.add, ["fins", "pb6"], ["fins"])
            dma(fin_s[l], flat(fins), reads=["fins"])
            A.release(m2)
            phase_barrier()

            m3 = A.mark()
            xt = [A.f32([1024]) for _ in range(2)]
            hb = [A.bf([1024]) for _ in range(2)]
            hT = [A.bf([8, 128]) for _ in range(2)]
            sscol = [A.f32([4]) for _ in range(2)]
            junk = A.bf([1024])
            wZ = A.bf([8, 1024])
            wB = A.bf([8, 1024])
            wC = A.bf([8, 1024])
            gpost = A.f32([1024])
            stage = [A.f32([1024]) for _ in range(2)]
            load_weight(wZ, a_w_in[l][:, 1024:2048], 1024, gpre, stage, "wZ")
            load_weight(wB, a_w_glu[l], 1024, None, stage, "wB")
            load_weight(wC, a_w_out[l], 1024, None, stage, "wC")
            dma(gpost, s5gpost[l], writes=["gpost"])
            sz = A.f32([8, 128])
            sg = A.f32([8, 128])
            vT = A.bf([8, 128])
            ysamp = A.bf([8, 128])
            res = [A.f32([1024]) for _ in range(2)]
            ss2 = [A.f32([4]) for _ in range(2)]
            S.op("pool", lambda e: e.memset(flat(ysamp), 0.0), writes=["ysamp"])
            cp("dve", ysamp.rearrange("p k (i r) -> p k r i", r=32)[:, :, 0:8, :], UYs, ["UYs", "ysamp"], ["ysamp"])
            src_pv = src_p.rearrange("(n s) f -> s n f", s=8)
            dst_pv = dst_p.rearrange("(n s) f -> s n f", s=8)
            sets = [(s_, nb) for s_ in range(8) for nb in range(NB)] + [None]
            for it, tset in enumerate(sets):
                q = it % 2
                tag = str(q)
                if tset is not None:
                    s_, nb = tset
                    src = src_pv[s_, nb * 128:(nb + 1) * 128, :]
                    dst = dst_pv[s_, nb * 128:(nb + 1) * 128, :]
                    yv = UY[:, :, s_, nb * 128:(nb + 1) * 128]
                else:
                    src, dst = src_s, dst_s
                    yv = ysamp
                norm_T(src, xt[q], hb[q], hT[q], tag, sscol[q], junk)
                for half in range(2):
                    pb = pbank[half]
                    groups = []
                    for jj in range(4):
                        o = half * 4 + jj
                        groups.append((pb[:, jj * 128:(jj + 1) * 128],
                                       [(wZ[:, k, o * 128:(o + 1) * 128], hT[q][:, k, :]) for k in range(8)]))
                    mm_multi(groups, ["wZ", "hT" + tag], ["pb%d" % half])
                    act(flat(sz[:, half * 4:half * 4 + 4, :]), pb[:, :], AF.Silu, ["pb%d" % half], ["sz"])
                for half in range(2):
                    pb = pbank[2 + half]
                    groups = []
                    for jj in range(4):
                        o = half * 4 + jj
                        groups.append((pb[:, jj * 128:(jj + 1) * 128],
                                       [(wB[:, k, o * 128:(o + 1) * 128], yv[:, k, :]) for k in range(8)]))
                    mm_multi(groups, ["wB", "UY", "ysamp"], ["pb%d" % (2 + half)])
                    for jj in range(4):
                        o = half * 4 + jj
                        act(sg[:, o, :], pb[:, jj * 128:(jj + 1) * 128], AF.Sigmoid, ["pb%d" % (2 + half), "s5cols"], ["sg"],
                            bias=bglu[:, o:o + 1])
                tt("dve", sg, sg, sz, ALU.mult, ["sg", "sz"], ["sg"])
                tt("dve", vT, sg, yv, ALU.mult, ["sg", "UY", "ysamp"], ["vT"])
                for half in range(2):
                    pb = pbank[4 + half]
                    mm(pb[:, :], [(vT[:, k, :], wC[:, k, half * 512:(half + 1) * 512]) for k in range(8)], ["vT", "wC"], ["pb%d" % (4 + half)])
                post_norm_residual(pbank[4], pbank[5], "pb4", "pb5", xt[q], "xt" + tag, gpost, res[q], "res" + tag, ss2[q], "ssb" + tag, junk, dst)
            A.release(m3)
            A.release(m0)
            phase_barrier()

        def post_norm_residual(pa, pb, ka, kb, xtile, kx, gpost, res, kres, ss, kss, junk, dst):
            act(junk[:, 0:512], pa[:, :], AF.Square, [ka], ["junk", kss + "a"], accum_out=ss[:, 0:1])
            act(junk[:, 512:1024], pb[:, :], AF.Square, [kb], ["junk", kss + "b"], accum_out=ss[:, 1:2])
            tt("dve", ss[:, 2:3], ss[:, 0:1], ss[:, 1:2], ALU.add, [kss + "a", kss + "b"], [kss + "c"])
            ts("dve", ss[:, 2:3], ss[:, 2:3], 1.0 / 1024.0, 1e-6, ALU.mult, ALU.add, [kss + "c"], [kss + "c"])
            act(ss[:, 3:4], ss[:, 2:3], AF.Sqrt, [kss + "c"], [kss + "d"])
            S.op("dve", lambda e: e.reciprocal(out=ss[:, 3:4], in_=ss[:, 3:4]), reads=[kss + "d"], writes=[kss + "d"])
            stt("dve", res[:, 0:512], pa[:, :], ss[:, 3:4], gpost[:, 0:512], ALU.mult, ALU.mult, [ka, kss + "d", "gpost"], [kres])
            stt("dve", res[:, 512:1024], pb[:, :], ss[:, 3:4], gpost[:, 512:1024], ALU.mult, ALU.mult, [kb, kss + "d", "gpost"], [kres])
            tt("pool", res, res, xtile, ALU.add, [kres, kx], [kres])
            dma(dst, res, reads=[kres], writes=["dram_res"])

        ATT = {}

        def load_weight2(dst_bf, w_dram, ncols, gcol, mul, stage, key, gkey):
            for k in range(8):
                q = k % len(stage)
                dma(stage[q][:, 0:ncols], w_dram[k * 128:(k + 1) * 128, :], writes=["wst%d" % q])
                ts("pool", dst_bf[:, k, :], stage[q][:, 0:ncols], gcol[:, k:k + 1], mul, ALU.mult, ALU.mult, ["wst%d" % q, gkey], [key])

        def kv_phase(src_p, src_s):
            KT = A.bf([8, SEQ])
            Vb = A.bf([NT, 1024])
            KTs = A.bf([8, 128])
            sbc = A.f32([24])
            ccore = A.f32([4])
            cattn = A.bf([6, 128])
            identf = A.f32([128])
            onesrow = A.f32([128])
            dma(sbc, sbcols[:, :], writes=["sbc"])
            dma(ccore, c_core[:, :], writes=["ccore"])
            dma(flat(cattn), c_attn.rearrange("p a b -> p (a b)"), writes=["cattn"])
            dma(identf, c_identf[:, :], writes=["identf"])
            dma(onesrow[0:1, :], c_onesrow[:, :], writes=["onesrow"])
            CONST_KEYS.extend(["sbc", "ccore", "cattn", "identf", "onesrow"])
            ATT.update(KT=KT, Vb=Vb, KTs=KTs, sbc=sbc, ccore=ccore, cattn=cattn, identf=identf, onesrow=onesrow)
            m = A.mark()
            wKV = A.bf([8, 2048])
            stage = [A.f32([2048])]
            load_weight2(wKV, w_kv, 2048, sbc[:, 0:8], 1.0, stage, "wKV", "sbc")
            xt = A.f32([1024]); hb = A.bf([1024]); hT = A.bf([8, 128]); ssc = A.f32([4]); junk = A.bf([1024])
            kvo = [A.f32([2048]) for _ in range(2)]
            for i in range(NT + 1):
                q = i % 2
                src = src_p[i * 128:(i + 1) * 128, :] if i < NT else src_s
                norm_T(src, xt, hb, hT, "k", ssc, junk)
                for half in range(2):
                    pb = pbank[half]
                    groups = []
                    for jj in range(4):
                        a = half * 4 + jj
                        groups.append((pb[:, jj * 128:(jj + 1) * 128], [(wKV[:, k, a * 128:(a + 1) * 128], hT[:, k, :]) for k in range(8)]))
                    mm_multi(groups, ["wKV", "hTk"], ["pb%d" % half])
                    pv = pb[:, :].rearrange("p (j m) -> p j m", j=4)
                    if i < NT:
                        outv = KT[:, half * 4:half * 4 + 4, i * 128:(i + 1) * 128]
                    else:
                        outv = KTs[:, half * 4:half * 4 + 4, :]
                    S.op("act", lambda e, outv=outv, pv=pv: e.copy(out=outv, in_=pv), reads=["pb%d" % half], writes=["KT"])
                for n in range(4):
                    pb = pbank[2 + n]
                    mm(pb[:, :], [(hT[:, k, :], wKV[:, k, n * 512:(n + 1) * 512]) for k in range(8)], ["wKV", "hTk"], ["pb%d" % (2 + n)])
                    S.op("act", lambda e, o=kvo[q][:, n * 512:(n + 1) * 512], pb=pb: e.copy(out=o, in_=pb[:, :]), reads=["pb%d" % (2 + n)], writes=["kvo%d" % q])
                if i < NT:
                    dma(k_p[i * 128:(i + 1) * 128, :], kvo[q][:, 0:1024], reads=["kvo%d" % q])
                    dma(v_p[i * 128:(i + 1) * 128, :], kvo[q][:, 1024:2048], reads=["kvo%d" % q])
                    cp("pool", Vb[:, i, :], kvo[q][:, 1024:2048], ["kvo%d" % q], ["Vb"])
                else:
                    dma(k_s[:, :], kvo[q][:, 0:1024], reads=["kvo%d" % q])
                    dma(v_s[:, :], kvo[q][:, 1024:2048], reads=["kvo%d" % q], writes=["v_s_dram"])
            A.release(m)
            phase_barrier()

        def sb_layer(j, first, src_nat, src_q, src_s, dst_q, dst_s):
            KT, Vb, KTs, sbc, ccore, cattn = ATT["KT"], ATT["Vb"], ATT["KTs"], ATT["sbc"], ATT["ccore"], ATT["cattn"]
            identf, onesrow = ATT["identf"], ATT["onesrow"]
            triL, onesb, M0, M1, mnew, zerosb = cattn[:, 0, :], cattn[:, 1, :], cattn[:, 2, :], cattn[:, 3, :], cattn[:, 4, :], cattn[:, 5, :]
            QG = cfg.QG
            NQ = QG * 128
            mL = A.mark()
            gpost = A.f32([1024])
            biasT = A.f32([16])
            biasrow = A.f32([128])
            dma(gpost, sbgpost[j], writes=["gpost"])
            dma(biasT, sbbias[j], writes=["biasT"])
            dma(biasrow, sbbiasrow[j], writes=["biasrow"])
            ebias = A.f32([128])
            act(ebias, biasrow, AF.Exp, ["biasrow"], ["ebias"])
            gcol = sbc[:, 8 + 8 * j:16 + 8 * j]
            xt = A.f32([1024]); hb = A.bf([1024]); hT = A.bf([8, 128]); ssc = A.f32([4]); junk = A.bf([1024])
            res = A.f32([1024]); ss2 = A.f32([4])
            qT = A.bf([8, NQ]); gT = A.bf([8, NQ]); off_hT4 = A.off; hT4 = A.bf([8, NQ])

            def load_x(slot):
                if slot is None:
                    dma(xt, src_s, writes=["xtk"])
                elif first:
                    dma(xt, src_nat[(2 * slot) * 128:(2 * slot + 1) * 128, :], writes=["xtk"])
                    dma(res, src_nat[(2 * slot + 1) * 128:(2 * slot + 2) * 128, :], writes=["res"])
                    ts("dve", xt, xt, ccore[:, 0:1], None, ALU.mult, None, ["xtk", "ccore"], ["xtk"])
                    stt("dve", xt, res, ccore[:, 1:2], xt, ALU.mult, ALU.add, ["res", "xtk", "ccore"], ["xtk"])
                else:
                    dma(xt, src_q[slot * 128:(slot + 1) * 128, :], writes=["xtk"])

            def project(slots, ncols):
                mR = A.mark()
                wQh = A.bf([8, 1024])
                stage = [A.f32([1024]) for _ in range(2)]
                load_weight2(wQh, b_w_in[j][:, 0:1024], 1024, gcol, 0.125, stage, "wQh", "sbc")
                for t, slot in enumerate(slots):
                    load_x(slot)
                    norm_T(None, xt, hb, hT, "k", ssc, junk)
                    cp("dve", hT4[:, :, t * 128:(t + 1) * 128], hT, ["hTk"], ["hT4"])
                for a in range(8):
                    pb = pbank[a % 2]
                    mm(pb[:, 0:ncols], [(wQh[:, k, a * 128:(a + 1) * 128], hT4[:, k, 0:ncols]) for k in range(8)], ["wQh", "hT4"], ["pb%d" % (a % 2)])
                    S.op("act", lambda e, o=qT[:, a, 0:ncols], pb=pb: e.copy(out=o, in_=pb[:, 0:ncols]), reads=["pb%d" % (a % 2)], writes=["qT"])
                load_weight2(wQh, b_w_in[j][:, 1024:2048], 1024, gcol, 1.0, stage, "wQh", "sbc")
                for a in range(8):
                    pb = pbank[a % 2]
                    mm(pb[:, 0:ncols], [(wQh[:, k, a * 128:(a + 1) * 128], hT4[:, k, 0:ncols]) for k in range(8)], ["wQh", "hT4"], ["pb%d" % (a % 2)])
                    act(gT[:, a, 0:ncols], pb[:, 0:ncols], AF.Silu, ["pb%d" % (a % 2)], ["gT"])
                A.release(mR)
                phase_barrier()

            def out_proj(slots):
                mR = A.mark()
                wO = A.bf([8, 1024])
                stage = [A.f32([1024]) for _ in range(2)]
                load_weight(wO, b_w_out[j], 1024, None, stage, "wO")
                for t, slot in enumerate(slots):
                    load_x(slot)
                    for half in range(2):
                        pb = pbank[4 + half]
                        mm(pb[:, :], [(gT[:, k, t * 128:(t + 1) * 128], wO[:, k, half * 512:(half + 1) * 512]) for k in range(8)], ["gT", "wO"], ["pb%d" % (4 + half)])
                    dst = dst_s if slot is None else dst_q[slot * 128:(slot + 1) * 128, :]
                    post_norm_residual(pbank[4], pbank[5], "pb4", "pb5", xt, "xtk", gpost, res, "res", ss2, "ssb", junk, dst)
                A.release(mR)
                phase_barrier()

            for G in range(0 if DBG.get("noprompt") else cfg.NSLOT // QG):
                slots = [G * QG + t for t in range(QG)]
                project(slots, NQ)
                mR = A.mark()
                ez = [A.f32([NQ]) for _ in range(2)]
                e1 = [A.f32([NQ]) for _ in range(2)]
                sp = [A.bf([NQ]) for _ in range(2)]
                Ab = [A.bf([NQ]) for _ in range(2)]
                RSb = [A.bf([NQ]) for _ in range(2)]
                RS32 = A.f32([NQ])
                KBmax = 2 * (G * QG + QG - 1) + 1
                for h in range(16):
                    a, hb_ = h // 2, (h % 2) * 64
                    po = pbank[4 + a % 2]
                    S.op("pool", lambda e: e.memset(RS32, 0.0), writes=["RS32"])
                    S.op("pool", lambda e: e.memset(RSb[0], 0.0), writes=["RSb0"])
                    S.op("pool", lambda e: e.memset(RSb[1], 0.0), writes=["RSb1"])
                    units = list(range(KBmax, -1, -1))
                    n = len(units)
                    S.op("pe", lambda e: e.matmul(po[hb_:hb_ + 64, 0:NQ], lhsT=zerosb[:, 0:64], rhs=KT[:, a, 0:NQ], start=True, stop=False),
                         reads=["cattn", "KT"], writes=["po%d_%d" % (a % 2, h % 2)])

                    def geo(kb):
                        c0 = max(0, kb // 2 - G * QG) * 128
                        t = kb // 2 - G * QG
                        return c0, (t if t >= 0 else None), kb % 2

                    def s1(u):
                        kb = units[u]; c0, t, x = geo(kb); p = u % 2
                        mm(pbank[p][:, c0:NQ], [(KT[hb_:hb_ + 64, a, kb * 128:(kb + 1) * 128], qT[hb_:hb_ + 64, a, c0:NQ])], ["KT", "qT"], ["pb%d" % p])

                    def s2(u):
                        kb = units[u]; c0, t, x = geo(kb); p = u % 2
                        act(ez[p][:, c0:NQ], pbank[p][:, c0:NQ], AF.Exp, ["pb%d" % p, "biasT"], ["ez%d" % p], bias=biasT[:, h:h + 1])
                        act(sp[p][:, c0:NQ], ez[p][:, c0:NQ], AF.Ln, ["ez%d" % p], ["sp%d" % p], bias=1.0)
                        if t is not None:
                            M = M0 if x == 0 else M1
                            tt("dve", sp[p][:, t * 128:(t + 1) * 128], sp[p][:, t * 128:(t + 1) * 128], M, ALU.mult, ["sp%d" % p, "cattn"], ["sp%d" % p])
                        if kb > 0:
                            tt("pool", RS32[:, c0:NQ], RS32[:, c0:NQ], sp[p][:, c0:NQ], ALU.add, ["RS32", "sp%d" % p], ["RS32"])
                            cp("pool", RSb[1 - p][:, c0:NQ], RS32[:, c0:NQ], ["RS32"], ["RSb%d" % (1 - p)])

                    def s3(u):
                        kb = units[u]; c0, t, x = geo(kb); p = u % 2
                        mm(pbank[2 + p][:, c0:NQ], [(triL, sp[p][:, c0:NQ]), (onesb, RSb[p][:, c0:NQ])], ["cattn", "sp%d" % p, "RSb%d" % p], ["pb%d" % (2 + p)])

                    def s4(u):
                        kb = units[u]; c0, t, x = geo(kb); p = u % 2
                        act(e1[p][:, c0:NQ], pbank[2 + p][:, c0:NQ], AF.Exp, ["pb%d" % (2 + p)], ["e1%d" % p], scale=-1.0)
                        tt("dve", Ab[p][:, c0:NQ], ez[p][:, c0:NQ], e1[p][:, c0:NQ], ALU.mult, ["ez%d" % p, "e1%d" % p], ["Ab%d" % p])
                        if t is not None:
                            M = M0 if x == 0 else M1
                            tt("dve", Ab[p][:, t * 128:(t + 1) * 128], Ab[p][:, t * 128:(t + 1) * 128], M, ALU.mult, ["Ab%d" % p, "cattn"], ["Ab%d" % p])

                    def s5(u):
                        kb = units[u]; c0, t, x = geo(kb); p = u % 2

                        def fn(e, kb=kb, c0=c0, p=p):
                            ins = None
                            for tl_ in range(c0 // 128, QG):
                                sig = G * QG + tl_
                                ins = e.matmul(po[hb_:hb_ + 64, tl_ * 128:(tl_ + 1) * 128], lhsT=Vb[:, kb, h * 64:(h + 1) * 64],
                                               rhs=Ab[p][:, tl_ * 128:(tl_ + 1) * 128], start=False, stop=(kb == 0))
                            return ins
                        S.op("pe", fn, reads=["Vb", "Ab%d" % p], writes=["po%d_%d" % (a % 2, h % 2)])

                    for tstep in range(n + 2):
                        if tstep < n:
                            s1(tstep); s2(tstep)
                        if 0 <= tstep - 1 < n:
                            s3(tstep - 1); s4(tstep - 1)
                        if 0 <= tstep - 2 < n:
                            s5(tstep - 2)
                    if h % 2 == 1:
                        tt("dve", gT[:, a, :], po[:, 0:NQ], gT[:, a, :], ALU.mult, ["po%d_0" % (a % 2), "po%d_1" % (a % 2), "gT"], ["gT"])
                A.release(mR)
                phase_barrier()
                out_proj(slots)

            if cfg.sample:
                project([None], 128)
                mR = A.mark()
                NPG = cfg.NPG
                idx = idx_t; idf = A.f32([4 * NPG]); ptt = A.i32([4 * NPG])
                if cfg.sample_attn:
                    dma(ptt, ptab[:, :], writes=["ptt"])
                    ts("dve", idf, ptt, 128.0, ccore[:, 2:3], ALU.mult, ALU.add, ["ptt", "ccore"], ["idf"])
                    cp("dve", idx[:, :], idf, ["idf"], ["idx"])
                A2 = Arena(arena_t[:, off_hT4:off_hT4 + 4 * NQ], 4 * NQ)
                Kpg = [A2.f32([1024])] if 4 * NQ >= 2048 else [A.f32([1024])]
                Vpg = [A2.f32([1024])] if 4 * NQ >= 2048 else [A.f32([1024])]
                KTp = [A.bf([8, 128])]
                Vbp = [A.bf([1024])]
                Kb = [A.bf([1024])]
                ez = [A.f32([128]) for _ in range(2)]
                e1 = [A.f32([128]) for _ in range(2)]
                sp = [A.bf([128]) for _ in range(2)]
                Ab = [A.bf([128]) for _ in range(2)]
                RSb = [A.bf([128]) for _ in range(2)]
                RS32 = A.f32([128])
                ogs = A.bf([8, 128])
                fz = A.f32([8])
                qTm = A.bf([8, 2, 128])
                S.op("pool", lambda e: e.memset(flat(qTm), 0.0), writes=["qTm"])
                cp("dve", qTm[0:64, :, 0, :], qT[0:64, :, 0:128], ["qT", "qTm"], ["qTm"])
                cp("dve", qTm[64:128, :, 1, :], qT[64:128, :, 0:128], ["qT", "qTm"], ["qTm"])
                pos = pbank[4]
                S.op("pool", lambda e: e.memset(flat(ogs), 0.0), writes=["ogs"])
                for i in range(4 if cfg.sample_attn else 0):
                    S.op("pool", lambda e: e.memset(RS32, 0.0), writes=["RS32"])
                    S.op("pool", lambda e: e.memset(RSb[0], 0.0), writes=["RSb0"])
                    S.op("pool", lambda e: e.memset(RSb[1], 0.0), writes=["RSb1"])
                    units = [None] + list(range(NPG - 1, -1, -1))
                    n = len(units)

                    def s0(u):
                        pg = units[u]; p = u % 2
                        if pg is None:
                            dma(Vpg[0][0:8, :], v_s[32 * i:32 * i + 8, :], reads=["v_s_dram"], writes=["Vpg0"])
                            cp("pool", Vbp[0][0:8, :], Vpg[0][0:8, :], ["Vpg0"], ["Vbp0"])
                            return
                        col = i * NPG + pg
                        if DBG.get("nogather"):
                            dma(Kpg[0][:, :], cache_k[pg * 128:(pg + 1) * 128, :], writes=["Kpg0"])
                            dma(Vpg[0][:, :], cache_v[pg * 128:(pg + 1) * 128, :], writes=["Vpg0"])
                        if not DBG.get("nogather"):
                            S.op("pool", lambda e, p=p, col=col: e.indirect_dma_start(out=Kpg[0][:, :], out_offset=None, in_=cache_k[:, :],
                                 in_offset=bass.IndirectOffsetOnAxis(ap=idx[:, col:col + 1], axis=0)), reads=["idx"], writes=["Kpg0"], dma=True)
                            S.op("pool", lambda e, p=p, col=col: e.indirect_dma_start(out=Vpg[0][:, :], out_offset=None, in_=cache_v[:, :],
                                 in_offset=bass.IndirectOffsetOnAxis(ap=idx[:, col:col + 1], axis=0)), reads=["idx"], writes=["Vpg0"], dma=True)
                        cp("pool", Kb[0], Kpg[0], ["Kpg0"], ["Kb0"])

                        def tp(e, p=p):
                            ins = None
                            for k in range(8):
                                ins = e.transpose(ptr[:, k, :], Kb[0][:, k * 128:(k + 1) * 128], ident)
                            return ins
                        S.op("pe", tp, reads=["Kb0", "ident", "fz"], writes=["ptr"])
                        S.op("act", lambda e, o=KTp[0]: e.copy(out=o, in_=ptr[:, :, :]), reads=["ptr"], writes=["KTp0"])
                        cp("pool", Vbp[0], Vpg[0], ["Vpg0"], ["Vbp0"])

                    def s1(u):
                        pg = units[u]; p = u % 2
                        nk = 8 if pg is None else 128

                        if (DBG.get("noA") and pg is None) or (DBG.get("noB") and pg is not None):
                            return

                        def fn(e, p=p, pg=pg, nk=nk):
                            ins = None
                            for h in range(1 if DBG.get("noD") else 16):
                                a_, hb_ = h // 2, (h % 2) * 64
                                if DBG.get("noC"):
                                    hb_ = 0
                                if pg is None:
                                    kt = KTs[:, a_, 32 * i:32 * i + 8]
                                else:
                                    kt = KTp[0][:, a_, :]
                                ins = e.matmul(pbank[p][0:nk, h * 8:(h + 1) * 8], lhsT=kt, rhs=qTm[:, a_, h % 2, 32 * i:32 * i + 8], start=True, stop=True)
                            return ins
                        S.op("pe", fn, reads=["KTp0", "KTp0", "KT", "qTm"], writes=["pb%d" % p])

                    def s2(u):
                        pg = units[u]; p = u % 2
                        nk = 8 if pg is None else 128
                        act(ez[p][0:nk, :], pbank[p][0:nk, 0:128], AF.Exp, ["pb%d" % p], ["ez%d" % p])
                        tt("dve", ez[p][0:nk, :], ez[p][0:nk, :], ebias[0:nk, :], ALU.mult, ["ez%d" % p, "ebias"], ["ez%d" % p])
                        act(sp[p][0:nk, :], ez[p][0:nk, :], AF.Ln, ["ez%d" % p], ["sp%d" % p], bias=1.0)
                        if pg is None:
                            tt("dve", sp[p][0:8, :], sp[p][0:8, :], mnew[0:8, :], ALU.mult, ["sp%d" % p, "cattn"], ["sp%d" % p])
                        if u < n - 1:
                            tt("pool", RS32[0:nk, :], RS32[0:nk, :], sp[p][0:nk, :], ALU.add, ["RS32", "sp%d" % p], ["RS32"])
                            cp("pool", RSb[1 - p], RS32, ["RS32"], ["RSb%d" % (1 - p)])

                    def s3(u):
                        pg = units[u]; p = u % 2
                        nk = 8 if pg is None else 128
                        if pg is None:
                            mm(pbank[2 + p][0:8, 0:128], [(triL[0:8, 0:8], sp[p][0:8, :])], ["cattn", "sp%d" % p], ["pb%d" % (2 + p)])
                        else:
                            mm(pbank[2 + p][:, 0:128], [(triL, sp[p]), (onesb, RSb[p])], ["cattn", "sp%d" % p, "RSb%d" % p], ["pb%d" % (2 + p)])

                    def s4(u):
                        pg = units[u]; p = u % 2
                        nk = 8 if pg is None else 128
                        act(e1[p][0:nk, :], pbank[2 + p][0:nk, 0:128], AF.Exp, ["pb%d" % (2 + p)], ["e1%d" % p], scale=-1.0)
                        tt("dve", Ab[p][0:nk, :], ez[p][0:nk, :], e1[p][0:nk, :], ALU.mult, ["ez%d" % p, "e1%d" % p], ["Ab%d" % p])
                        if pg is None:
                            tt("dve", Ab[p][0:8, :], Ab[p][0:8, :], mnew[0:8, :], ALU.mult, ["Ab%d" % p, "cattn"], ["Ab%d" % p])

                    def s5(u):
                        pg = units[u]; p = u % 2
                        nk = 8 if pg is None else 128

                        def fn(e, p=p, nk=nk, u=u):
                            ins = None
                            for h in range(16):
                                a_, hb_ = h // 2, (h % 2) * 64
                                c_ = (i * 8 + a_) * 8
                                ins = e.matmul(pos[hb_:hb_ + 64, c_:c_ + 8], lhsT=Vbp[0][0:nk, h * 64:(h + 1) * 64], rhs=Ab[p][0:nk, h * 8:(h + 1) * 8],
                                               start=(u == 0 and h < 2), stop=(u == n - 1))
                            return ins
                        S.op("pe", fn, reads=["Vbp0", "Ab%d" % p], writes=["pos"])

                    DS = DBG.get("stages", 9)
                    for u in range(n):
                        for si_, sf_ in enumerate((s0, s1, s2, s3, s4, s5)):
                            if si_ <= DS:
                                sf_(u)
                        act(fz, pbank[2][:, 0:8], AF.Copy, ["pos", "pb2"], ["fz"])
                ov = pos[:, 0:256].rearrange("p (i a q) -> p i a q", i=4, a=8)
                gv = gT[:, :, 0:128].rearrange("p a (i r) -> p i a r", r=32)[:, :, :, 0:8]
                tt("dve", gv, ov, gv, ALU.mult, ["pos", "gT"], ["gT"])
                A.release(mR)
                phase_barrier()
                out_proj([None])
            A.release(mL)
            phase_barrier()

        XA_P, XA_S = xa[0:SEQ, :], xa[SEQ:SEQ + 128, :]
        XB_P, XB_S = xb[0:SEQ, :], xb[SEQ:SEQ + 128, :]
        XQ_P, XQ_S = xq[0:cfg.NSLOT * 128, :], xq[cfg.NSLOT * 128:cfg.NSLOT * 128 + 128, :]
        if cfg.stages == ("s5",):
            if cfg.n_s5 == 1:
                s5_layer(0, xp, xs, y_p, y_s)
            else:
                s5_layer(0, xp, xs, XA_P, XA_S)
                s5_layer(1, XA_P, XA_S, y_p, y_s)
        elif cfg.stages == ("kv", "sb"):
            kv_phase(xp, xs)
            if cfg.n_sb == 1:
                sb_layer(0, True, xp, None, xs, yq_p, y_s)
            else:
                sb_layer(0, True, xp, None, xs, XQ_P, XQ_S)
                sb_layer(1, False, None, XQ_P, XQ_S, yq_p, y_s)
        else:
            s5_layer(0, xp, xs, XA_P, XA_S)
            s5_layer(1, XA_P, XA_S, XB_P, XB_S)
            kv_phase(XB_P, XB_S)
            sb_layer(0, True, XB_P, None, XB_S, XQ_P, XQ_S)
            sb_layer(1, False, None, XQ_P, XQ_S, yq_p, y_s)
        S.finish()
        S.emit()
        print("ops", S.nops, "arena peak words", A.peak)
    return nc


def _bc(a, n=128):
    return np.broadcast_to(a[None], (n,) + a.shape)


def make_consts():
    c = {}
    c["c_ident"] = np.eye(128, dtype=np.float32).astype(ml_dtypes.bfloat16)
    rows = np.arange(128)
    colsi = np.arange(512)
    c["c_mask4"] = (((colsi[None, :] % 128) // 16) >= (rows[:, None] // 16)).astype(np.float32)
    P = np.zeros((128, 128), np.float32)
    P[rows, (rows + 64) % 128] = 1.0
    c["c_pswap"] = P
    cc = np.zeros((128, 4), np.float32)
    cc[:, 0] = 7 - rows // 16
    cc[:, 1] = np.where(rows < 64, 1.0, -1.0)
    cc[:, 2] = -cc[:, 1]
    c["c_cols"] = cc
    c["c_iota"] = np.ascontiguousarray(_bc(np.arange(512, dtype=np.float32)))
    ev = np.array([0, -1, -2, -3, -4, -5, -6, -7, 0, 1, 2, 3, 4, 5, 6, 7, 8], np.float32)
    c["c_evals"] = np.ascontiguousarray(_bc(np.repeat(ev, 64)))
    return c


def prep_core(inp, cfg, c):
    f = np.float32
    b, r = c // 2, c % 2
    SEQ = cfg.SEQ
    d = {}
    d["xp"] = np.ascontiguousarray(inp["x_prompt"][b, :SEQ])
    xs = np.zeros((128, 1024), f)
    for i in range(4):
        xs[32 * i:32 * i + 8] = inp["x_sample"][4 * c + i]
    d["xs"] = xs
    pA, pB, pAT, pBT, pCT, cols, gpost, h0 = [], [], [], [], [], [], [], []
    for l in range(2):
        Ar, Ai, ld = inp["a_A_re"][l], inp["a_A_im"][l], inp["a_log_dt"][l]
        pA.append(np.stack([_bc(Ar.reshape(-1)), _bc(Ai.reshape(-1)), _bc(np.repeat(ld, 64))], axis=1))
        Br, Bi = inp["a_B_re"][l], inp["a_B_im"][l]
        brw = np.tile(Br.transpose(2, 0, 1), (8, 1, 1)).reshape(128, 4096)
        biw = np.tile(Bi.transpose(2, 0, 1), (8, 1, 1)).reshape(128, 4096)
        pB.append(np.stack([brw, biw], axis=1))
        ArT = np.concatenate([Ar.T, Ar.T], 0)
        AiT = np.concatenate([Ai.T, Ai.T], 0)
        pAT.append(np.stack([ArT, AiT, _bc(ld)], axis=1))
        BrT, BiT = Br.transpose(1, 0, 2), Bi.transpose(1, 0, 2)
        B2 = np.concatenate([BrT, BiT], 0).reshape(128, 1024)
        B2s = np.concatenate([BiT, BrT], 0).reshape(128, 1024)
        pBT.append(np.stack([B2, B2s], axis=1))
        Cr, Ci = inp["a_C_re"][l].transpose(2, 0, 1), inp["a_C_im"][l].transpose(2, 0, 1)
        C2 = np.concatenate([Cr, Ci], 0).reshape(128, 1024)
        C2s = np.concatenate([Ci, Cr], 0).reshape(128, 1024)
        pCT.append(np.stack([C2, C2s], axis=1))
        cols.append(np.concatenate([inp["a_norm_pre"][l].reshape(8, 128).T, inp["a_D"][l].reshape(8, 128).T,
                                    inp["a_b_glu"][l].reshape(8, 128).T], axis=1))
        gpost.append(_bc(inp["a_norm_post"][l]))
        hr = inp["state_ssm_re"][l, 4 * c:4 * c + 4].transpose(2, 1, 0)
        hi = inp["state_ssm_im"][l, 4 * c:4 * c + 4].transpose(2, 1, 0)
        h0.append(np.stack([np.concatenate([hr, hi], 0).reshape(128, 256),
                            np.concatenate([hi, hr], 0).reshape(128, 256)], axis=1))
    d["s5pA"] = np.ascontiguousarray(np.stack(pA), f)
    d["s5pB"] = np.ascontiguousarray(np.stack(pB), f)
    d["s5pAT"] = np.ascontiguousarray(np.stack(pAT), f)
    d["s5pBT"] = np.ascontiguousarray(np.stack(pBT), f)
    d["s5pCT"] = np.ascontiguousarray(np.stack(pCT), f)
    d["s5cols"] = np.ascontiguousarray(np.stack(cols), f)
    d["s5gpost"] = np.ascontiguousarray(np.stack(gpost), f)
    d["s5h0"] = np.ascontiguousarray(np.stack(h0), f)
    for k in ("a_w_in", "a_w_glu", "a_w_out"):
        d[k] = np.ascontiguousarray(inp[k], f)
    return d


def prep_core_attn(inp, cfg, c, d):
    f = np.float32
    r = c % 2
    d["w_kv"] = np.ascontiguousarray(inp["w_kv"], f)
    d["b_w_in"] = np.ascontiguousarray(inp["b_w_in"], f)
    d["b_w_out"] = np.ascontiguousarray(inp["b_w_out"], f)
    d["sbcols"] = np.ascontiguousarray(np.concatenate([inp["kv_norm"].reshape(8, 128).T, inp["b_norm_pre"][0].reshape(8, 128).T,
                                                       inp["b_norm_pre"][1].reshape(8, 128).T], axis=1), f)
    d["sbgpost"] = np.ascontiguousarray(np.stack([_bc(inp["b_norm_post"][j]) for j in range(2)]), f)
    d["sbbias"] = np.ascontiguousarray(np.stack([_bc(inp["b_logit_bias"][j]) for j in range(2)]), f)
    d["sbbiasrow"] = np.ascontiguousarray(np.stack([_bc(np.repeat(inp["b_logit_bias"][j], 8)) for j in range(2)]), f)
    cc = np.zeros((128, 4), f)
    cc[:, 0] = 1.0 if r == 0 else 0.0
    cc[:, 1] = 0.0 if r == 0 else 1.0
    cc[:, 2] = np.arange(128)
    d["c_core"] = cc
    k = np.arange(128)
    triL = (k[:, None] >= k[None, :]).astype(f)
    ones = np.ones((128, 128), f)
    triT = (k[:, None] < k[None, :]).astype(f)
    zeros = np.zeros((128, 128), f)
    M0, M1 = (triT, zeros) if r == 0 else (ones, triT)
    mnew = np.zeros((128, 128), f)
    svals = np.tile(np.arange(8), 16)
    for j in range(8):
        mnew[j] = (j < svals)
    d["c_attn"] = np.ascontiguousarray(np.stack([triL, ones, M0, M1, mnew, zeros], axis=1)).astype(ml_dtypes.bfloat16)
    d["c_identf"] = np.eye(128, dtype=f)
    d["c_onesrow"] = np.ones((1, 128), f)
    if cfg.sample_attn:
        d["cache_k"] = inp["cache_k"].reshape(-1, 1024)
        d["cache_v"] = inp["cache_v"].reshape(-1, 1024)
        pt = inp["page_table"][4 * c:4 * c + 4, :cfg.NPG].reshape(-1).astype(np.int32)
        d["ptab"] = np.ascontiguousarray(_bc(pt))
    return d


_NC_CACHE = {}


def kernel(**inputs):
    cfg = Cfg()
    inp = {k: np.asarray(v) for k, v in inputs.items()}
    if "nc" not in _NC_CACHE:
        _NC_CACHE["nc"] = build(cfg)
    nc = _NC_CACHE["nc"]
    consts = make_consts()
    in_maps = []
    for c in range(8):
        d = prep_core(inp, cfg, c)
        prep_core_attn(inp, cfg, c, d)
        d.update(consts)
        in_maps.append(d)
    res = run_bass_kernel_spmd(nc, in_maps, core_ids=list(range(8)))
    R = res.results
    f = np.float32
    SEQ, NT = cfg.SEQ, cfg.NT
    y_prompt = np.empty((4, NT, 128, 1024), f)
    y_sample = np.empty((32, 8, 1024), f)
    re_p = np.empty((2, 4, 64, 64), f)
    im_p = np.empty((2, 4, 64, 64), f)
    k_prompt = np.empty((4, SEQ, 1024), f)
    v_prompt = np.empty((4, SEQ, 1024), f)
    re_s = np.empty((2, 32, 64, 64), f)
    im_s = np.empty((2, 32, 64, 64), f)
    k_sample = np.empty((32, 8, 1024), f)
    v_sample = np.empty((32, 8, 1024), f)
    for c in range(8):
        b, r = c // 2, c % 2
        o = R[c]
        y_prompt[b, r::2] = np.asarray(o["yq_p"]).reshape(cfg.NSLOT, 128, 1024)
        ys, ks, vs = np.asarray(o["y_s"]), np.asarray(o["k_s"]), np.asarray(o["v_s"])
        for i in range(4):
            y_sample[4 * c + i] = ys[32 * i:32 * i + 8]
            k_sample[4 * c + i] = ks[32 * i:32 * i + 8]
            v_sample[4 * c + i] = vs[32 * i:32 * i + 8]
        fs = np.asarray(o["fin_s"]).reshape(2, 128, 64, 4)
        for l in range(2):
            re_s[l, 4 * c:4 * c + 4] = fs[l, :64].transpose(2, 1, 0)
            im_s[l, 4 * c:4 * c + 4] = fs[l, 64:].transpose(2, 1, 0)
        if r == 0:
            fp = np.asarray(o["fin_p"])
            for l in range(2):
                re_p[l, b] = fp[l, :64].T
                im_p[l, b] = fp[l, 64:].T
            k_prompt[b] = np.asarray(o["k_p"])
            v_prompt[b] = np.asarray(o["v_p"])
    return (y_prompt.reshape(4, SEQ, 1024), y_sample, re_p, im_p,
            k_prompt.reshape(4, SEQ, 16, 64), v_prompt.reshape(4, SEQ, 16, 64),
            re_s, im_s, k_sample.reshape(32, 8, 16, 64), v_sample.reshape(32, 8, 16, 64))
```

```python
import math
import contextlib
import numpy as np
import ml_dtypes
import concourse.bass as bass
import concourse.mybir as mybir
from concourse.bass_utils import run_bass_kernel_spmd

F32 = mybir.dt.float32
BF16 = mybir.dt.bfloat16
I32 = mybir.dt.int32
AF = mybir.ActivationFunctionType
ALU = mybir.AluOpType

NDMA_LANES = 6
DBG = {}
TWO_PI = 2.0 * math.pi
NE = 17


import types


def _freeze(fn):
    if fn is None or fn.__closure__ is None:
        return fn
    cells = []
    for c in fn.__closure__:
        try:
            cells.append(types.CellType(c.cell_contents))
        except ValueError:
            cells.append(c)
    return types.FunctionType(fn.__code__, fn.__globals__, fn.__name__, fn.__defaults__, tuple(cells))


class Sched:
    PHYS = ("pe", "act", "dve", "pool", "sp")

    def __init__(self, nc):
        self.nc = nc
        self.ops = {p: [] for p in self.PHYS}
        self.lanes = {}
        self.seen = {p: {} for p in self.PHYS}
        self.lastw = {}
        self.readers = {}
        self.dma_i = {p: 0 for p in self.PHYS}
        self.nops = 0

    def op(self, phys, fn, reads=(), writes=(), dma=False):
        fn = _freeze(fn)
        if dma:
            lane = "%s_q%d" % (phys, self.dma_i[phys] % NDMA_LANES)
            self.dma_i[phys] += 1
        else:
            lane = phys
        if lane not in self.lanes:
            self.lanes[lane] = 0
        waits = {}
        seen = self.seen[phys]

        def need(l2, v):
            if seen.get(l2, 0) < v and waits.get(l2, 0) < v:
                waits[l2] = v

        if dma and self.lanes[lane] > 0:
            need(lane, self.lanes[lane])
        for k in reads:
            if k in self.lastw:
                need(*self.lastw[k])
        for k in writes:
            if k in self.lastw:
                need(*self.lastw[k])
            for l2, v in self.readers.get(k, {}).items():
                need(l2, v)
        for l2, v in waits.items():
            seen[l2] = v
        inc = 16 if dma else 1
        self.lanes[lane] += inc
        val = self.lanes[lane]
        self.ops[phys].append((sorted(waits.items()), fn, lane, inc))
        for k in writes:
            self.lastw[k] = (lane, val)
            self.readers[k] = {}
        for k in reads:
            self.readers.setdefault(k, {})[lane] = val
        self.nops += 1
        return val

    def barrier(self):
        cur = sorted(self.lanes.items())
        for p in self.PHYS:
            w = [(l, v) for l, v in cur if self.seen[p].get(l, 0) < v]
            for l, v in w:
                self.seen[p][l] = v
            if w:
                self.ops[p].append((w, None, None, 0))
        self.lastw = {}
        self.readers = {}

    def finish(self):
        self.barrier()

    def emit(self):
        nc = self.nc
        with contextlib.ExitStack() as st:
            sems = {}
            for lane in self.lanes:
                sems[lane] = st.enter_context(nc.semaphore("s_" + lane))
            block = st.enter_context(nc.Block())

            def replay(phys, eng):
                for waits, fn, lane, inc in self.ops[phys]:
                    for l2, v in waits:
                        if v > 0:
                            eng.wait_ge(sems[l2], v)
                    if fn is not None:
                        ins = fn(eng)
                        ins.then_inc(sems[lane], inc)

            @block.tensor
            def _(e):
                replay("pe", e)

            @block.scalar
            def _(e):
                replay("act", e)

            @block.vector
            def _(e):
                replay("dve", e)

            @block.gpsimd
            def _(e):
                replay("pool", e)

            @block.sync
            def _(e):
                replay("sp", e)


class Arena:
    def __init__(self, t, nwords):
        self.t = t
        self.n = nwords
        self.off = 0
        self.peak = 0

    def mark(self):
        return self.off

    def release(self, m):
        self.off = m

    def f32(self, shape):
        n = int(np.prod(shape))
        assert self.off + n <= self.n, ("arena overflow", self.off, n, self.n)
        ap = self.t[:, self.off:self.off + n]
        self.off += n
        self.peak = max(self.peak, self.off)
        return self._shape(ap, shape)

    def i32(self, shape):
        n = int(np.prod(shape))
        assert self.off + n <= self.n, ("arena overflow", self.off, n, self.n)
        ap = self.t[:, self.off:self.off + n].bitcast(I32)
        self.off += n
        self.peak = max(self.peak, self.off)
        return self._shape(ap, shape)

    def bf(self, shape):
        n = int(np.prod(shape))
        w = (n + 1) // 2
        assert self.off + w <= self.n, ("arena overflow", self.off, w, self.n)
        ap = self.t[:, self.off:self.off + w].bitcast(BF16)
        if 2 * w != n:
            ap = ap[:, 0:n]
        self.off += w
        self.peak = max(self.peak, self.off)
        return self._shape(ap, shape)

    @staticmethod
    def _shape(ap, shape):
        if len(shape) == 1:
            return ap
        if len(shape) == 2:
            return ap.rearrange("p (a b) -> p a b", a=shape[0])
        if len(shape) == 3:
            return ap.rearrange("p (a b c) -> p a b c", a=shape[0], b=shape[1])
        if len(shape) == 4:
            return ap.rearrange("p (a b c d) -> p a b c d", a=shape[0], b=shape[1], c=shape[2])
        raise ValueError(shape)


def flat(ap):
    nd = len(ap.shape)
    if nd == 2:
        return ap
    if nd == 3:
        return ap.rearrange("p a b -> p (a b)")
    if nd == 4:
        return ap.rearrange("p a b c -> p (a b c)")
    if nd == 5:
        return ap.rearrange("p a b c d -> p (a b c d)")
    raise ValueError


class Cfg:
    def __init__(self, SEQ=4096, NPG=64, stages=("s5", "kv", "sb"), n_s5=2, n_sb=2, NPHYS=2560, QG=4, sample=True, sample_attn=True):
        self.NPHYS = NPHYS
        self.QG = QG
        self.sample = sample
        self.sample_attn = sample_attn
        self.SEQ = SEQ
        self.NT = SEQ // 128
        self.NCHP = SEQ // 8
        self.NB = self.NCHP // 128
        self.NPG = NPG
        self.NSLOT = self.NT // 2
        self.stages = stages
        self.n_s5 = n_s5
        self.n_sb = n_sb
        assert self.NCHP % 128 == 0 and self.NCHP <= 512


def build(cfg):
    nc = bass.Bass("TRN2", target_bir_lowering=False)
    SEQ, NT, NCHP, NB = cfg.SEQ, cfg.NT, cfg.NCHP, cfg.NB

    def din(name, shape, dt=F32):
        return nc.dram_tensor(name, list(shape), dt, kind="ExternalInput").ap()

    def dout(name, shape, dt=F32):
        return nc.dram_tensor(name, list(shape), dt, kind="ExternalOutput").ap()

    def dscr(name, shape, dt=F32):
        return nc.dram_tensor(name, list(shape), dt, kind="Internal").ap()

    xp = din("xp", [SEQ, 1024])
    xs = din("xs", [128, 1024])
    s5pA = din("s5pA", [2, 128, 3, 4096])
    s5pB = din("s5pB", [2, 128, 2, 4096])
    s5pAT = din("s5pAT", [2, 128, 3, 64])
    s5pBT = din("s5pBT", [2, 128, 2, 64 * 16])
    s5pCT = din("s5pCT", [2, 128, 2, 64 * 16])
    s5cols = din("s5cols", [2, 128, 24])
    s5gpost = din("s5gpost", [2, 128, 1024])
    s5h0 = din("s5h0", [2, 128, 2, 256])
    a_w_in = din("a_w_in", [2, 1024, 2048])
    a_w_glu = din("a_w_glu", [2, 1024, 1024])
    a_w_out = din("a_w_out", [2, 1024, 1024])
    c_ident = din("c_ident", [128, 128], BF16)
    c_mask4 = din("c_mask4", [128, 512])
    c_pswap = din("c_pswap", [128, 128])
    c_cols = din("c_cols", [128, 4])
    c_iota = din("c_iota", [128, 512])
    c_evals = din("c_evals", [128, NE * 64])

    w_kv = din("w_kv", [1024, 2048])
    b_w_in = din("b_w_in", [2, 1024, 2048])
    b_w_out = din("b_w_out", [2, 1024, 1024])
    sbcols = din("sbcols", [128, 24])
    sbgpost = din("sbgpost", [2, 128, 1024])
    sbbias = din("sbbias", [2, 128, 16])
    sbbiasrow = din("sbbiasrow", [2, 128, 128])
    c_core = din("c_core", [128, 4])
    c_attn = din("c_attn", [128, 6, 128], BF16)
    c_identf = din("c_identf", [128, 128])
    c_onesrow = din("c_onesrow", [1, 128])
    NPHYS = cfg.NPHYS
    if cfg.sample_attn:
        cache_k = din("cache_k", [NPHYS * 128, 1024])
        cache_v = din("cache_v", [NPHYS * 128, 1024])
        ptab = din("ptab", [128, 4 * cfg.NPG], I32)
    k_p = dout("k_p", [SEQ, 1024])
    v_p = dout("v_p", [SEQ, 1024])
    k_s = dout("k_s", [128, 1024])
    v_s = dout("v_s", [128, 1024])
    yq_p = dout("yq_p", [cfg.NSLOT * 128, 1024])
    xq = dscr("xq", [cfg.NSLOT * 128 + 128, 1024])
    y_p = dout("y_p", [SEQ, 1024]) if cfg.stages == ("s5",) else None
    y_s = dout("y_s", [128, 1024])
    fin_p = dout("fin_p", [2, 128, 64])
    fin_s = dout("fin_s", [2, 128, 256])

    xa = dscr("xa", [SEQ + 128, 1024])
    xb = dscr("xb", [SEQ + 128, 1024])
    scrU = dscr("scrU", [8, 16, 8, NCHP], BF16)
    scrY = dscr("scrY", [8, 16, 8, NCHP], BF16)
    scrUs = dscr("scrUs", [8, 16, 8, 4], BF16)
    scrYs = dscr("scrYs", [8, 16, 8, 4], BF16)

    with contextlib.ExitStack() as st:
        NW = 52600
        idx_t = st.enter_context(nc.sbuf_tensor("idx_t", [128, 4 * cfg.NPG], I32))
        arena_t = st.enter_context(nc.sbuf_tensor("arena", [128, NW], F32))
        A = Arena(arena_t, NW)
        pbank = [st.enter_context(nc.psum_tensor("pb%d" % i, [128, 512], F32)) for i in range(7)]
        ptr = st.enter_context(nc.psum_tensor("ptr", [128, 8, 128], BF16))
        S = Sched(nc)

        def dma(out, in_, reads=(), writes=()):
            S.op("sp", lambda e: e.dma_start(out=out, in_=in_), reads=reads, writes=writes, dma=True)

        def tt(eng, out, a, b, op, reads, writes):
            S.op(eng, lambda e: e.tensor_tensor(out=out, in0=a, in1=b, op=op), reads=reads, writes=writes)

        def ts(eng, out, a, s1, s2, op0, op1, reads, writes):
            if s2 is None:
                S.op(eng, lambda e: e.tensor_scalar(out=out, in0=a, scalar1=s1, scalar2=None, op0=op0), reads=reads, writes=writes)
            else:
                S.op(eng, lambda e: e.tensor_scalar(out=out, in0=a, scalar1=s1, scalar2=s2, op0=op0, op1=op1), reads=reads, writes=writes)

        def stt(eng, out, a, sc, b, op0, op1, reads, writes):
            S.op(eng, lambda e: e.scalar_tensor_tensor(out=out, in0=a, scalar=sc, in1=b, op0=op0, op1=op1), reads=reads, writes=writes)

        def cp(eng, out, a, reads, writes):
            S.op(eng, lambda e: e.tensor_copy(out=out, in_=a), reads=reads, writes=writes)

        def act(out, a, func, reads, writes, **kw):
            S.op("act", lambda e: e.activation(out=out, in_=a, func=func, **kw), reads=reads, writes=writes)

        def mm(out, pairs, reads, writes):
            def fn(e):
                n = len(pairs)
                ins = None
                for i, (l, r) in enumerate(pairs):
                    ins = e.matmul(out, lhsT=l, rhs=r, start=(i == 0), stop=(i == n - 1))
                return ins
            S.op("pe", fn, reads=reads, writes=writes)

        def mm_multi(groups, reads, writes):
            def fn(e):
                ins = None
                for out, pairs in groups:
                    n = len(pairs)
                    for i, (l, r) in enumerate(pairs):
                        ins = e.matmul(out, lhsT=l, rhs=r, start=(i == 0), stop=(i == n - 1))
                return ins
            S.op("pe", fn, reads=reads, writes=writes)

        ident = A.bf([128])
        mask4 = A.f32([512])
        pswap = A.f32([128])
        ccols = A.f32([4])
        iota = A.f32([512])
        dma(ident, c_ident[:, :], writes=["ident"])
        dma(mask4, c_mask4[:, :], writes=["mask4"])
        dma(pswap, c_pswap[:, :], writes=["pswap"])
        dma(ccols, c_cols[:, :], writes=["ccols"])
        dma(iota, c_iota[:, :], writes=["iota"])
        esc = ccols[:, 0:1]
        sgnA = ccols[:, 1:2]
        nsgnA = ccols[:, 2:3]
        CONST_KEYS = ["ident", "mask4", "pswap", "ccols", "iota"]

        def phase_barrier():
            S.barrier()
            for k in CONST_KEYS:
                S.lastw.pop(k, None)

        def norm_T(src_ap, xt, hb, hT, tag, sscol, junk):
            if src_ap is not None:
                dma(xt, src_ap, writes=["xt" + tag])
            act(junk, xt, AF.Square, reads=["xt" + tag], writes=["junk", "ss" + tag], accum_out=sscol[:, 0:1])
            ts("dve", sscol[:, 1:2], sscol[:, 0:1], 1.0 / 1024.0, 1e-6, ALU.mult, ALU.add, ["ss" + tag], ["ss1" + tag])
            act(sscol[:, 2:3], sscol[:, 1:2], AF.Sqrt, reads=["ss1" + tag], writes=["ss2" + tag])
            S.op("dve", lambda e: e.reciprocal(out=sscol[:, 3:4], in_=sscol[:, 2:3]), reads=["ss2" + tag], writes=["ss3" + tag])
            S.op("act", lambda e: e.mul(out=hb, in_=xt, mul=sscol[:, 3:4]), reads=["xt" + tag, "ss3" + tag], writes=["hb" + tag])

            def tp(e):
                ins = None
                for k in range(8):
                    ins = e.transpose(ptr[:, k, :], hb[:, k * 128:(k + 1) * 128], ident)
                return ins
            S.op("pe", tp, reads=["hb" + tag, "ident"], writes=["ptr"])
            cp("dve", hT, ptr[:, :, :], ["ptr"], ["hT" + tag])

        def load_weight(dst_bf, w_dram, ncols, gcol, stage, key):
            for k in range(8):
                q = k % 2
                dma(stage[q][:, 0:ncols], w_dram[k * 128:(k + 1) * 128, :], writes=["wst%d" % q])
                if gcol is None:
                    cp("pool", dst_bf[:, k, :], stage[q][:, 0:ncols], ["wst%d" % q], [key])
                else:
                    ts("pool", dst_bf[:, k, :], stage[q][:, 0:ncols], gcol[:, k:k + 1], None, ALU.mult, None, ["wst%d" % q, "s5cols"], [key])

        def s5_layer(l, src_p, src_s, dst_p, dst_s):
            m0 = A.mark()
            UY = A.bf([8, 8, NCHP])
            UYs = A.bf([8, 8, 4])
            cols = A.f32([24])
            dma(cols, s5cols[l], writes=["s5cols"])
            gpre, Dcol, bglu = cols[:, 0:8], cols[:, 8:16], cols[:, 16:24]
            fin = A.f32([64])
            fins = A.f32([64, 4])
            h0 = A.f32([2, 256])
            h0b = A.bf([64, 4])
            dma(h0, s5h0[l], writes=["h0"])
            cp("pool", flat(h0b), h0[:, 0, :], ["h0"], ["h0b"])
            m1 = A.mark()
            xt = [A.f32([1024]) for _ in range(2)]
            hb = [A.bf([1024]) for _ in range(2)]
            hT = [A.bf([8, 128]) for _ in range(2)]
            sscol = [A.f32([4]) for _ in range(2)]
            junk = A.bf([1024])

            wA = A.bf([8, 1024])
            stage = [A.f32([1024]) for _ in range(2)]
            load_weight(wA, a_w_in[l][:, 0:1024], 1024, gpre, stage, "wA")
            for i in range(NT + 1):
                q = i % 2
                tag = str(q)
                src = src_p[i * 128:(i + 1) * 128, :] if i < NT else src_s
                norm_T(src, xt[q], hb[q], hT[q], tag, sscol[q], junk)
                for half in range(2):
                    pb = pbank[half * 2 + q]
                    groups = []
                    for jj in range(4):
                        j = half * 4 + jj
                        groups.append((pb[:, jj * 128:(jj + 1) * 128],
                                       [(wA[:, k, j * 128:(j + 1) * 128], hT[q][:, k, :]) for k in range(8)]))
                    mm_multi(groups, ["wA", "hT" + tag], ["pb%d" % (half * 2 + q)])
                    pv = pb[:, :].rearrange("p (j m) -> p j m", j=4)
                    if i < NT:
                        outv = UY[:, half * 4:half * 4 + 4, :, i * 16:(i + 1) * 16]
                        inv = pv.rearrange("p j (n s) -> p j s n", s=8)
                        S.op("act", lambda e, outv=outv, inv=inv: e.copy(out=outv, in_=inv),
                             reads=["pb%d" % (half * 2 + q)], writes=["UY"])
                    else:
                        outv = UYs[:, half * 4:half * 4 + 4, :, :]
                        inv = pv.rearrange("p j (i r) -> p j r i", r=32)[:, :, 0:8, :]
                        S.op("act", lambda e, outv=outv, inv=inv: e.copy(out=outv, in_=inv),
                             reads=["pb%d" % (half * 2 + q)], writes=["UYs"])
            A.release(m1)
            phase_barrier()

            m2 = A.mark()
            pAT = A.f32([3, 64])
            pBT = A.f32([2, 64, 16])
            pCT = A.f32([2, 64, 16])
            dma(pAT, s5pAT[l], writes=["pAT"])
            dma(flat(pBT), s5pBT[l].rearrange("p a b -> p (a b)"), writes=["pBT"])
            dma(flat(pCT), s5pCT[l].rearrange("p a b -> p (a b)"), writes=["pCT"])
            PEr = A.f32([NE, 64])
            PEi = A.f32([NE, 64])
            magE = A.f32([NE, 64])
            sm = [A.f32([64]) for _ in range(10)]
            Ewr = A.f32([8, 64])
            Ewi = A.f32([8, 64])
            Prs = A.f32([9, 64])
            f8 = A.f32([64])
            mT = A.mark()
            evals = A.f32([NE, 64])
            dma(flat(evals), c_evals[:, :], writes=["evals"])
            tE = [A.f32([NE, 64]) for _ in range(3)]
            tEi = A.i32([NE, 64])
            Ewt = A.f32([8, 64])
            ArT, AiT, LdT = pAT[:, 0, :], pAT[:, 1, :], pAT[:, 2, :]

            def bcE(v):
                return v.unsqueeze(1).broadcast_to([128, NE, 64])

            act(sm[0], LdT, AF.Exp, ["pAT"], ["sm0"])
            tt("dve", sm[1], sm[0], ArT, ALU.mult, ["sm0", "pAT"], ["sm1"])
            stt("dve", sm[2], sm[0], 1.0 / TWO_PI, AiT, ALU.mult, ALU.mult, ["sm0", "pAT"], ["sm2"])
            tt("dve", tE[0], evals, bcE(sm[1]), ALU.mult, ["evals", "sm1"], ["tE0"])
            act(magE, tE[0], AF.Exp, ["tE0"], ["magE"])
            tt("dve", tE[0], evals, bcE(sm[2]), ALU.mult, ["evals", "sm2", "magE"], ["tE0"])
            cp("dve", tEi, tE[0], ["tE0"], ["tEi"])
            tt("dve", tE[1], tE[0], tEi, ALU.subtract, ["tE0", "tEi"], ["tE1"])
            ts("dve", tE[0], tE[0], 0.25, None, ALU.add, None, ["tE0", "tE1"], ["tE0"])
            cp("dve", tEi, tE[0], ["tE0", "tE1"], ["tEi"])
            tt("dve", tE[2], tE[0], tEi, ALU.subtract, ["tE0", "tEi"], ["tE2"])
            act(tE[1], tE[1], AF.Sin, ["tE1"], ["tE1"], scale=TWO_PI)
            act(tE[2], tE[2], AF.Sin, ["tE2"], ["tE2"], scale=TWO_PI)
            tt("dve", PEr, magE, tE[2], ALU.mult, ["magE", "tE2"], ["PEr"])
            tt("dve", PEi, magE, tE[1], ALU.mult, ["magE", "tE1"], ["PEi"])
            PK = ["PEr", "PEi"]
            lr1, li1 = PEr[:, 9, :], PEi[:, 9, :]
            ts("dve", sm[3], lr1, -1.0, None, ALU.add, None, PK, ["sm3"])
            tt("dve", sm[4], ArT, ArT, ALU.mult, ["pAT"], ["sm4"])
            tt("dve", sm[5], AiT, AiT, ALU.mult, ["pAT"], ["sm5"])
            tt("dve", sm[4], sm[4], sm[5], ALU.add, ["sm4", "sm5"], ["sm4"])
            S.op("dve", lambda e: e.reciprocal(out=sm[4], in_=sm[4]), reads=["sm4"], writes=["sm4"])
            tt("dve", sm[5], sm[3], ArT, ALU.mult, ["sm3", "pAT"], ["sm5"])
            tt("dve", sm[6], li1, AiT, ALU.mult, PK + ["pAT"], ["sm6"])
            tt("dve", sm[5], sm[5], sm[6], ALU.add, ["sm5", "sm6"], ["sm5"])
            tt("dve", sm[7], sm[5], sm[4], ALU.mult, ["sm5", "sm4"], ["sm7"])
            tt("dve", sm[5], li1, ArT, ALU.mult, PK + ["pAT", "sm7"], ["sm5"])
            tt("dve", sm[6], sm[3], AiT, ALU.mult, ["sm3", "pAT", "sm5"], ["sm6"])
            tt("dve", sm[5], sm[5], sm[6], ALU.subtract, ["sm5", "sm6"], ["sm5"])
            tt("dve", sm[8], sm[5], sm[4], ALU.mult, ["sm5", "sm4"], ["sm8"])
            wre, wim = sm[7], sm[8]

            def bc8(v):
                return v.unsqueeze(1).broadcast_to([128, 8, 64])
            tt("dve", Ewr, PEr[:, 0:8, :], bc8(wre), ALU.mult, PK + ["sm7"], ["Ewr"])
            tt("dve", Ewt, PEi[:, 0:8, :], bc8(wim), ALU.mult, PK + ["sm8"], ["Ewt"])
            tt("dve", Ewr, Ewr, Ewt, ALU.subtract, ["Ewr", "Ewt"], ["Ewr"])
            tt("dve", Ewi, PEr[:, 0:8, :], bc8(wim), ALU.mult, PK + ["sm8"], ["Ewi"])
            tt("dve", Ewt, PEi[:, 0:8, :], bc8(wre), ALU.mult, PK + ["sm7", "Ewr"], ["Ewt"])
            tt("dve", Ewi, Ewi, Ewt, ALU.add, ["Ewi", "Ewt"], ["Ewi"])
            ts("dve", flat(Ewi), flat(Ewi), nsgnA, None, ALU.mult, None, ["Ewi", "ccols"], ["Ewi"])
            ts("dve", flat(Prs), flat(PEr[:, 8:17, :]), sgnA, None, ALU.mult, None, PK + ["ccols"], ["Prs"])
            ts("dve", f8, sm[2], 8.0, None, ALU.mult, None, ["sm2"], ["f8"])
            rho8 = magE[:, 16, :]
            a8, b8 = PEr[:, 16, :], PEi[:, 16, :]

            phase_barrier()
            A.release(mT)
            LBs = A.bf([8, 8, 16])
            RCs = A.bf([8, 9, 16])
            Tt = A.bf([8, 128])
            Wt = A.bf([8, 2, 64])
            tl = [A.f32([8, 9, 16]) for _ in range(2)]
            pAs = A.f32([3, 512])
            pBs = A.f32([2, 512])
            tw = [A.f32([512]) for _ in range(8)]
            twi = A.i32([512])
            U = A.bf([8, NCHP])
            Us = A.bf([8, 4])
            Yg = A.bf([8, NCHP])
            Ygs = A.bf([8, 4])
            yT = A.bf([8, NCHP])
            yTs = A.bf([8, 4])
            ga0 = [A.f32([NCHP]) for _ in range(5)]
            ga = [ga0, ga0]
            gi0 = A.i32([NCHP])
            gi = [gi0, gi0]
            Hp = [A.bf([NCHP]) for _ in range(2)]
            gb = [A.f32([NCHP]) for _ in range(2)]
            for q in range(2):
                S.op("pool", lambda e, q=q: e.memset(Hp[q][:, 0:1], 0.0), writes=["Hp%d" % q])
            pM = pbank[6]

            for j in range(8):
                g0 = 8 * j
                in0 = Ewr[:, :, g0:g0 + 8].rearrange("p s g -> p g s").unsqueeze(3).broadcast_to([128, 8, 8, 16])
                in1 = pBT[:, 0, g0:g0 + 8, :].unsqueeze(2).broadcast_to([128, 8, 8, 16])
                tl0 = tl[0][:, :, 0:8, :]
                tl1 = tl[1][:, :, 0:8, :]
                tt("dve", tl0, in0, in1, ALU.mult, ["Ewr", "pBT"], ["tl0"])
                in0 = Ewi[:, :, g0:g0 + 8].rearrange("p s g -> p g s").unsqueeze(3).broadcast_to([128, 8, 8, 16])
                in1 = pBT[:, 1, g0:g0 + 8, :].unsqueeze(2).broadcast_to([128, 8, 8, 16])
                tt("dve", tl1, in0, in1, ALU.mult, ["Ewi", "pBT"], ["tl1"])
                tt("dve", LBs, tl0, tl1, ALU.add, ["tl0", "tl1"], ["LBs"])
                in0 = Prs[:, :, g0:g0 + 8].rearrange("p t g -> p g t").unsqueeze(3).broadcast_to([128, 8, 9, 16])
                in1 = pCT[:, 0, g0:g0 + 8, :].unsqueeze(2).broadcast_to([128, 8, 9, 16])
                tt("dve", tl[0], in0, in1, ALU.mult, ["Prs", "pCT", "tl0"], ["tl0"])
                in0 = PEi[:, 8:17, g0:g0 + 8].rearrange("p t g -> p g t").unsqueeze(3).broadcast_to([128, 8, 9, 16])
                in1 = pCT[:, 1, g0:g0 + 8, :].unsqueeze(2).broadcast_to([128, 8, 9, 16])
                tt("dve", tl[1], in0, in1, ALU.mult, PK + ["pCT", "tl1"], ["tl1"])
                tt("dve", RCs, tl[0], tl[1], ALU.subtract, ["tl0", "tl1"], ["RCs"])
                for hh in range(2):
                    groups = []
                    for gg in range(4):
                        gl = hh * 4 + gg
                        groups.append((pbank[4][:, gg * 128:(gg + 1) * 128],
                                       [(LBs[:, gl, :, :].rearrange("p s c -> p (s c)"),
                                         RCs[:, gl, 0:8, :].rearrange("p t c -> p (t c)"))]))
                    mm_multi(groups, ["LBs", "RCs"], ["pb4"])
                    tt("dve", flat(Tt[:, hh * 4:hh * 4 + 4, :]), pbank[4][:, :], mask4, ALU.mult, ["pb4", "mask4"], ["Tt"])
                dma(pAs, s5pA[l][:, :, g0 * 64:(g0 + 8) * 64], writes=["pAs"])
                dma(pBs, s5pB[l][:, :, g0 * 64:(g0 + 8) * 64], writes=["pBs"])
                AR, AI, LD = pAs[:, 0, :], pAs[:, 1, :], pAs[:, 2, :]
                BR, BI = pBs[:, 0, :], pBs[:, 1, :]
                t = tw
                K = lambda *ix: ["tw%d" % i for i in ix]
                act(t[0], LD, AF.Exp, ["pAs"], K(0))
                tt("dve", t[1], t[0], AR, ALU.mult, K(0) + ["pAs"], K(1))
                stt("dve", t[2], t[0], 1.0 / TWO_PI, AI, ALU.mult, ALU.mult, K(0) + ["pAs"], K(2))
                act(t[3], t[1], AF.Exp, K(1), K(3))
                cp("dve", twi, t[2], K(2), ["twi"])
                tt("dve", t[4], t[2], twi, ALU.subtract, K(2) + ["twi"], K(4))
                ts("dve", t[5], t[2], 0.25, None, ALU.add, None, K(2), K(5))
                cp("dve", twi, t[5], K(5, 4), ["twi"])
                tt("dve", t[5], t[5], twi, ALU.subtract, K(5) + ["twi"], K(5))
                act(t[4], t[4], AF.Sin, K(4), K(4), scale=TWO_PI)
                act(t[5], t[5], AF.Sin, K(5), K(5), scale=TWO_PI)
                tt("dve", t[5], t[3], t[5], ALU.mult, K(3, 5), K(5))
                tt("dve", t[4], t[3], t[4], ALU.mult, K(3, 4), K(4))
                ts("dve", t[5], t[5], -1.0, None, ALU.add, None, K(5), K(5))
                tt("dve", t[0], AR, AR, ALU.mult, ["pAs"] + K(0, 1, 2), K(0))
                tt("dve", t[3], AI, AI, ALU.mult, ["pAs"] + K(3, 4, 5), K(3))
                tt("dve", t[0], t[0], t[3], ALU.add, K(0, 3), K(0))
                S.op("dve", lambda e, a=t[0]: e.reciprocal(out=a, in_=a), reads=K(0), writes=K(0))
                tt("dve", t[3], t[5], AR, ALU.mult, K(5) + ["pAs"], K(3))
                tt("dve", t[6], t[4], AI, ALU.mult, K(4) + ["pAs"], K(6))
                tt("dve", t[3], t[3], t[6], ALU.add, K(3, 6), K(3))
                tt("dve", t[3], t[3], t[0], ALU.mult, K(3, 0), K(3))
                tt("dve", t[6], t[4], AR, ALU.mult, K(4, 3) + ["pAs"], K(6))
                tt("dve", t[7], t[5], AI, ALU.mult, K(5) + ["pAs"], K(7))
                tt("dve", t[6], t[6], t[7], ALU.subtract, K(6, 7), K(6))
                tt("dve", t[6], t[6], t[0], ALU.mult, K(6, 0), K(6))
                tt("dve", t[0], t[3], BR, ALU.mult, K(3, 0) + ["pBs"], K(0))
                tt("dve", t[4], t[6], BI, ALU.mult, K(6, 4) + ["pBs"], K(4))
                tt("dve", t[0], t[0], t[4], ALU.subtract, K(0, 4), K(0))
                tt("dve", t[4], t[3], BI, ALU.mult, K(3, 4) + ["pBs"], K(4))
                tt("dve", t[5], t[6], BR, ALU.mult, K(6, 5) + ["pBs"], K(5))
                tt("dve", t[4], t[4], t[5], ALU.add, K(4, 5), K(4))
                act(t[3], t[1], AF.Exp, K(1, 3), K(3), scale=esc)
                ts("dve", t[5], t[2], esc, None, ALU.mult, None, K(2, 5) + ["ccols"], K(5))
                cp("dve", twi, t[5], K(5), ["twi"])
                tt("dve", t[6], t[5], twi, ALU.subtract, K(5, 6) + ["twi"], K(6))
                ts("dve", t[5], t[5], 0.25, None, ALU.add, None, K(5, 6), K(5))
                cp("dve", twi, t[5], K(5, 6), ["twi"])
                tt("dve", t[5], t[5], twi, ALU.subtract, K(5) + ["twi"], K(5))
                act(t[6], t[6], AF.Sin, K(6), K(6), scale=TWO_PI)
                act(t[5], t[5], AF.Sin, K(5), K(5), scale=TWO_PI)
                tt("dve", t[5], t[3], t[5], ALU.mult, K(3, 5), K(5))
                tt("dve", t[6], t[3], t[6], ALU.mult, K(3, 6), K(6))
                v3 = lambda a: a.rearrange("p (g q) -> p g q", g=8)
                tt("dve", t[3], t[5], t[0], ALU.mult, K(5, 0, 3), K(3))
                tt("dve", t[7], t[6], t[4], ALU.mult, K(6, 4, 7), K(7))
                tt("dve", Wt[:, :, 0, :], v3(t[3]), v3(t[7]), ALU.subtract, K(3, 7), ["Wt"])
                tt("dve", t[3], t[5], t[4], ALU.mult, K(5, 4, 3), K(3))
                tt("dve", t[7], t[6], t[0], ALU.mult, K(6, 0, 7), K(7))
                tt("dve", Wt[:, :, 1, :], v3(t[3]), v3(t[7]), ALU.add, K(3, 7), ["Wt"])
                dma(scrU.rearrange("g c s n -> (g c) s n"), UY[:, j, :, :], reads=["UY"], writes=["scrU"])
                for s_ in range(8):
                    dma(U[16 * s_:16 * s_ + 16, :, :], scrU[:, :, s_, :].rearrange("g c n -> c g n"),
                        reads=["scrU"], writes=["U"])
                dma(scrUs.rearrange("g c s n -> (g c) s n"), UYs[:, j, :, :], reads=["UYs"], writes=["scrUs"])
                for s_ in range(8):
                    dma(Us[16 * s_:16 * s_ + 16, :, :], scrUs[:, :, s_, :].rearrange("g c n -> c g n"),
                        reads=["scrUs"], writes=["Us"])
                for gl in range(8):
                    g = g0 + gl
                    q = gl % 2
                    a = ga[q]
                    KA = lambda *ix: ["ga_%d" % i for i in ix]
                    pS, pSw, pG, pY = pbank[q], pbank[2 + q], pbank[4], pbank[5]
                    Wg = Wt[:, gl, :, :]
                    Ug = U[:, gl, :]
                    mm_multi([(pS[:, 0:NCHP], [(Wg.rearrange("p r q -> p (r q)"), Ug)]),
                              (pSw[64:128, 0:NCHP], [(Wg[:, 0, :], Ug)]),
                              (pSw[0:64, 0:NCHP], [(Wg[:, 1, :], Ug)])],
                             ["Wt", "U"], ["pb%d" % q, "pb%d" % (2 + q)])
                    ts("dve", a[0], iota[:, 0:NCHP], f8[:, g:g + 1], None, ALU.mult, None, ["iota", "f8"] + KA(0), KA(0))
                    cp("dve", gi[q], a[0], KA(0), ["gi"])
                    tt("dve", a[1], a[0], gi[q], ALU.subtract, KA(0, 1) + ["gi"], KA(1))
                    ts("dve", a[0], a[0], 0.25, None, ALU.add, None, KA(0, 1), KA(0))
                    cp("dve", gi[q], a[0], KA(0, 1), ["gi"])
                    tt("dve", a[0], a[0], gi[q], ALU.subtract, KA(0) + ["gi"], KA(0))
                    act(a[1], a[1], AF.Sin, KA(1), KA(1), scale=TWO_PI)
                    act(a[0], a[0], AF.Sin, KA(0), KA(0), scale=TWO_PI)
                    tt("dve", a[2], a[0], pS[:, 0:NCHP], ALU.mult, KA(0, 2) + ["pb%d" % q], KA(2))
                    stt("dve", a[3], a[1], sgnA, pSw[:, 0:NCHP], ALU.mult, ALU.mult, KA(1, 3) + ["ccols", "pb%d" % (2 + q)], KA(3))
                    tt("dve", a[2], a[2], a[3], ALU.add, KA(2, 3), KA(2))
                    S.op("dve", lambda e, o=a[3], d1=a[2], g=g: e.tensor_tensor_scan(
                        out=o, data0=rho8[:, g:g + 1].to_broadcast([128, NCHP]), data1=d1, initial=0.0,
                        op0=ALU.mult, op1=ALU.add), reads=KA(2, 3) + ["magE"], writes=KA(3))
                    mm(pG[:, 0:NCHP], [(pswap, a[3])], ["pswap"] + KA(3), ["pb4"])
                    tt("dve", a[2], a[0], a[3], ALU.mult, KA(0, 3, 2), KA(2))
                    stt("dve", a[4], a[1], nsgnA, pG[:, 0:NCHP], ALU.mult, ALU.mult, KA(1, 4) + ["ccols", "pb4"], KA(4))
                    tt("dve", Hp[q][:, 1:NCHP], a[2][:, 0:NCHP - 1], a[4][:, 0:NCHP - 1], ALU.add, KA(2, 4), ["Hp%d" % q])
                    tt("dve", fin[:, g:g + 1], a[2][:, NCHP - 1:NCHP], a[4][:, NCHP - 1:NCHP], ALU.add, KA(2, 4), ["fin"])
                    Tg = Tt[:, gl, :]
                    Vg = RCs[:, gl, 1:9, :].rearrange("p t c -> p (t c)")
                    mm(pY[:, 0:NCHP], [(Tg, Ug), (Vg, Hp[q])], ["Tt", "RCs", "U", "Hp%d" % q], ["pb5"])
                    S.op("act", lambda e, o=Yg[:, gl, :], i_=pY[:, 0:NCHP]: e.copy(out=o, in_=i_), reads=["pb5"], writes=["Yg"])
                    mm_multi([(pM[:, g * 4:g * 4 + 4], [(Wg.rearrange("p r q -> p (r q)"), Us[:, gl, :])]),
                              (pM[:, 256 + gl * 4:256 + gl * 4 + 4], [(Tg, Us[:, gl, :]), (Vg, h0b[:, g, :])])],
                             ["Wt", "Tt", "RCs", "Us", "h0b"], ["pb6"])
                S.op("act", lambda e: e.copy(out=flat(Ygs), in_=pM[:, 256:288]), reads=["pb6"], writes=["Ygs"])
                dma(scrY.rearrange("t c g n -> (t c) g n"), Yg, reads=["Yg"], writes=["scrY"])
                for g_ in range(8):
                    dma(yT[16 * g_:16 * g_ + 16, :, :], scrY[:, :, g_, :].rearrange("t c n -> c t n"),
                        reads=["scrY"], writes=["yT"])
                dma(scrYs.rearrange("t c g n -> (t c) g n"), Ygs, reads=["Ygs"], writes=["scrYs"])
                for g_ in range(8):
                    dma(yTs[16 * g_:16 * g_ + 16, :, :], scrYs[:, :, g_, :].rearrange("t c n -> c t n"),
                        reads=["scrYs"], writes=["yTs"])
                def gelu_block(uview, yview, n, q):
                    b0 = gb[0][:, 0:n]
                    b1 = gb[1][:, 0:n]
                    stt("dve", b0, uview, Dcol[:, j:j + 1], yview, ALU.mult, ALU.add, ["UY", "UYs", "yT", "yTs", "s5cols", "gb0"], ["gb0"])
                    act(b1, b0, AF.Square, ["gb0", "gb1"], ["gb1"])
                    ts("dve", b1, b1, 0.044715, 1.0, ALU.mult, ALU.add, ["gb1"], ["gb1"])
                    tt("dve", b1, b1, b0, ALU.mult, ["gb1", "gb0"], ["gb1"])
                    act(b1, b1, AF.Sigmoid, ["gb1"], ["gb1"], scale=1.5957691216057308)
                    tt("dve", uview, b0, b1, ALU.mult, ["gb0", "gb1"], ["UY", "UYs"])
                for s_ in range(8):
                    gelu_block(UY[:, j, s_, :], yT[:, s_, :], NCHP, 0)
                gelu_block(flat(UYs[:, j, :, :]), flat(yTs), 32, 0)
            dma(fin_p[l], fin, reads=["fin"])
            h0v = h0[:, 0, :].rearrange("p (g i) -> p g i", i=4)
            h0s = h0[:, 1, :].rearrange("p (g i) -> p g i", i=4)
            bc4 = lambda v: v.unsqueeze(2).broadcast_to([128, 64, 4])
            fs2 = A.f32([64, 4])
            b8n = A.f32([64])
            ts("dve", b8n, b8, nsgnA, None, ALU.mult, None, PK + ["ccols"], ["b8n"])
            tt("dve", fins, h0v, bc4(a8), ALU.mult, ["h0"] + PK, ["fins"])
            tt("dve", fs2, h0s, bc4(b8n), ALU.mult, ["h0", "b8n"], ["fs2"])
            tt("dve", fins, fins, fs2, ALU.add, ["fins", "fs2"], ["fins"])
            tt("dve", flat(fins), flat(fins), pM[:, 0:256], ALU.add, ["fins", "pb6"], ["fins"])
            dma(fin_s[l], flat(fins), reads=["fins"])
            A.release(m2)
            phase_barrier()

            m3 = A.mark()
            xt = [A.f32([1024]) for _ in range(2)]
            hb = [A.bf([1024]) for _ in range(2)]
            hT = [A.bf([8, 128]) for _ in range(2)]
            sscol = [A.f32([4]) for _ in range(2)]
            junk = A.bf([1024])
            wZ = A.bf([8, 1024])
            wB = A.bf([8, 1024])
            wC = A.bf([8, 1024])
            gpost = A.f32([1024])
            stage = [A.f32([1024]) for _ in range(2)]
            load_weight(wZ, a_w_in[l][:, 1024:2048], 1024, gpre, stage, "wZ")
            load_weight(wB, a_w_glu[l], 1024, None, stage, "wB")
            load_weight(wC, a_w_out[l], 1024, None, stage, "wC")
            dma(gpost, s5gpost[l], writes=["gpost"])
            sz = A.f32([8, 128])
            sg = A.f32([8, 128])
            vT = A.bf([8, 128])
            ysamp = A.bf([8, 128])
            res = [A.f32([1024]) for _ in range(2)]
            ss2 = [A.f32([4]) for _ in range(2)]
            S.op("pool", lambda e: e.memset(flat(ysamp), 0.0), writes=["ysamp"])
            cp("dve", ysamp.rearrange("p k (i r) -> p k r i", r=32)[:, :, 0:8, :], UYs, ["UYs", "ysamp"], ["ysamp"])
            src_pv = src_p.rearrange("(n s) f -> s n f", s=8)
            dst_pv = dst_p.rearrange("(n s) f -> s n f", s=8)
            sets = [(s_, nb) for s_ in range(8) for nb in range(NB)] + [None]
            for it, tset in enumerate(sets):
                q = it % 2
                tag = str(q)
                if tset is not None:
                    s_, nb = tset
                    src = src_pv[s_, nb * 128:(nb + 1) * 128, :]
                    dst = dst_pv[s_, nb * 128:(nb + 1) * 128, :]
                    yv = UY[:, :, s_, nb * 128:(nb + 1) * 128]
                else:
                    src, dst = src_s, dst_s
                    yv = ysamp
                norm_T(src, xt[q], hb[q], hT[q], tag, sscol[q], junk)
                for half in range(2):
                    pb = pbank[half]
                    groups = []
                    for jj in range(4):
                        o = half * 4 + jj
                        groups.append((pb[:, jj * 128:(jj + 1) * 128],
                                       [(wZ[:, k, o * 128:(o + 1) * 128], hT[q][:, k, :]) for k in range(8)]))
                    mm_multi(groups, ["wZ", "hT" + tag], ["pb%d" % half])
                    act(flat(sz[:, half * 4:half * 4 + 4, :]), pb[:, :], AF.Silu, ["pb%d" % half], ["sz"])
                for half in range(2):
                    pb = pbank[2 + half]
                    groups = []
                    for jj in range(4):
                        o = half * 4 + jj
                        groups.append((pb[:, jj * 128:(jj + 1) * 128],
                                       [(wB[:, k, o * 128:(o + 1) * 128], yv[:, k, :]) for k in range(8)]))
                    mm_multi(groups, ["wB", "UY", "ysamp"], ["pb%d" % (2 + half)])
                    for jj in range(4):
                        o = half * 4 + jj
                        act(sg[:, o, :], pb[:, jj * 128:(jj + 1) * 128], AF.Sigmoid, ["pb%d" % (2 + half), "s5cols"], ["sg"],
                            bias=bglu[:, o:o + 1])
                tt("dve", sg, sg, sz, ALU.mult, ["sg", "sz"], ["sg"])
                tt("dve", vT, sg, yv, ALU.mult, ["sg", "UY", "ysamp"], ["vT"])
                for half in range(2):
                    pb = pbank[4 + half]
                    mm(pb[:, :], [(vT[:, k, :], wC[:, k, half * 512:(half + 1) * 512]) for k in range(8)], ["vT", "wC"], ["pb%d" % (4 + half)])
                post_norm_residual(pbank[4], pbank[5], "pb4", "pb5", xt[q], "xt" + tag, gpost, res[q], "res" + tag, ss2[q], "ssb" + tag, junk, dst)
            A.release(m3)
            A.release(m0)
            phase_barrier()

        def post_norm_residual(pa, pb, ka, kb, xtile, kx, gpost, res, kres, ss, kss, junk, dst):
            act(junk[:, 0:512], pa[:, :], AF.Square, [ka], ["junk", kss + "a"], accum_out=ss[:, 0:1])
            act(junk[:, 512:1024], pb[:, :], AF.Square, [kb], ["junk", kss + "b"], accum_out=ss[:, 1:2])
            tt("dve", ss[:, 2:3], ss[:, 0:1], ss[:, 1:2], ALU.add, [kss + "a", kss + "b"], [kss + "c"])
            ts("dve", ss[:, 2:3], ss[:, 2:3], 1.0 / 1024.0, 1e-6, ALU.mult, ALU.add, [kss + "c"], [kss + "c"])
            act(ss[:, 3:4], ss[:, 2:3], AF.Sqrt, [kss + "c"], [kss + "d"])
            S.op("dve", lambda e: e.reciprocal(out=ss[:, 3:4], in_=ss[:, 3:4]), reads=[kss + "d"], writes=[kss + "d"])
            stt("dve", res[:, 0:512], pa[:, :], ss[:, 3:4], gpost[:, 0:512], ALU.mult, ALU.mult, [ka, kss + "d", "gpost"], [kres])
            stt("dve", res[:, 512:1024], pb[:, :], ss[:, 3:4], gpost[:, 512:1024], ALU.mult, ALU.mult, [kb, kss + "d", "gpost"], [kres])
            tt("pool", res, res, xtile, ALU.add, [kres, kx], [kres])
            dma(dst, res, reads=[kres], writes=["dram_res"])

        ATT = {}

        def load_weight2(dst_bf, w_dram, ncols, gcol, mul, stage, key, gkey):
            for k in range(8):
                q = k % len(stage)
                dma(stage[q][:, 0:ncols], w_dram[k * 128:(k + 1) * 128, :], writes=["wst%d" % q])
                ts("pool", dst_bf[:, k, :], stage[q][:, 0:ncols], gcol[:, k:k + 1], mul, ALU.mult, ALU.mult, ["wst%d" % q, gkey], [key])

        def kv_phase(src_p, src_s):
            KT = A.bf([8, SEQ])
            Vb = A.bf([NT, 1024])
            KTs = A.bf([8, 128])
            sbc = A.f32([24])
            ccore = A.f32([4])
            cattn = A.bf([6, 128])
            identf = A.f32([128])
            onesrow = A.f32([128])
            dma(sbc, sbcols[:, :], writes=["sbc"])
            dma(ccore, c_core[:, :], writes=["ccore"])
            dma(flat(cattn), c_attn.rearrange("p a b -> p (a b)"), writes=["cattn"])
            dma(identf, c_identf[:, :], writes=["identf"])
            dma(onesrow[0:1, :], c_onesrow[:, :], writes=["onesrow"])
            CONST_KEYS.extend(["sbc", "ccore", "cattn", "identf", "onesrow"])
            ATT.update(KT=KT, Vb=Vb, KTs=KTs, sbc=sbc, ccore=ccore, cattn=cattn, identf=identf, onesrow=onesrow)
            m = A.mark()
            wKV = A.bf([8, 2048])
            stage = [A.f32([2048])]
            load_weight2(wKV, w_kv, 2048, sbc[:, 0:8], 1.0, stage, "wKV", "sbc")
            xt = A.f32([1024]); hb = A.bf([1024]); hT = A.bf([8, 128]); ssc = A.f32([4]); junk = A.bf([1024])
            kvo = [A.f32([2048]) for _ in range(2)]
            for i in range(NT + 1):
                q = i % 2
                src = src_p[i * 128:(i + 1) * 128, :] if i < NT else src_s
                norm_T(src, xt, hb, hT, "k", ssc, junk)
                for half in range(2):
                    pb = pbank[half]
                    groups = []
                    for jj in range(4):
                        a = half * 4 + jj
                        groups.append((pb[:, jj * 128:(jj + 1) * 128], [(wKV[:, k, a * 128:(a + 1) * 128], hT[:, k, :]) for k in range(8)]))
                    mm_multi(groups, ["wKV", "hTk"], ["pb%d" % half])
                    pv = pb[:, :].rearrange("p (j m) -> p j m", j=4)
                    if i < NT:
                        outv = KT[:, half * 4:half * 4 + 4, i * 128:(i + 1) * 128]
                    else:
                        outv = KTs[:, half * 4:half * 4 + 4, :]
                    S.op("act", lambda e, outv=outv, pv=pv: e.copy(out=outv, in_=pv), reads=["pb%d" % half], writes=["KT"])
                for n in range(4):
                    pb = pbank[2 + n]
                    mm(pb[:, :], [(hT[:, k, :], wKV[:, k, n * 512:(n + 1) * 512]) for k in range(8)], ["wKV", "hTk"], ["pb%d" % (2 + n)])
                    S.op("act", lambda e, o=kvo[q][:, n * 512:(n + 1) * 512], pb=pb: e.copy(out=o, in_=pb[:, :]), reads=["pb%d" % (2 + n)], writes=["kvo%d" % q])
                if i < NT:
                    dma(k_p[i * 128:(i + 1) * 128, :], kvo[q][:, 0:1024], reads=["kvo%d" % q])
                    dma(v_p[i * 128:(i + 1) * 128, :], kvo[q][:, 1024:2048], reads=["kvo%d" % q])
                    cp("pool", Vb[:, i, :], kvo[q][:, 1024:2048], ["kvo%d" % q], ["Vb"])
                else:
                    dma(k_s[:, :], kvo[q][:, 0:1024], reads=["kvo%d" % q])
                    dma(v_s[:, :], kvo[q][:, 1024:2048], reads=["kvo%d" % q], writes=["v_s_dram"])
            A.release(m)
            phase_barrier()

        def sb_layer(j, first, src_nat, src_q, src_s, dst_q, dst_s):
            KT, Vb, KTs, sbc, ccore, cattn = ATT["KT"], ATT["Vb"], ATT["KTs"], ATT["sbc"], ATT["ccore"], ATT["cattn"]
            identf, onesrow = ATT["identf"], ATT["onesrow"]
            triL, onesb, M0, M1, mnew, zerosb = cattn[:, 0, :], cattn[:, 1, :], cattn[:, 2, :], cattn[:, 3, :], cattn[:, 4, :], cattn[:, 5, :]
            QG = cfg.QG
            NQ = QG * 128
            mL = A.mark()
            gpost = A.f32([1024])
            biasT = A.f32([16])
            biasrow = A.f32([128])
            dma(gpost, sbgpost[j], writes=["gpost"])
            dma(biasT, sbbias[j], writes=["biasT"])
            dma(biasrow, sbbiasrow[j], writes=["biasrow"])
            ebias = A.f32([128])
            act(ebias, biasrow, AF.Exp, ["biasrow"], ["ebias"])
            gcol = sbc[:, 8 + 8 * j:16 + 8 * j]
            xt = A.f32([1024]); hb = A.bf([1024]); hT = A.bf([8, 128]); ssc = A.f32([4]); junk = A.bf([1024])
            res = A.f32([1024]); ss2 = A.f32([4])
            off_qT = A.off; qT = A.bf([8, NQ]); gT = A.bf([8, NQ]); off_hT4 = A.off; hT4 = A.bf([8, NQ])

            def load_x(slot):
                if slot is None:
                    dma(xt, src_s, writes=["xtk"])
                elif first:
                    dma(xt, src_nat[(2 * slot) * 128:(2 * slot + 1) * 128, :], writes=["xtk"])
                    dma(res, src_nat[(2 * slot + 1) * 128:(2 * slot + 2) * 128, :], writes=["res"])
                    ts("dve", xt, xt, ccore[:, 0:1], None, ALU.mult, None, ["xtk", "ccore"], ["xtk"])
                    stt("dve", xt, res, ccore[:, 1:2], xt, ALU.mult, ALU.add, ["res", "xtk", "ccore"], ["xtk"])
                else:
                    dma(xt, src_q[slot * 128:(slot + 1) * 128, :], writes=["xtk"])

            def project(slots, ncols):
                mR = A.mark()
                wQh = A.bf([8, 1024])
                stage = [A.f32([1024]) for _ in range(2)]
                load_weight2(wQh, b_w_in[j][:, 0:1024], 1024, gcol, 0.125, stage, "wQh", "sbc")
                for t, slot in enumerate(slots):
                    load_x(slot)
                    norm_T(None, xt, hb, hT, "k", ssc, junk)
                    cp("dve", hT4[:, :, t * 128:(t + 1) * 128], hT, ["hTk"], ["hT4"])
                for a in range(8):
                    pb = pbank[a % 2]
                    mm(pb[:, 0:ncols], [(wQh[:, k, a * 128:(a + 1) * 128], hT4[:, k, 0:ncols]) for k in range(8)], ["wQh", "hT4"], ["pb%d" % (a % 2)])
                    S.op("act", lambda e, o=qT[:, a, 0:ncols], pb=pb: e.copy(out=o, in_=pb[:, 0:ncols]), reads=["pb%d" % (a % 2)], writes=["qT"])
                load_weight2(wQh, b_w_in[j][:, 1024:2048], 1024, gcol, 1.0, stage, "wQh", "sbc")
                for a in range(8):
                    pb = pbank[a % 2]
                    mm(pb[:, 0:ncols], [(wQh[:, k, a * 128:(a + 1) * 128], hT4[:, k, 0:ncols]) for k in range(8)], ["wQh", "hT4"], ["pb%d" % (a % 2)])
                    act(gT[:, a, 0:ncols], pb[:, 0:ncols], AF.Silu, ["pb%d" % (a % 2)], ["gT"])
                A.release(mR)
                phase_barrier()

            def out_proj(slots):
                mR = A.mark()
                wO = A.bf([8, 1024])
                stage = [A.f32([1024]) for _ in range(2)]
                load_weight(wO, b_w_out[j], 1024, None, stage, "wO")
                for t, slot in enumerate(slots):
                    load_x(slot)
                    for half in range(2):
                        pb = pbank[4 + half]
                        mm(pb[:, :], [(gT[:, k, t * 128:(t + 1) * 128], wO[:, k, half * 512:(half + 1) * 512]) for k in range(8)], ["gT", "wO"], ["pb%d" % (4 + half)])
                    dst = dst_s if slot is None else dst_q[slot * 128:(slot + 1) * 128, :]
                    post_norm_residual(pbank[4], pbank[5], "pb4", "pb5", xt, "xtk", gpost, res, "res", ss2, "ssb", junk, dst)
                A.release(mR)
                phase_barrier()

            for G in range(0 if DBG.get("noprompt") else cfg.NSLOT // QG):
                slots = [G * QG + t for t in range(QG)]
                project(slots, NQ)
                mR = A.mark()
                ez = [A.f32([NQ]) for _ in range(2)]
                e1 = [A.f32([NQ]) for _ in range(2)]
                sp = [A.bf([NQ]) for _ in range(2)]
                Ab = [A.bf([NQ]) for _ in range(2)]
                RSb = [A.bf([NQ]) for _ in range(2)]
                RS32 = A.f32([NQ])
                KBmax = 2 * (G * QG + QG - 1) + 1
                for h in range(16):
                    a, hb_ = h // 2, (h % 2) * 64
                    po = pbank[4 + a % 2]
                    S.op("pool", lambda e: e.memset(RS32, 0.0), writes=["RS32"])
                    S.op("pool", lambda e: e.memset(RSb[0], 0.0), writes=["RSb0"])
                    S.op("pool", lambda e: e.memset(RSb[1], 0.0), writes=["RSb1"])
                    units = list(range(KBmax, -1, -1))
                    n = len(units)
                    S.op("pe", lambda e: e.matmul(po[hb_:hb_ + 64, 0:NQ], lhsT=zerosb[:, 0:64], rhs=KT[:, a, 0:NQ], start=True, stop=False),
                         reads=["cattn", "KT"], writes=["po%d_%d" % (a % 2, h % 2)])

                    def geo(kb):
                        c0 = max(0, kb // 2 - G * QG) * 128
                        t = kb // 2 - G * QG
                        return c0, (t if t >= 0 else None), kb % 2

                    def s1(u):
                        kb = units[u]; c0, t, x = geo(kb); p = u % 2
                        mm(pbank[p][:, c0:NQ], [(KT[hb_:hb_ + 64, a, kb * 128:(kb + 1) * 128], qT[hb_:hb_ + 64, a, c0:NQ])], ["KT", "qT"], ["pb%d" % p])

                    def s2(u):
                        kb = units[u]; c0, t, x = geo(kb); p = u % 2
                        act(ez[p][:, c0:NQ], pbank[p][:, c0:NQ], AF.Exp, ["pb%d" % p, "biasT"], ["ez%d" % p], bias=biasT[:, h:h + 1])
                        act(sp[p][:, c0:NQ], ez[p][:, c0:NQ], AF.Ln, ["ez%d" % p], ["sp%d" % p], bias=1.0)
                        if t is not None:
                            M = M0 if x == 0 else M1
                            tt("dve", sp[p][:, t * 128:(t + 1) * 128], sp[p][:, t * 128:(t + 1) * 128], M, ALU.mult, ["sp%d" % p, "cattn"], ["sp%d" % p])
                        if kb > 0:
                            tt("pool", RS32[:, c0:NQ], RS32[:, c0:NQ], sp[p][:, c0:NQ], ALU.add, ["RS32", "sp%d" % p], ["RS32"])
                            cp("pool", RSb[1 - p][:, c0:NQ], RS32[:, c0:NQ], ["RS32"], ["RSb%d" % (1 - p)])

                    def s3(u):
                        kb = units[u]; c0, t, x = geo(kb); p = u % 2
                        mm(pbank[2 + p][:, c0:NQ], [(triL, sp[p][:, c0:NQ]), (onesb, RSb[p][:, c0:NQ])], ["cattn", "sp%d" % p, "RSb%d" % p], ["pb%d" % (2 + p)])

                    def s4(u):
                        kb = units[u]; c0, t, x = geo(kb); p = u % 2
                        act(e1[p][:, c0:NQ], pbank[2 + p][:, c0:NQ], AF.Exp, ["pb%d" % (2 + p)], ["e1%d" % p], scale=-1.0)
                        tt("dve", Ab[p][:, c0:NQ], ez[p][:, c0:NQ], e1[p][:, c0:NQ], ALU.mult, ["ez%d" % p, "e1%d" % p], ["Ab%d" % p])
                        if t is not None:
                            M = M0 if x == 0 else M1
                            tt("dve", Ab[p][:, t * 128:(t + 1) * 128], Ab[p][:, t * 128:(t + 1) * 128], M, ALU.mult, ["Ab%d" % p, "cattn"], ["Ab%d" % p])

                    def s5(u):
                        kb = units[u]; c0, t, x = geo(kb); p = u % 2

                        def fn(e, kb=kb, c0=c0, p=p):
                            ins = None
                            for tl_ in range(c0 // 128, QG):
                                sig = G * QG + tl_
                                ins = e.matmul(po[hb_:hb_ + 64, tl_ * 128:(tl_ + 1) * 128], lhsT=Vb[:, kb, h * 64:(h + 1) * 64],
                                               rhs=Ab[p][:, tl_ * 128:(tl_ + 1) * 128], start=False, stop=(kb == 0))
                            return ins
                        S.op("pe", fn, reads=["Vb", "Ab%d" % p], writes=["po%d_%d" % (a % 2, h % 2)])

                    for tstep in range(n + 2):
                        if tstep < n:
                            s1(tstep); s2(tstep)
                        if 0 <= tstep - 1 < n:
                            s3(tstep - 1); s4(tstep - 1)
                        if 0 <= tstep - 2 < n:
                            s5(tstep - 2)
                    if h % 2 == 1:
                        tt("dve", gT[:, a, :], po[:, 0:NQ], gT[:, a, :], ALU.mult, ["po%d_0" % (a % 2), "po%d_1" % (a % 2), "gT"], ["gT"])
                A.release(mR)
                phase_barrier()
                out_proj(slots)

            if cfg.sample:
                project([None], 128)
                mR = A.mark()
                NPG = cfg.NPG
                idx = idx_t; idf = A.f32([4 * NPG]); ptt = A.i32([4 * NPG])
                if cfg.sample_attn:
                    dma(ptt, ptab[:, :], writes=["ptt"])
                    ts("dve", idf, ptt, 128.0, ccore[:, 2:3], ALU.mult, ALU.add, ["ptt", "ccore"], ["idf"])
                    cp("dve", idx[:, :], idf, ["idf"], ["idx"])
                A2 = Arena(arena_t[:, off_hT4:off_hT4 + 4 * NQ], 4 * NQ)
                Kpg = [A2.f32([1024])] if 4 * NQ >= 2048 else [A.f32([1024])]
                Vpg = [A2.f32([1024])] if 4 * NQ >= 2048 else [A.f32([1024])]
                KTp = [A.bf([8, 128])]
                Vbp = [A.bf([1024])]
                Kb = [A.bf([1024])]
                ez = [A.f32([128]) for _ in range(2)]
                e1 = [A.f32([128]) for _ in range(2)]
                sp = [A.bf([128]) for _ in range(2)]
                Ab = [A.bf([128]) for _ in range(2)]
                RSb = [A.bf([128]) for _ in range(2)]
                RS32 = A.f32([128])
                fz = A.f32([8])
                qTm = A.bf([8, 2, 128])
                S.op("pool", lambda e: e.memset(flat(qTm), 0.0), writes=["qTm"])
                cp("dve", qTm[0:64, :, 0, :], qT[0:64, :, 0:128], ["qT", "qTm"], ["qTm"])
                cp("dve", qTm[64:128, :, 1, :], qT[64:128, :, 0:128], ["qT", "qTm"], ["qTm"])
                pos = pbank[4]
                phase_barrier()
                if 4 * NQ >= 2048:
                    A3 = Arena(arena_t[:, off_qT:off_qT + 4 * NQ], 4 * NQ)
                    Kpg.append(A3.f32([1024])); Vpg.append(A3.f32([1024]))
                else:
                    Kpg.append(A.f32([1024])); Vpg.append(A.f32([1024]))
                for i in range(4 if cfg.sample_attn else 0):
                    S.op("pool", lambda e: e.memset(RS32, 0.0), writes=["RS32"])
                    S.op("pool", lambda e: e.memset(RSb[0], 0.0), writes=["RSb0"])
                    S.op("pool", lambda e: e.memset(RSb[1], 0.0), writes=["RSb1"])
                    units = [None] + list(range(NPG - 1, -1, -1))
                    n = len(units)

                    def G(u):
                        pg = units[u]; pk = u % 2
                        col = i * NPG + pg
                        if DBG.get("nogather"):
                            dma(Kpg[pk][:, :], cache_k[pg * 128:(pg + 1) * 128, :], writes=["Kpg%d" % pk])
                            dma(Vpg[pk][:, :], cache_v[pg * 128:(pg + 1) * 128, :], writes=["Vpg%d" % pk])
                            return
                        S.op("pool", lambda e, pk=pk, col=col: e.indirect_dma_start(out=Kpg[pk][:, :], out_offset=None, in_=cache_k[:, :],
                             in_offset=bass.IndirectOffsetOnAxis(ap=idx[:, col:col + 1], axis=0)), reads=["idx"], writes=["Kpg%d" % pk], dma=True)
                        S.op("pool", lambda e, pk=pk, col=col: e.indirect_dma_start(out=Vpg[pk][:, :], out_offset=None, in_=cache_v[:, :],
                             in_offset=bass.IndirectOffsetOnAxis(ap=idx[:, col:col + 1], axis=0)), reads=["idx"], writes=["Vpg%d" % pk], dma=True)

                    def s0(u):
                        pg = units[u]; pk = u % 2
                        if pg is None:
                            dma(Vpg[pk][0:8, :], v_s[32 * i:32 * i + 8, :], reads=["v_s_dram"], writes=["Vpg%d" % pk])
                            cp("pool", Vbp[0][0:8, :], Vpg[pk][0:8, :], ["Vpg%d" % pk], ["Vbp0"])
                            return
                        cp("dve", Kb[0], Kpg[pk], ["Kpg%d" % pk], ["Kb0"])

                        def tp(e):
                            ins = None
                            for k in range(8):
                                ins = e.transpose(ptr[:, k, :], Kb[0][:, k * 128:(k + 1) * 128], ident)
                            return ins
                        S.op("pe", tp, reads=["Kb0", "ident", "fz"], writes=["ptr"])
                        S.op("act", lambda e, o=KTp[0]: e.copy(out=o, in_=ptr[:, :, :]), reads=["ptr"], writes=["KTp0"])

                    def cpV(u):
                        pk = u % 2
                        cp("pool", Vbp[0], Vpg[pk], ["Vpg%d" % pk], ["Vbp0"])

                    def s1(u):
                        pg = units[u]; p = u % 2
                        nk = 8 if pg is None else 128

                        if (DBG.get("noA") and pg is None) or (DBG.get("noB") and pg is not None):
                            return

                        def fn(e, p=p, pg=pg, nk=nk):
                            ins = None
                            for h in range(1 if DBG.get("noD") else 16):
                                a_, hb_ = h // 2, (h % 2) * 64
                                if DBG.get("noC"):
                                    hb_ = 0
                                if pg is None:
                                    kt = KTs[:, a_, 32 * i:32 * i + 8]
                                else:
                                    kt = KTp[0][:, a_, :]
                                ins = e.matmul(pbank[p][0:nk, h * 8:(h + 1) * 8], lhsT=kt, rhs=qTm[:, a_, h % 2, 32 * i:32 * i + 8], start=True, stop=True)
                            return ins
                        S.op("pe", fn, reads=["KTp0", "KTp0", "KT", "qTm"], writes=["pb%d" % p])

                    def s2(u):
                        pg = units[u]; p = u % 2
                        nk = 8 if pg is None else 128
                        act(ez[p][0:nk, :], pbank[p][0:nk, 0:128], AF.Exp, ["pb%d" % p], ["ez%d" % p])
                        tt("dve", ez[p][0:nk, :], ez[p][0:nk, :], ebias[0:nk, :], ALU.mult, ["ez%d" % p, "ebias"], ["ez%d" % p])
                        act(sp[p][0:nk, :], ez[p][0:nk, :], AF.Ln, ["ez%d" % p], ["sp%d" % p], bias=1.0)
                        if pg is None:
                            tt("dve", sp[p][0:8, :], sp[p][0:8, :], mnew[0:8, :], ALU.mult, ["sp%d" % p, "cattn"], ["sp%d" % p])
                        if u < n - 1:
                            tt("pool", RS32[0:nk, :], RS32[0:nk, :], sp[p][0:nk, :], ALU.add, ["RS32", "sp%d" % p], ["RS32"])
                            cp("pool", RSb[1 - p], RS32, ["RS32"], ["RSb%d" % (1 - p)])

                    def s3(u):
                        pg = units[u]; p = u % 2
                        nk = 8 if pg is None else 128
                        if pg is None:
                            mm(pbank[2 + p][0:8, 0:128], [(triL[0:8, 0:8], sp[p][0:8, :])], ["cattn", "sp%d" % p], ["pb%d" % (2 + p)])
                        else:
                            mm(pbank[2 + p][:, 0:128], [(triL, sp[p]), (onesb, RSb[p])], ["cattn", "sp%d" % p, "RSb%d" % p], ["pb%d" % (2 + p)])

                    def s4(u):
                        pg = units[u]; p = u % 2
                        nk = 8 if pg is None else 128
                        act(e1[p][0:nk, :], pbank[2 + p][0:nk, 0:128], AF.Exp, ["pb%d" % (2 + p)], ["e1%d" % p], scale=-1.0)
                        tt("dve", Ab[p][0:nk, :], ez[p][0:nk, :], e1[p][0:nk, :], ALU.mult, ["ez%d" % p, "e1%d" % p], ["Ab%d" % p])
                        if pg is None:
                            tt("dve", Ab[p][0:8, :], Ab[p][0:8, :], mnew[0:8, :], ALU.mult, ["Ab%d" % p, "cattn"], ["Ab%d" % p])

                    def s5(u):
                        pg = units[u]; p = u % 2
                        nk = 8 if pg is None else 128

                        def fn(e, p=p, nk=nk, u=u):
                            ins = None
                            for h in range(16):
                                a_, hb_ = h // 2, (h % 2) * 64
                                c_ = (i * 8 + a_) * 8
                                ins = e.matmul(pos[hb_:hb_ + 64, c_:c_ + 8], lhsT=Vbp[0][0:nk, h * 64:(h + 1) * 64], rhs=Ab[p][0:nk, h * 8:(h + 1) * 8],
                                               start=(u == 0 and h < 2), stop=(u == n - 1))
                            return ins
                        S.op("pe", fn, reads=["Vbp0", "Ab%d" % p], writes=["pos"])

                    s0(0); s1(0); s2(0); s3(0); s4(0); s5(0)
                    act(fz, pbank[2][:, 0:8], AF.Copy, ["pos", "pb2"], ["fz"])
                    if n > 1:
                        G(1)
                    for t in range(1, n + 1):
                        if t < n:
                            s0(t)
                        if t - 1 >= 1:
                            s3(t - 1); s4(t - 1); cpV(t - 1); s5(t - 1)
                        if t + 1 < n:
                            G(t + 1)
                        if t < n:
                            s1(t); s2(t)
                ov = pos[:, 0:256].rearrange("p (i a q) -> p i a q", i=4, a=8)
                gv = gT[:, :, 0:128].rearrange("p a (i r) -> p i a r", r=32)[:, :, :, 0:8]
                tt("dve", gv, ov, gv, ALU.mult, ["pos", "gT"], ["gT"])
                A.release(mR)
                phase_barrier()
                out_proj([None])
            A.release(mL)
            phase_barrier()

        XA_P, XA_S = xa[0:SEQ, :], xa[SEQ:SEQ + 128, :]
        XB_P, XB_S = xb[0:SEQ, :], xb[SEQ:SEQ + 128, :]
        XQ_P, XQ_S = xq[0:cfg.NSLOT * 128, :], xq[cfg.NSLOT * 128:cfg.NSLOT * 128 + 128, :]
        if cfg.stages == ("s5",):
            if cfg.n_s5 == 1:
                s5_layer(0, xp, xs, y_p, y_s)
            else:
                s5_layer(0, xp, xs, XA_P, XA_S)
                s5_layer(1, XA_P, XA_S, y_p, y_s)
        elif cfg.stages == ("kv", "sb"):
            kv_phase(xp, xs)
            if cfg.n_sb == 1:
                sb_layer(0, True, xp, None, xs, yq_p, y_s)
            else:
                sb_layer(0, True, xp, None, xs, XQ_P, XQ_S)
                sb_layer(1, False, None, XQ_P, XQ_S, yq_p, y_s)
        else:
            s5_layer(0, xp, xs, XA_P, XA_S)
            s5_layer(1, XA_P, XA_S, XB_P, XB_S)
            kv_phase(XB_P, XB_S)
            sb_layer(0, True, XB_P, None, XB_S, XQ_P, XQ_S)
            sb_layer(1, False, None, XQ_P, XQ_S, yq_p, y_s)
        S.finish()
        S.emit()
        print("ops", S.nops, "arena peak words", A.peak)
    return nc


def _bc(a, n=128):
    return np.broadcast_to(a[None], (n,) + a.shape)


def make_consts():
    c = {}
    c["c_ident"] = np.eye(128, dtype=np.float32).astype(ml_dtypes.bfloat16)
    rows = np.arange(128)
    colsi = np.arange(512)
    c["c_mask4"] = (((colsi[None, :] % 128) // 16) >= (rows[:, None] // 16)).astype(np.float32)
    P = np.zeros((128, 128), np.float32)
    P[rows, (rows + 64) % 128] = 1.0
    c["c_pswap"] = P
    cc = np.zeros((128, 4), np.float32)
    cc[:, 0] = 7 - rows // 16
    cc[:, 1] = np.where(rows < 64, 1.0, -1.0)
    cc[:, 2] = -cc[:, 1]
    c["c_cols"] = cc
    c["c_iota"] = np.ascontiguousarray(_bc(np.arange(512, dtype=np.float32)))
    ev = np.array([0, -1, -2, -3, -4, -5, -6, -7, 0, 1, 2, 3, 4, 5, 6, 7, 8], np.float32)
    c["c_evals"] = np.ascontiguousarray(_bc(np.repeat(ev, 64)))
    return c


def prep_core(inp, cfg, c):
    f = np.float32
    b, r = c // 2, c % 2
    SEQ = cfg.SEQ
    d = {}
    d["xp"] = np.ascontiguousarray(inp["x_prompt"][b, :SEQ])
    xs = np.zeros((128, 1024), f)
    for i in range(4):
        xs[32 * i:32 * i + 8] = inp["x_sample"][4 * c + i]
    d["xs"] = xs
    pA, pB, pAT, pBT, pCT, cols, gpost, h0 = [], [], [], [], [], [], [], []
    for l in range(2):
        Ar, Ai, ld = inp["a_A_re"][l], inp["a_A_im"][l], inp["a_log_dt"][l]
        pA.append(np.stack([_bc(Ar.reshape(-1)), _bc(Ai.reshape(-1)), _bc(np.repeat(ld, 64))], axis=1))
        Br, Bi = inp["a_B_re"][l], inp["a_B_im"][l]
        brw = np.tile(Br.transpose(2, 0, 1), (8, 1, 1)).reshape(128, 4096)
        biw = np.tile(Bi.transpose(2, 0, 1), (8, 1, 1)).reshape(128, 4096)
        pB.append(np.stack([brw, biw], axis=1))
        ArT = np.concatenate([Ar.T, Ar.T], 0)
        AiT = np.concatenate([Ai.T, Ai.T], 0)
        pAT.append(np.stack([ArT, AiT, _bc(ld)], axis=1))
        BrT, BiT = Br.transpose(1, 0, 2), Bi.transpose(1, 0, 2)
        B2 = np.concatenate([BrT, BiT], 0).reshape(128, 1024)
        B2s = np.concatenate([BiT, BrT], 0).reshape(128, 1024)
        pBT.append(np.stack([B2, B2s], axis=1))
        Cr, Ci = inp["a_C_re"][l].transpose(2, 0, 1), inp["a_C_im"][l].transpose(2, 0, 1)
        C2 = np.concatenate([Cr, Ci], 0).reshape(128, 1024)
        C2s = np.concatenate([Ci, Cr], 0).reshape(128, 1024)
        pCT.append(np.stack([C2, C2s], axis=1))
        cols.append(np.concatenate([inp["a_norm_pre"][l].reshape(8, 128).T, inp["a_D"][l].reshape(8, 128).T,
                                    inp["a_b_glu"][l].reshape(8, 128).T], axis=1))
        gpost.append(_bc(inp["a_norm_post"][l]))
        hr = inp["state_ssm_re"][l, 4 * c:4 * c + 4].transpose(2, 1, 0)
        hi = inp["state_ssm_im"][l, 4 * c:4 * c + 4].transpose(2, 1, 0)
        h0.append(np.stack([np.concatenate([hr, hi], 0).reshape(128, 256),
                            np.concatenate([hi, hr], 0).reshape(128, 256)], axis=1))
    d["s5pA"] = np.ascontiguousarray(np.stack(pA), f)
    d["s5pB"] = np.ascontiguousarray(np.stack(pB), f)
    d["s5pAT"] = np.ascontiguousarray(np.stack(pAT), f)
    d["s5pBT"] = np.ascontiguousarray(np.stack(pBT), f)
    d["s5pCT"] = np.ascontiguousarray(np.stack(pCT), f)
    d["s5cols"] = np.ascontiguousarray(np.stack(cols), f)
    d["s5gpost"] = np.ascontiguousarray(np.stack(gpost), f)
    d["s5h0"] = np.ascontiguousarray(np.stack(h0), f)
    for k in ("a_w_in", "a_w_glu", "a_w_out"):
        d[k] = np.ascontiguousarray(inp[k], f)
    return d


def prep_core_attn(inp, cfg, c, d):
    f = np.float32
    r = c % 2
    d["w_kv"] = np.ascontiguousarray(inp["w_kv"], f)
    d["b_w_in"] = np.ascontiguousarray(inp["b_w_in"], f)
    d["b_w_out"] = np.ascontiguousarray(inp["b_w_out"], f)
    d["sbcols"] = np.ascontiguousarray(np.concatenate([inp["kv_norm"].reshape(8, 128).T, inp["b_norm_pre"][0].reshape(8, 128).T,
                                                       inp["b_norm_pre"][1].reshape(8, 128).T], axis=1), f)
    d["sbgpost"] = np.ascontiguousarray(np.stack([_bc(inp["b_norm_post"][j]) for j in range(2)]), f)
    d["sbbias"] = np.ascontiguousarray(np.stack([_bc(inp["b_logit_bias"][j]) for j in range(2)]), f)
    d["sbbiasrow"] = np.ascontiguousarray(np.stack([_bc(np.repeat(inp["b_logit_bias"][j], 8)) for j in range(2)]), f)
    cc = np.zeros((128, 4), f)
    cc[:, 0] = 1.0 if r == 0 else 0.0
    cc[:, 1] = 0.0 if r == 0 else 1.0
    cc[:, 2] = np.arange(128)
    d["c_core"] = cc
    k = np.arange(128)
    triL = (k[:, None] >= k[None, :]).astype(f)
    ones = np.ones((128, 128), f)
    triT = (k[:, None] < k[None, :]).astype(f)
    zeros = np.zeros((128, 128), f)
    M0, M1 = (triT, zeros) if r == 0 else (ones, triT)
    mnew = np.zeros((128, 128), f)
    svals = np.tile(np.arange(8), 16)
    for j in range(8):
        mnew[j] = (j < svals)
    d["c_attn"] = np.ascontiguousarray(np.stack([triL, ones, M0, M1, mnew, zeros], axis=1)).astype(ml_dtypes.bfloat16)
    d["c_identf"] = np.eye(128, dtype=f)
    d["c_onesrow"] = np.ones((1, 128), f)
    if cfg.sample_attn:
        d["cache_k"] = inp["cache_k"].reshape(-1, 1024)
        d["cache_v"] = inp["cache_v"].reshape(-1, 1024)
        pt = inp["page_table"][4 * c:4 * c + 4, :cfg.NPG].reshape(-1).astype(np.int32)
        d["ptab"] = np.ascontiguousarray(_bc(pt))
    return d


_NC_CACHE = {}


def kernel(**inputs):
    cfg = Cfg()
    inp = {k: np.asarray(v) for k, v in inputs.items()}
    if "nc" not in _NC_CACHE:
        _NC_CACHE["nc"] = build(cfg)
    nc = _NC_CACHE["nc"]
    consts = make_consts()
    in_maps = []
    for c in range(8):
        d = prep_core(inp, cfg, c)
        prep_core_attn(inp, cfg, c, d)
        d.update(consts)
        in_maps.append(d)
    res = run_bass_kernel_spmd(nc, in_maps, core_ids=list(range(8)))
    R = res.results
    f = np.float32
    SEQ, NT = cfg.SEQ, cfg.NT
    y_prompt = np.empty((4, NT, 128, 1024), f)
    y_sample = np.empty((32, 8, 1024), f)
    re_p = np.empty((2, 4, 64, 64), f)
    im_p = np.empty((2, 4, 64, 64), f)
    k_prompt = np.empty((4, SEQ, 1024), f)
    v_prompt = np.empty((4, SEQ, 1024), f)
    re_s = np.empty((2, 32, 64, 64), f)
    im_s = np.empty((2, 32, 64, 64), f)
    k_sample = np.empty((32, 8, 1024), f)
    v_sample = np.empty((32, 8, 1024), f)
    for c in range(8):
        b, r = c // 2, c % 2
        o = R[c]
        y_prompt[b, r::2] = np.asarray(o["yq_p"]).reshape(cfg.NSLOT, 128, 1024)
        ys, ks, vs = np.asarray(o["y_s"]), np.asarray(o["k_s"]), np.asarray(o["v_s"])
        for i in range(4):
            y_sample[4 * c + i] = ys[32 * i:32 * i + 8]
            k_sample[4 * c + i] = ks[32 * i:32 * i + 8]
            v_sample[4 * c + i] = vs[32 * i:32 * i + 8]
        fs = np.asarray(o["fin_s"]).reshape(2, 128, 64, 4)
        for l in range(2):
            re_s[l, 4 * c:4 * c + 4] = fs[l, :64].transpose(2, 1, 0)
            im_s[l, 4 * c:4 * c + 4] = fs[l, 64:].transpose(2, 1, 0)
        if r == 0:
            fp = np.asarray(o["fin_p"])
            for l in range(2):
                re_p[l, b] = fp[l, :64].T
                im_p[l, b] = fp[l, 64:].T
            k_prompt[b] = np.asarray(o["k_p"])
            v_prompt[b] = np.asarray(o["v_p"])
    return (y_prompt.reshape(4, SEQ, 1024), y_sample, re_p, im_p,
            k_prompt.reshape(4, SEQ, 16, 64), v_prompt.reshape(4, SEQ, 16, 64),
            re_s, im_s, k_sample.reshape(32, 8, 16, 64), v_sample.reshape(32, 8, 16, 64))
```

```python
import math
import contextlib
import numpy as np
import ml_dtypes
import concourse.bass as bass
import concourse.mybir as mybir
from concourse.bass_utils import run_bass_kernel_spmd

F32 = mybir.dt.float32
BF16 = mybir.dt.bfloat16
I32 = mybir.dt.int32
AF = mybir.ActivationFunctionType
ALU = mybir.AluOpType

NDMA_LANES = 6
DBG = {}
TWO_PI = 2.0 * math.pi
NE = 17


import types


def _freeze(fn):
    if fn is None or fn.__closure__ is None:
        return fn
    cells = []
    for c in fn.__closure__:
        try:
            cells.append(types.CellType(c.cell_contents))
        except ValueError:
            cells.append(c)
    return types.FunctionType(fn.__code__, fn.__globals__, fn.__name__, fn.__defaults__, tuple(cells))


class Sched:
    PHYS = ("pe", "act", "dve", "pool", "sp")

    def __init__(self, nc):
        self.nc = nc
        self.ops = {p: [] for p in self.PHYS}
        self.lanes = {}
        self.seen = {p: {} for p in self.PHYS}
        self.lastw = {}
        self.readers = {}
        self.dma_i = {p: 0 for p in self.PHYS}
        self.nops = 0

    def op(self, phys, fn, reads=(), writes=(), dma=False):
        fn = _freeze(fn)
        if dma:
            lane = "%s_q%d" % (phys, self.dma_i[phys] % NDMA_LANES)
            self.dma_i[phys] += 1
        else:
            lane = phys
        if lane not in self.lanes:
            self.lanes[lane] = 0
        waits = {}
        seen = self.seen[phys]

        def need(l2, v):
            if seen.get(l2, 0) < v and waits.get(l2, 0) < v:
                waits[l2] = v

        if dma and self.lanes[lane] > 0:
            need(lane, self.lanes[lane])
        for k in reads:
            if k in self.lastw:
                need(*self.lastw[k])
        for k in writes:
            if k in self.lastw:
                need(*self.lastw[k])
            for l2, v in self.readers.get(k, {}).items():
                need(l2, v)
        for l2, v in waits.items():
            seen[l2] = v
        inc = 16 if dma else 1
        self.lanes[lane] += inc
        val = self.lanes[lane]
        self.ops[phys].append((sorted(waits.items()), fn, lane, inc))
        for k in writes:
            self.lastw[k] = (lane, val)
            self.readers[k] = {}
        for k in reads:
            self.readers.setdefault(k, {})[lane] = val
        self.nops += 1
        return val

    def barrier(self):
        cur = sorted(self.lanes.items())
        for p in self.PHYS:
            w = [(l, v) for l, v in cur if self.seen[p].get(l, 0) < v]
            for l, v in w:
                self.seen[p][l] = v
            if w:
                self.ops[p].append((w, None, None, 0))
        self.lastw = {}
        self.readers = {}

    def finish(self):
        self.barrier()

    def emit(self):
        nc = self.nc
        with contextlib.ExitStack() as st:
            sems = {}
            for lane in self.lanes:
                sems[lane] = st.enter_context(nc.semaphore("s_" + lane))
            block = st.enter_context(nc.Block())

            def replay(phys, eng):
                for waits, fn, lane, inc in self.ops[phys]:
                    for l2, v in waits:
                        if v > 0:
                            eng.wait_ge(sems[l2], v)
                    if fn is not None:
                        ins = fn(eng)
                        ins.then_inc(sems[lane], inc)

            @block.tensor
            def _(e):
                replay("pe", e)

            @block.scalar
            def _(e):
                replay("act", e)

            @block.vector
            def _(e):
                replay("dve", e)

            @block.gpsimd
            def _(e):
                replay("pool", e)

            @block.sync
            def _(e):
                replay("sp", e)


class Arena:
    def __init__(self, t, nwords):
        self.t = t
        self.n = nwords
        self.off = 0
        self.peak = 0

    def mark(self):
        return self.off

    def release(self, m):
        self.off = m

    def f32(self, shape):
        n = int(np.prod(shape))
        assert self.off + n <= self.n, ("arena overflow", self.off, n, self.n)
        ap = self.t[:, self.off:self.off + n]
        self.off += n
        self.peak = max(self.peak, self.off)
        return self._shape(ap, shape)

    def i32(self, shape):
        n = int(np.prod(shape))
        assert self.off + n <= self.n, ("arena overflow", self.off, n, self.n)
        ap = self.t[:, self.off:self.off + n].bitcast(I32)
        self.off += n
        self.peak = max(self.peak, self.off)
        return self._shape(ap, shape)

    def bf(self, shape):
        n = int(np.prod(shape))
        w = (n + 1) // 2
        assert self.off + w <= self.n, ("arena overflow", self.off, w, self.n)
        ap = self.t[:, self.off:self.off + w].bitcast(BF16)
        if 2 * w != n:
            ap = ap[:, 0:n]
        self.off += w
        self.peak = max(self.peak, self.off)
        return self._shape(ap, shape)

    @staticmethod
    def _shape(ap, shape):
        if len(shape) == 1:
            return ap
        if len(shape) == 2:
            return ap.rearrange("p (a b) -> p a b", a=shape[0])
        if len(shape) == 3:
            return ap.rearrange("p (a b c) -> p a b c", a=shape[0], b=shape[1])
        if len(shape) == 4:
            return ap.rearrange("p (a b c d) -> p a b c d", a=shape[0], b=shape[1], c=shape[2])
        raise ValueError(shape)


def flat(ap):
    nd = len(ap.shape)
    if nd == 2:
        return ap
    if nd == 3:
        return ap.rearrange("p a b -> p (a b)")
    if nd == 4:
        return ap.rearrange("p a b c -> p (a b c)")
    if nd == 5:
        return ap.rearrange("p a b c d -> p (a b c d)")
    raise ValueError


class Cfg:
    def __init__(self, SEQ=4096, NPG=64, stages=("s5", "kv", "sb"), n_s5=2, n_sb=2, NPHYS=2560, QG=4, sample=True, sample_attn=True):
        self.NPHYS = NPHYS
        self.QG = QG
        self.sample = sample
        self.sample_attn = sample_attn
        self.SEQ = SEQ
        self.NT = SEQ // 128
        self.NCHP = SEQ // 8
        self.NB = self.NCHP // 128
        self.NPG = NPG
        self.NSLOT = self.NT // 2
        self.stages = stages
        self.n_s5 = n_s5
        self.n_sb = n_sb
        assert self.NCHP % 128 == 0 and self.NCHP <= 512


def build(cfg):
    nc = bass.Bass("TRN2", target_bir_lowering=False)
    SEQ, NT, NCHP, NB = cfg.SEQ, cfg.NT, cfg.NCHP, cfg.NB

    def din(name, shape, dt=F32):
        return nc.dram_tensor(name, list(shape), dt, kind="ExternalInput").ap()

    def dout(name, shape, dt=F32):
        return nc.dram_tensor(name, list(shape), dt, kind="ExternalOutput").ap()

    def dscr(name, shape, dt=F32):
        return nc.dram_tensor(name, list(shape), dt, kind="Internal").ap()

    xp = din("xp", [SEQ, 1024])
    xs = din("xs", [128, 1024])
    s5pA = din("s5pA", [2, 128, 3, 4096])
    s5pB = din("s5pB", [2, 128, 2, 4096])
    s5pAT = din("s5pAT", [2, 128, 3, 64])
    s5pBT = din("s5pBT", [2, 128, 2, 64 * 16])
    s5pCT = din("s5pCT", [2, 128, 2, 64 * 16])
    s5cols = din("s5cols", [2, 128, 24])
    s5gpost = din("s5gpost", [2, 128, 1024])
    s5h0 = din("s5h0", [2, 128, 2, 256])
    a_w_in = din("a_w_in", [2, 1024, 2048])
    a_w_glu = din("a_w_glu", [2, 1024, 1024])
    a_w_out = din("a_w_out", [2, 1024, 1024])
    c_ident = din("c_ident", [128, 128], BF16)
    c_mask4 = din("c_mask4", [128, 512])
    c_pswap = din("c_pswap", [128, 128])
    c_cols = din("c_cols", [128, 4])
    c_iota = din("c_iota", [128, 512])
    c_evals = din("c_evals", [128, NE * 64])

    w_kv = din("w_kv", [1024, 2048])
    b_w_in = din("b_w_in", [2, 1024, 2048])
    b_w_out = din("b_w_out", [2, 1024, 1024])
    sbcols = din("sbcols", [128, 24])
    sbgpost = din("sbgpost", [2, 128, 1024])
    sbbias = din("sbbias", [2, 128, 16])
    sbbiasrow = din("sbbiasrow", [2, 128, 128])
    c_core = din("c_core", [128, 4])
    c_attn = din("c_attn", [128, 6, 128], BF16)
    c_identf = din("c_identf", [128, 128])
    c_onesrow = din("c_onesrow", [1, 128])
    NPHYS = cfg.NPHYS
    if cfg.sample_attn:
        cache_k = din("cache_k", [NPHYS * 128, 1024])
        cache_v = din("cache_v", [NPHYS * 128, 1024])
        ptab = din("ptab", [128, 4 * cfg.NPG], I32)
    k_p = dout("k_p", [SEQ, 1024])
    v_p = dout("v_p", [SEQ, 1024])
    k_s = dout("k_s", [128, 1024])
    v_s = dout("v_s", [128, 1024])
    yq_p = dout("yq_p", [cfg.NSLOT * 128, 1024])
    xq = dscr("xq", [cfg.NSLOT * 128 + 128, 1024])
    y_p = dout("y_p", [SEQ, 1024]) if cfg.stages == ("s5",) else None
    y_s = dout("y_s", [128, 1024])
    fin_p = dout("fin_p", [2, 128, 64])
    fin_s = dout("fin_s", [2, 128, 256])

    xa = dscr("xa", [SEQ + 128, 1024])
    xb = dscr("xb", [SEQ + 128, 1024])
    scrU = dscr("scrU", [8, 16, 8, NCHP], BF16)
    scrY = dscr("scrY", [8, 16, 8, NCHP], BF16)
    scrUs = dscr("scrUs", [8, 16, 8, 4], BF16)
    scrYs = dscr("scrYs", [8, 16, 8, 4], BF16)

    with contextlib.ExitStack() as st:
        NW = 52600
        idx_t = st.enter_context(nc.sbuf_tensor("idx_t", [128, 4 * cfg.NPG], I32))
        arena_t = st.enter_context(nc.sbuf_tensor("arena", [128, NW], F32))
        A = Arena(arena_t, NW)
        pbank = [st.enter_context(nc.psum_tensor("pb%d" % i, [128, 512], F32)) for i in range(7)]
        ptr = st.enter_context(nc.psum_tensor("ptr", [128, 8, 128], BF16))
        S = Sched(nc)

        def dma(out, in_, reads=(), writes=()):
            S.op("sp", lambda e: e.dma_start(out=out, in_=in_), reads=reads, writes=writes, dma=True)

        def tt(eng, out, a, b, op, reads, writes):
            S.op(eng, lambda e: e.tensor_tensor(out=out, in0=a, in1=b, op=op), reads=reads, writes=writes)

        def ts(eng, out, a, s1, s2, op0, op1, reads, writes):
            if s2 is None:
                S.op(eng, lambda e: e.tensor_scalar(out=out, in0=a, scalar1=s1, scalar2=None, op0=op0), reads=reads, writes=writes)
            else:
                S.op(eng, lambda e: e.tensor_scalar(out=out, in0=a, scalar1=s1, scalar2=s2, op0=op0, op1=op1), reads=reads, writes=writes)

        def stt(eng, out, a, sc, b, op0, op1, reads, writes):
            S.op(eng, lambda e: e.scalar_tensor_tensor(out=out, in0=a, scalar=sc, in1=b, op0=op0, op1=op1), reads=reads, writes=writes)

        def cp(eng, out, a, reads, writes):
            S.op(eng, lambda e: e.tensor_copy(out=out, in_=a), reads=reads, writes=writes)

        def act(out, a, func, reads, writes, **kw):
            S.op("act", lambda e: e.activation(out=out, in_=a, func=func, **kw), reads=reads, writes=writes)

        def mm(out, pairs, reads, writes):
            def fn(e):
                n = len(pairs)
                ins = None
                for i, (l, r) in enumerate(pairs):
                    ins = e.matmul(out, lhsT=l, rhs=r, start=(i == 0), stop=(i == n - 1))
                return ins
            S.op("pe", fn, reads=reads, writes=writes)

        def mm_multi(groups, reads, writes):
            def fn(e):
                ins = None
                for out, pairs in groups:
                    n = len(pairs)
                    for i, (l, r) in enumerate(pairs):
                        ins = e.matmul(out, lhsT=l, rhs=r, start=(i == 0), stop=(i == n - 1))
                return ins
            S.op("pe", fn, reads=reads, writes=writes)

        ident = A.bf([128])
        mask4 = A.f32([512])
        pswap = A.f32([128])
        ccols = A.f32([4])
        iota = A.f32([512])
        dma(ident, c_ident[:, :], writes=["ident"])
        dma(mask4, c_mask4[:, :], writes=["mask4"])
        dma(pswap, c_pswap[:, :], writes=["pswap"])
        dma(ccols, c_cols[:, :], writes=["ccols"])
        dma(iota, c_iota[:, :], writes=["iota"])
        esc = ccols[:, 0:1]
        sgnA = ccols[:, 1:2]
        nsgnA = ccols[:, 2:3]
        CONST_KEYS = ["ident", "mask4", "pswap", "ccols", "iota"]

        def phase_barrier():
            S.barrier()
            for k in CONST_KEYS:
                S.lastw.pop(k, None)

        def norm_T(src_ap, xt, hb, hT, tag, sscol, junk):
            if src_ap is not None:
                dma(xt, src_ap, writes=["xt" + tag])
            act(junk, xt, AF.Square, reads=["xt" + tag], writes=["junk", "ss" + tag], accum_out=sscol[:, 0:1])
            ts("dve", sscol[:, 1:2], sscol[:, 0:1], 1.0 / 1024.0, 1e-6, ALU.mult, ALU.add, ["ss" + tag], ["ss1" + tag])
            act(sscol[:, 2:3], sscol[:, 1:2], AF.Sqrt, reads=["ss1" + tag], writes=["ss2" + tag])
            S.op("dve", lambda e: e.reciprocal(out=sscol[:, 3:4], in_=sscol[:, 2:3]), reads=["ss2" + tag], writes=["ss3" + tag])
            S.op("act", lambda e: e.mul(out=hb, in_=xt, mul=sscol[:, 3:4]), reads=["xt" + tag, "ss3" + tag], writes=["hb" + tag])

            def tp(e):
                ins = None
                for k in range(8):
                    ins = e.transpose(ptr[:, k, :], hb[:, k * 128:(k + 1) * 128], ident)
                return ins
            S.op("pe", tp, reads=["hb" + tag, "ident"], writes=["ptr"])
            cp("dve", hT, ptr[:, :, :], ["ptr"], ["hT" + tag])

        def load_weight(dst_bf, w_dram, ncols, gcol, stage, key):
            for k in range(8):
                q = k % 2
                dma(stage[q][:, 0:ncols], w_dram[k * 128:(k + 1) * 128, :], writes=["wst%d" % q])
                if gcol is None:
                    cp("pool", dst_bf[:, k, :], stage[q][:, 0:ncols], ["wst%d" % q], [key])
                else:
                    ts("pool", dst_bf[:, k, :], stage[q][:, 0:ncols], gcol[:, k:k + 1], None, ALU.mult, None, ["wst%d" % q, "s5cols"], [key])

        def s5_layer(l, src_p, src_s, dst_p, dst_s):
            m0 = A.mark()
            UY = A.bf([8, 8, NCHP])
            UYs = A.bf([8, 8, 4])
            cols = A.f32([24])
            dma(cols, s5cols[l], writes=["s5cols"])
            gpre, Dcol, bglu = cols[:, 0:8], cols[:, 8:16], cols[:, 16:24]
            fin = A.f32([64])
            fins = A.f32([64, 4])
            h0 = A.f32([2, 256])
            h0b = A.bf([64, 4])
            dma(h0, s5h0[l], writes=["h0"])
            cp("pool", flat(h0b), h0[:, 0, :], ["h0"], ["h0b"])
            m1 = A.mark()
            xt = [A.f32([1024]) for _ in range(2)]
            hb = [A.bf([1024]) for _ in range(2)]
            hT = [A.bf([8, 128]) for _ in range(2)]
            sscol = [A.f32([4]) for _ in range(2)]
            junk = A.bf([1024])

            wA = A.bf([8, 1024])
            stage = [A.f32([1024]) for _ in range(2)]
            load_weight(wA, a_w_in[l][:, 0:1024], 1024, gpre, stage, "wA")
            for i in range(NT + 1):
                q = i % 2
                tag = str(q)
                src = src_p[i * 128:(i + 1) * 128, :] if i < NT else src_s
                norm_T(src, xt[q], hb[q], hT[q], tag, sscol[q], junk)
                for half in range(2):
                    pb = pbank[half * 2 + q]
                    groups = []
                    for jj in range(4):
                        j = half * 4 + jj
                        groups.append((pb[:, jj * 128:(jj + 1) * 128],
                                       [(wA[:, k, j * 128:(j + 1) * 128], hT[q][:, k, :]) for k in range(8)]))
                    mm_multi(groups, ["wA", "hT" + tag], ["pb%d" % (half * 2 + q)])
                    pv = pb[:, :].rearrange("p (j m) -> p j m", j=4)
                    if i < NT:
                        outv = UY[:, half * 4:half * 4 + 4, :, i * 16:(i + 1) * 16]
                        inv = pv.rearrange("p j (n s) -> p j s n", s=8)
                        S.op("act", lambda e, outv=outv, inv=inv: e.copy(out=outv, in_=inv),
                             reads=["pb%d" % (half * 2 + q)], writes=["UY"])
                    else:
                        outv = UYs[:, half * 4:half * 4 + 4, :, :]
                        inv = pv.rearrange("p j (i r) -> p j r i", r=32)[:, :, 0:8, :]
                        S.op("act", lambda e, outv=outv, inv=inv: e.copy(out=outv, in_=inv),
                             reads=["pb%d" % (half * 2 + q)], writes=["UYs"])
            A.release(m1)
            phase_barrier()

            m2 = A.mark()
            pAT = A.f32([3, 64])
            pBT = A.f32([2, 64, 16])
            pCT = A.f32([2, 64, 16])
            dma(pAT, s5pAT[l], writes=["pAT"])
            dma(flat(pBT), s5pBT[l].rearrange("p a b -> p (a b)"), writes=["pBT"])
            dma(flat(pCT), s5pCT[l].rearrange("p a b -> p (a b)"), writes=["pCT"])
            PEr = A.f32([NE, 64])
            PEi = A.f32([NE, 64])
            magE = A.f32([NE, 64])
            sm = [A.f32([64]) for _ in range(10)]
            Ewr = A.f32([8, 64])
            Ewi = A.f32([8, 64])
            Prs = A.f32([9, 64])
            f8 = A.f32([64])
            mT = A.mark()
            evals = A.f32([NE, 64])
            dma(flat(evals), c_evals[:, :], writes=["evals"])
            tE = [A.f32([NE, 64]) for _ in range(3)]
            tEi = A.i32([NE, 64])
            Ewt = A.f32([8, 64])
            ArT, AiT, LdT = pAT[:, 0, :], pAT[:, 1, :], pAT[:, 2, :]

            def bcE(v):
                return v.unsqueeze(1).broadcast_to([128, NE, 64])

            act(sm[0], LdT, AF.Exp, ["pAT"], ["sm0"])
            tt("dve", sm[1], sm[0], ArT, ALU.mult, ["sm0", "pAT"], ["sm1"])
            stt("dve", sm[2], sm[0], 1.0 / TWO_PI, AiT, ALU.mult, ALU.mult, ["sm0", "pAT"], ["sm2"])
            tt("dve", tE[0], evals, bcE(sm[1]), ALU.mult, ["evals", "sm1"], ["tE0"])
            act(magE, tE[0], AF.Exp, ["tE0"], ["magE"])
            tt("dve", tE[0], evals, bcE(sm[2]), ALU.mult, ["evals", "sm2", "magE"], ["tE0"])
            cp("dve", tEi, tE[0], ["tE0"], ["tEi"])
            tt("dve", tE[1], tE[0], tEi, ALU.subtract, ["tE0", "tEi"], ["tE1"])
            ts("dve", tE[0], tE[0], 0.25, None, ALU.add, None, ["tE0", "tE1"], ["tE0"])
            cp("dve", tEi, tE[0], ["tE0", "tE1"], ["tEi"])
            tt("dve", tE[2], tE[0], tEi, ALU.subtract, ["tE0", "tEi"], ["tE2"])
            act(tE[1], tE[1], AF.Sin, ["tE1"], ["tE1"], scale=TWO_PI)
            act(tE[2], tE[2], AF.Sin, ["tE2"], ["tE2"], scale=TWO_PI)
            tt("dve", PEr, magE, tE[2], ALU.mult, ["magE", "tE2"], ["PEr"])
            tt("dve", PEi, magE, tE[1], ALU.mult, ["magE", "tE1"], ["PEi"])
            PK = ["PEr", "PEi"]
            lr1, li1 = PEr[:, 9, :], PEi[:, 9, :]
            ts("dve", sm[3], lr1, -1.0, None, ALU.add, None, PK, ["sm3"])
            tt("dve", sm[4], ArT, ArT, ALU.mult, ["pAT"], ["sm4"])
            tt("dve", sm[5], AiT, AiT, ALU.mult, ["pAT"], ["sm5"])
            tt("dve", sm[4], sm[4], sm[5], ALU.add, ["sm4", "sm5"], ["sm4"])
            S.op("dve", lambda e: e.reciprocal(out=sm[4], in_=sm[4]), reads=["sm4"], writes=["sm4"])
            tt("dve", sm[5], sm[3], ArT, ALU.mult, ["sm3", "pAT"], ["sm5"])
            tt("dve", sm[6], li1, AiT, ALU.mult, PK + ["pAT"], ["sm6"])
            tt("dve", sm[5], sm[5], sm[6], ALU.add, ["sm5", "sm6"], ["sm5"])
            tt("dve", sm[7], sm[5], sm[4], ALU.mult, ["sm5", "sm4"], ["sm7"])
            tt("dve", sm[5], li1, ArT, ALU.mult, PK + ["pAT", "sm7"], ["sm5"])
            tt("dve", sm[6], sm[3], AiT, ALU.mult, ["sm3", "pAT", "sm5"], ["sm6"])
            tt("dve", sm[5], sm[5], sm[6], ALU.subtract, ["sm5", "sm6"], ["sm5"])
            tt("dve", sm[8], sm[5], sm[4], ALU.mult, ["sm5", "sm4"], ["sm8"])
            wre, wim = sm[7], sm[8]

            def bc8(v):
                return v.unsqueeze(1).broadcast_to([128, 8, 64])
            tt("dve", Ewr, PEr[:, 0:8, :], bc8(wre), ALU.mult, PK + ["sm7"], ["Ewr"])
            tt("dve", Ewt, PEi[:, 0:8, :], bc8(wim), ALU.mult, PK + ["sm8"], ["Ewt"])
            tt("dve", Ewr, Ewr, Ewt, ALU.subtract, ["Ewr", "Ewt"], ["Ewr"])
            tt("dve", Ewi, PEr[:, 0:8, :], bc8(wim), ALU.mult, PK + ["sm8"], ["Ewi"])
            tt("dve", Ewt, PEi[:, 0:8, :], bc8(wre), ALU.mult, PK + ["sm7", "Ewr"], ["Ewt"])
            tt("dve", Ewi, Ewi, Ewt, ALU.add, ["Ewi", "Ewt"], ["Ewi"])
            ts("dve", flat(Ewi), flat(Ewi), nsgnA, None, ALU.mult, None, ["Ewi", "ccols"], ["Ewi"])
            ts("dve", flat(Prs), flat(PEr[:, 8:17, :]), sgnA, None, ALU.mult, None, PK + ["ccols"], ["Prs"])
            ts("dve", f8, sm[2], 8.0, None, ALU.mult, None, ["sm2"], ["f8"])
            rho8 = magE[:, 16, :]
            a8, b8 = PEr[:, 16, :], PEi[:, 16, :]

            phase_barrier()
            A.release(mT)
            LBs = A.bf([8, 8, 16])
            RCs = A.bf([8, 9, 16])
            Tt = A.bf([8, 128])
            Wt = A.bf([8, 2, 64])
            tl = [A.f32([8, 9, 16]) for _ in range(2)]
            pAs = A.f32([3, 512])
            pBs = A.f32([2, 512])
            tw = [A.f32([512]) for _ in range(8)]
            twi = A.i32([512])
            U = A.bf([8, NCHP])
            Us = A.bf([8, 4])
            Yg = A.bf([8, NCHP])
            Ygs = A.bf([8, 4])
            yT = A.bf([8, NCHP])
            yTs = A.bf([8, 4])
            ga0 = [A.f32([NCHP]) for _ in range(5)]
            ga = [ga0, ga0]
            gi0 = A.i32([NCHP])
            gi = [gi0, gi0]
            Hp = [A.bf([NCHP]) for _ in range(2)]
            gb = [A.f32([NCHP]) for _ in range(2)]
            for q in range(2):
                S.op("pool", lambda e, q=q: e.memset(Hp[q][:, 0:1], 0.0), writes=["Hp%d" % q])
            pM = pbank[6]

            for j in range(8):
                g0 = 8 * j
                in0 = Ewr[:, :, g0:g0 + 8].rearrange("p s g -> p g s").unsqueeze(3).broadcast_to([128, 8, 8, 16])
                in1 = pBT[:, 0, g0:g0 + 8, :].unsqueeze(2).broadcast_to([128, 8, 8, 16])
                tl0 = tl[0][:, :, 0:8, :]
                tl1 = tl[1][:, :, 0:8, :]
                tt("dve", tl0, in0, in1, ALU.mult, ["Ewr", "pBT"], ["tl0"])
                in0 = Ewi[:, :, g0:g0 + 8].rearrange("p s g -> p g s").unsqueeze(3).broadcast_to([128, 8, 8, 16])
                in1 = pBT[:, 1, g0:g0 + 8, :].unsqueeze(2).broadcast_to([128, 8, 8, 16])
                tt("dve", tl1, in0, in1, ALU.mult, ["Ewi", "pBT"], ["tl1"])
                tt("dve", LBs, tl0, tl1, ALU.add, ["tl0", "tl1"], ["LBs"])
                in0 = Prs[:, :, g0:g0 + 8].rearrange("p t g -> p g t").unsqueeze(3).broadcast_to([128, 8, 9, 16])
                in1 = pCT[:, 0, g0:g0 + 8, :].unsqueeze(2).broadcast_to([128, 8, 9, 16])
                tt("dve", tl[0], in0, in1, ALU.mult, ["Prs", "pCT", "tl0"], ["tl0"])
                in0 = PEi[:, 8:17, g0:g0 + 8].rearrange("p t g -> p g t").unsqueeze(3).broadcast_to([128, 8, 9, 16])
                in1 = pCT[:, 1, g0:g0 + 8, :].unsqueeze(2).broadcast_to([128, 8, 9, 16])
                tt("dve", tl[1], in0, in1, ALU.mult, PK + ["pCT", "tl1"], ["tl1"])
                tt("dve", RCs, tl[0], tl[1], ALU.subtract, ["tl0", "tl1"], ["RCs"])
                for hh in range(2):
                    groups = []
                    for gg in range(4):
                        gl = hh * 4 + gg
                        groups.append((pbank[4][:, gg * 128:(gg + 1) * 128],
                                       [(LBs[:, gl, :, :].rearrange("p s c -> p (s c)"),
                                         RCs[:, gl, 0:8, :].rearrange("p t c -> p (t c)"))]))
                    mm_multi(groups, ["LBs", "RCs"], ["pb4"])
                    tt("dve", flat(Tt[:, hh * 4:hh * 4 + 4, :]), pbank[4][:, :], mask4, ALU.mult, ["pb4", "mask4"], ["Tt"])
                dma(pAs, s5pA[l][:, :, g0 * 64:(g0 + 8) * 64], writes=["pAs"])
                dma(pBs, s5pB[l][:, :, g0 * 64:(g0 + 8) * 64], writes=["pBs"])
                AR, AI, LD = pAs[:, 0, :], pAs[:, 1, :], pAs[:, 2, :]
                BR, BI = pBs[:, 0, :], pBs[:, 1, :]
                t = tw
                K = lambda *ix: ["tw%d" % i for i in ix]
                act(t[0], LD, AF.Exp, ["pAs"], K(0))
                tt("dve", t[1], t[0], AR, ALU.mult, K(0) + ["pAs"], K(1))
                stt("dve", t[2], t[0], 1.0 / TWO_PI, AI, ALU.mult, ALU.mult, K(0) + ["pAs"], K(2))
                act(t[3], t[1], AF.Exp, K(1), K(3))
                cp("dve", twi, t[2], K(2), ["twi"])
                tt("dve", t[4], t[2], twi, ALU.subtract, K(2) + ["twi"], K(4))
                ts("dve", t[5], t[2], 0.25, None, ALU.add, None, K(2), K(5))
                cp("dve", twi, t[5], K(5, 4), ["twi"])
                tt("dve", t[5], t[5], twi, ALU.subtract, K(5) + ["twi"], K(5))
                act(t[4], t[4], AF.Sin, K(4), K(4), scale=TWO_PI)
                act(t[5], t[5], AF.Sin, K(5), K(5), scale=TWO_PI)
                tt("dve", t[5], t[3], t[5], ALU.mult, K(3, 5), K(5))
                tt("dve", t[4], t[3], t[4], ALU.mult, K(3, 4), K(4))
                ts("dve", t[5], t[5], -1.0, None, ALU.add, None, K(5), K(5))
                tt("dve", t[0], AR, AR, ALU.mult, ["pAs"] + K(0, 1, 2), K(0))
                tt("dve", t[3], AI, AI, ALU.mult, ["pAs"] + K(3, 4, 5), K(3))
                tt("dve", t[0], t[0], t[3], ALU.add, K(0, 3), K(0))
                S.op("dve", lambda e, a=t[0]: e.reciprocal(out=a, in_=a), reads=K(0), writes=K(0))
                tt("dve", t[3], t[5], AR, ALU.mult, K(5) + ["pAs"], K(3))
                tt("dve", t[6], t[4], AI, ALU.mult, K(4) + ["pAs"], K(6))
                tt("dve", t[3], t[3], t[6], ALU.add, K(3, 6), K(3))
                tt("dve", t[3], t[3], t[0], ALU.mult, K(3, 0), K(3))
                tt("dve", t[6], t[4], AR, ALU.mult, K(4, 3) + ["pAs"], K(6))
                tt("dve", t[7], t[5], AI, ALU.mult, K(5) + ["pAs"], K(7))
                tt("dve", t[6], t[6], t[7], ALU.subtract, K(6, 7), K(6))
                tt("dve", t[6], t[6], t[0], ALU.mult, K(6, 0), K(6))
                tt("dve", t[0], t[3], BR, ALU.mult, K(3, 0) + ["pBs"], K(0))
                tt("dve", t[4], t[6], BI, ALU.mult, K(6, 4) + ["pBs"], K(4))
                tt("dve", t[0], t[0], t[4], ALU.subtract, K(0, 4), K(0))
                tt("dve", t[4], t[3], BI, ALU.mult, K(3, 4) + ["pBs"], K(4))
                tt("dve", t[5], t[6], BR, ALU.mult, K(6, 5) + ["pBs"], K(5))
                tt("dve", t[4], t[4], t[5], ALU.add, K(4, 5), K(4))
                act(t[3], t[1], AF.Exp, K(1, 3), K(3), scale=esc)
                ts("dve", t[5], t[2], esc, None, ALU.mult, None, K(2, 5) + ["ccols"], K(5))
                cp("dve", twi, t[5], K(5), ["twi"])
                tt("dve", t[6], t[5], twi, ALU.subtract, K(5, 6) + ["twi"], K(6))
                ts("dve", t[5], t[5], 0.25, None, ALU.add, None, K(5, 6), K(5))
                cp("dve", twi, t[5], K(5, 6), ["twi"])
                tt("dve", t[5], t[5], twi, ALU.subtract, K(5) + ["twi"], K(5))
                act(t[6], t[6], AF.Sin, K(6), K(6), scale=TWO_PI)
                act(t[5], t[5], AF.Sin, K(5), K(5), scale=TWO_PI)
                tt("dve", t[5], t[3], t[5], ALU.mult, K(3, 5), K(5))
                tt("dve", t[6], t[3], t[6], ALU.mult, K(3, 6), K(6))
                v3 = lambda a: a.rearrange("p (g q) -> p g q", g=8)
                tt("dve", t[3], t[5], t[0], ALU.mult, K(5, 0, 3), K(3))
                tt("dve", t[7], t[6], t[4], ALU.mult, K(6, 4, 7), K(7))
                tt("dve", Wt[:, :, 0, :], v3(t[3]), v3(t[7]), ALU.subtract, K(3, 7), ["Wt"])
                tt("dve", t[3], t[5], t[4], ALU.mult, K(5, 4, 3), K(3))
                tt("dve", t[7], t[6], t[0], ALU.mult, K(6, 0, 7), K(7))
                tt("dve", Wt[:, :, 1, :], v3(t[3]), v3(t[7]), ALU.add, K(3, 7), ["Wt"])
                dma(scrU.rearrange("g c s n -> (g c) s n"), UY[:, j, :, :], reads=["UY"], writes=["scrU"])
                for s_ in range(8):
                    dma(U[16 * s_:16 * s_ + 16, :, :], scrU[:, :, s_, :].rearrange("g c n -> c g n"),
                        reads=["scrU"], writes=["U"])
                dma(scrUs.rearrange("g c s n -> (g c) s n"), UYs[:, j, :, :], reads=["UYs"], writes=["scrUs"])
                for s_ in range(8):
                    dma(Us[16 * s_:16 * s_ + 16, :, :], scrUs[:, :, s_, :].rearrange("g c n -> c g n"),
                        reads=["scrUs"], writes=["Us"])
                for gl in range(8):
                    g = g0 + gl
                    q = gl % 2
                    a = ga[q]
                    KA = lambda *ix: ["ga_%d" % i for i in ix]
                    pS, pSw, pG, pY = pbank[q], pbank[2 + q], pbank[4], pbank[5]
                    Wg = Wt[:, gl, :, :]
                    Ug = U[:, gl, :]
                    mm_multi([(pS[:, 0:NCHP], [(Wg.rearrange("p r q -> p (r q)"), Ug)]),
                              (pSw[64:128, 0:NCHP], [(Wg[:, 0, :], Ug)]),
                              (pSw[0:64, 0:NCHP], [(Wg[:, 1, :], Ug)])],
                             ["Wt", "U"], ["pb%d" % q, "pb%d" % (2 + q)])
                    ts("dve", a[0], iota[:, 0:NCHP], f8[:, g:g + 1], None, ALU.mult, None, ["iota", "f8"] + KA(0), KA(0))
                    cp("dve", gi[q], a[0], KA(0), ["gi"])
                    tt("dve", a[1], a[0], gi[q], ALU.subtract, KA(0, 1) + ["gi"], KA(1))
                    ts("dve", a[0], a[0], 0.25, None, ALU.add, None, KA(0, 1), KA(0))
                    cp("dve", gi[q], a[0], KA(0, 1), ["gi"])
                    tt("dve", a[0], a[0], gi[q], ALU.subtract, KA(0) + ["gi"], KA(0))
                    act(a[1], a[1], AF.Sin, KA(1), KA(1), scale=TWO_PI)
                    act(a[0], a[0], AF.Sin, KA(0), KA(0), scale=TWO_PI)
                    tt("dve", a[2], a[0], pS[:, 0:NCHP], ALU.mult, KA(0, 2) + ["pb%d" % q], KA(2))
                    stt("dve", a[3], a[1], sgnA, pSw[:, 0:NCHP], ALU.mult, ALU.mult, KA(1, 3) + ["ccols", "pb%d" % (2 + q)], KA(3))
                    tt("dve", a[2], a[2], a[3], ALU.add, KA(2, 3), KA(2))
                    S.op("dve", lambda e, o=a[3], d1=a[2], g=g: e.tensor_tensor_scan(
                        out=o, data0=rho8[:, g:g + 1].to_broadcast([128, NCHP]), data1=d1, initial=0.0,
                        op0=ALU.mult, op1=ALU.add), reads=KA(2, 3) + ["magE"], writes=KA(3))
                    mm(pG[:, 0:NCHP], [(pswap, a[3])], ["pswap"] + KA(3), ["pb4"])
                    tt("dve", a[2], a[0], a[3], ALU.mult, KA(0, 3, 2), KA(2))
                    stt("dve", a[4], a[1], nsgnA, pG[:, 0:NCHP], ALU.mult, ALU.mult, KA(1, 4) + ["ccols", "pb4"], KA(4))
                    tt("dve", Hp[q][:, 1:NCHP], a[2][:, 0:NCHP - 1], a[4][:, 0:NCHP - 1], ALU.add, KA(2, 4), ["Hp%d" % q])
                    tt("dve", fin[:, g:g + 1], a[2][:, NCHP - 1:NCHP], a[4][:, NCHP - 1:NCHP], ALU.add, KA(2, 4), ["fin"])
                    Tg = Tt[:, gl, :]
                    Vg = RCs[:, gl, 1:9, :].rearrange("p t c -> p (t c)")
                    mm(pY[:, 0:NCHP], [(Tg, Ug), (Vg, Hp[q])], ["Tt", "RCs", "U", "Hp%d" % q], ["pb5"])
                    S.op("act", lambda e, o=Yg[:, gl, :], i_=pY[:, 0:NCHP]: e.copy(out=o, in_=i_), reads=["pb5"], writes=["Yg"])
                    mm_multi([(pM[:, g * 4:g * 4 + 4], [(Wg.rearrange("p r q -> p (r q)"), Us[:, gl, :])]),
                              (pM[:, 256 + gl * 4:256 + gl * 4 + 4], [(Tg, Us[:, gl, :]), (Vg, h0b[:, g, :])])],
                             ["Wt", "Tt", "RCs", "Us", "h0b"], ["pb6"])
                S.op("act", lambda e: e.copy(out=flat(Ygs), in_=pM[:, 256:288]), reads=["pb6"], writes=["Ygs"])
                dma(scrY.rearrange("t c g n -> (t c) g n"), Yg, reads=["Yg"], writes=["scrY"])
                for g_ in range(8):
                    dma(yT[16 * g_:16 * g_ + 16, :, :], scrY[:, :, g_, :].rearrange("t c n -> c t n"),
                        reads=["scrY"], writes=["yT"])
                dma(scrYs.rearrange("t c g n -> (t c) g n"), Ygs, reads=["Ygs"], writes=["scrYs"])
                for g_ in range(8):
                    dma(yTs[16 * g_:16 * g_ + 16, :, :], scrYs[:, :, g_, :].rearrange("t c n -> c t n"),
                        reads=["scrYs"], writes=["yTs"])
                def gelu_block(uview, yview, n, q):
                    b0 = gb[0][:, 0:n]
                    b1 = gb[1][:, 0:n]
                    stt("dve", b0, uview, Dcol[:, j:j + 1], yview, ALU.mult, ALU.add, ["UY", "UYs", "yT", "yTs", "s5cols", "gb0"], ["gb0"])
                    act(b1, b0, AF.Square, ["gb0", "gb1"], ["gb1"])
                    ts("dve", b1, b1, 0.044715, 1.0, ALU.mult, ALU.add, ["gb1"], ["gb1"])
                    tt("dve", b1, b1, b0, ALU.mult, ["gb1", "gb0"], ["gb1"])
                    act(b1, b1, AF.Sigmoid, ["gb1"], ["gb1"], scale=1.5957691216057308)
                    tt("dve", uview, b0, b1, ALU.mult, ["gb0", "gb1"], ["UY", "UYs"])
                for s_ in range(8):
                    gelu_block(UY[:, j, s_, :], yT[:, s_, :], NCHP, 0)
                gelu_block(flat(UYs[:, j, :, :]), flat(yTs), 32, 0)
            dma(fin_p[l], fin, reads=["fin"])
            h0v = h0[:, 0, :].rearrange("p (g i) -> p g i", i=4)
            h0s = h0[:, 1, :].rearrange("p (g i) -> p g i", i=4)
            bc4 = lambda v: v.unsqueeze(2).broadcast_to([128, 64, 4])
            fs2 = A.f32([64, 4])
            b8n = A.f32([64])
            ts("dve", b8n, b8, nsgnA, None, ALU.mult, None, PK + ["ccols"], ["b8n"])
            tt("dve", fins, h0v, bc4(a8), ALU.mult, ["h0"] + PK, ["fins"])
            tt("dve", fs2, h0s, bc4(b8n), ALU.mult, ["h0", "b8n"], ["fs2"])
            tt("dve", fins, fins, fs2, ALU.add, ["fins", "fs2"], ["fins"])
            tt("dve", flat(fins), flat(fins), pM[:, 0:256], ALU.add, ["fins", "pb6"], ["fins"])
            dma(fin_s[l], flat(fins), reads=["fins"])
            A.release(m2)
            phase_barrier()

            m3 = A.mark()
            xt = [A.f32([1024]) for _ in range(2)]
            hb = [A.bf([1024]) for _ in range(2)]
            hT = [A.bf([8, 128]) for _ in range(2)]
            sscol = [A.f32([4]) for _ in range(2)]
            junk = A.bf([1024])
            wZ = A.bf([8, 1024])
            wB = A.bf([8, 1024])
            wC = A.bf([8, 1024])
            gpost = A.f32([1024])
            stage = [A.f32([1024]) for _ in range(2)]
            load_weight(wZ, a_w_in[l][:, 1024:2048], 1024, gpre, stage, "wZ")
            load_weight(wB, a_w_glu[l], 1024, None, stage, "wB")
            load_weight(wC, a_w_out[l], 1024, None, stage, "wC")
            dma(gpost, s5gpost[l], writes=["gpost"])
            sz = A.f32([8, 128])
            sg = A.f32([8, 128])
            vT = A.bf([8, 128])
            ysamp = A.bf([8, 128])
            res = [A.f32([1024]) for _ in range(2)]
            ss2 = [A.f32([4]) for _ in range(2)]
            S.op("pool", lambda e: e.memset(flat(ysamp), 0.0), writes=["ysamp"])
            cp("dve", ysamp.rearrange("p k (i r) -> p k r i", r=32)[:, :, 0:8, :], UYs, ["UYs", "ysamp"], ["ysamp"])
            src_pv = src_p.rearrange("(n s) f -> s n f", s=8)
            dst_pv = dst_p.rearrange("(n s) f -> s n f", s=8)
            sets = [(s_, nb) for s_ in range(8) for nb in range(NB)] + [None]
            for it, tset in enumerate(sets):
                q = it % 2
                tag = str(q)
                if tset is not None:
                    s_, nb = tset
                    src = src_pv[s_, nb * 128:(nb + 1) * 128, :]
                    dst = dst_pv[s_, nb * 128:(nb + 1) * 128, :]
                    yv = UY[:, :, s_, nb * 128:(nb + 1) * 128]
                else:
                    src, dst = src_s, dst_s
                    yv = ysamp
                norm_T(src, xt[q], hb[q], hT[q], tag, sscol[q], junk)
                for half in range(2):
                    pb = pbank[half]
                    groups = []
                    for jj in range(4):
                        o = half * 4 + jj
                        groups.append((pb[:, jj * 128:(jj + 1) * 128],
                                       [(wZ[:, k, o * 128:(o + 1) * 128], hT[q][:, k, :]) for k in range(8)]))
                    mm_multi(groups, ["wZ", "hT" + tag], ["pb%d" % half])
                    act(flat(sz[:, half * 4:half * 4 + 4, :]), pb[:, :], AF.Silu, ["pb%d" % half], ["sz"])
                for half in range(2):
                    pb = pbank[2 + half]
                    groups = []
                    for jj in range(4):
                        o = half * 4 + jj
                        groups.append((pb[:, jj * 128:(jj + 1) * 128],
                                       [(wB[:, k, o * 128:(o + 1) * 128], yv[:, k, :]) for k in range(8)]))
                    mm_multi(groups, ["wB", "UY", "ysamp"], ["pb%d" % (2 + half)])
                    for jj in range(4):
                        o = half * 4 + jj
                        act(sg[:, o, :], pb[:, jj * 128:(jj + 1) * 128], AF.Sigmoid, ["pb%d" % (2 + half), "s5cols"], ["sg"],
                            bias=bglu[:, o:o + 1])
                tt("dve", sg, sg, sz, ALU.mult, ["sg", "sz"], ["sg"])
                tt("dve", vT, sg, yv, ALU.mult, ["sg", "UY", "ysamp"], ["vT"])
                for half in range(2):
                    pb = pbank[4 + half]
                    mm(pb[:, :], [(vT[:, k, :], wC[:, k, half * 512:(half + 1) * 512]) for k in range(8)], ["vT", "wC"], ["pb%d" % (4 + half)])
                post_norm_residual(pbank[4], pbank[5], "pb4", "pb5", xt[q], "xt" + tag, gpost, res[q], "res" + tag, ss2[q], "ssb" + tag, junk, dst)
            A.release(m3)
            A.release(m0)
            phase_barrier()

        def post_norm_residual(pa, pb, ka, kb, xtile, kx, gpost, res, kres, ss, kss, junk, dst):
            act(junk[:, 0:512], pa[:, :], AF.Square, [ka], ["junk", kss + "a"], accum_out=ss[:, 0:1])
            act(junk[:, 512:1024], pb[:, :], AF.Square, [kb], ["junk", kss + "b"], accum_out=ss[:, 1:2])
            tt("dve", ss[:, 2:3], ss[:, 0:1], ss[:, 1:2], ALU.add, [kss + "a", kss + "b"], [kss + "c"])
            ts("dve", ss[:, 2:3], ss[:, 2:3], 1.0 / 1024.0, 1e-6, ALU.mult, ALU.add, [kss + "c"], [kss + "c"])
            act(ss[:, 3:4], ss[:, 2:3], AF.Sqrt, [kss + "c"], [kss + "d"])
            S.op("dve", lambda e: e.reciprocal(out=ss[:, 3:4], in_=ss[:, 3:4]), reads=[kss + "d"], writes=[kss + "d"])
            stt("dve", res[:, 0:512], pa[:, :], ss[:, 3:4], gpost[:, 0:512], ALU.mult, ALU.mult, [ka, kss + "d", "gpost"], [kres])
            stt("dve", res[:, 512:1024], pb[:, :], ss[:, 3:4], gpost[:, 512:1024], ALU.mult, ALU.mult, [kb, kss + "d", "gpost"], [kres])
            tt("pool", res, res, xtile, ALU.add, [kres, kx], [kres])
            dma(dst, res, reads=[kres], writes=["dram_res"])

        ATT = {}

        def load_weight2(dst_bf, w_dram, ncols, gcol, mul, stage, key, gkey):
            for k in range(8):
                q = k % len(stage)
                dma(stage[q][:, 0:ncols], w_dram[k * 128:(k + 1) * 128, :], writes=["wst%d" % q])
                ts("pool", dst_bf[:, k, :], stage[q][:, 0:ncols], gcol[:, k:k + 1], mul, ALU.mult, ALU.mult, ["wst%d" % q, gkey], [key])

        def kv_phase(src_p, src_s):
            KT = A.bf([8, SEQ])
            Vb = A.bf([NT, 1024])
            KTs = A.bf([8, 128])
            sbc = A.f32([24])
            ccore = A.f32([4])
            cattn = A.bf([6, 128])
            identf = A.f32([128])
            onesrow = A.f32([128])
            dma(sbc, sbcols[:, :], writes=["sbc"])
            dma(ccore, c_core[:, :], writes=["ccore"])
            dma(flat(cattn), c_attn.rearrange("p a b -> p (a b)"), writes=["cattn"])
            dma(identf, c_identf[:, :], writes=["identf"])
            dma(onesrow[0:1, :], c_onesrow[:, :], writes=["onesrow"])
            CONST_KEYS.extend(["sbc", "ccore", "cattn", "identf", "onesrow"])
            ATT.update(KT=KT, Vb=Vb, KTs=KTs, sbc=sbc, ccore=ccore, cattn=cattn, identf=identf, onesrow=onesrow)
            m = A.mark()
            wKV = A.bf([8, 2048])
            stage = [A.f32([2048])]
            load_weight2(wKV, w_kv, 2048, sbc[:, 0:8], 1.0, stage, "wKV", "sbc")
            xt = A.f32([1024]); hb = A.bf([1024]); hT = A.bf([8, 128]); ssc = A.f32([4]); junk = A.bf([1024])
            kvo = [A.f32([2048]) for _ in range(2)]
            for i in range(NT + 1):
                q = i % 2
                src = src_p[i * 128:(i + 1) * 128, :] if i < NT else src_s
                norm_T(src, xt, hb, hT, "k", ssc, junk)
                for half in range(2):
                    pb = pbank[half]
                    groups = []
                    for jj in range(4):
                        a = half * 4 + jj
                        groups.append((pb[:, jj * 128:(jj + 1) * 128], [(wKV[:, k, a * 128:(a + 1) * 128], hT[:, k, :]) for k in range(8)]))
                    mm_multi(groups, ["wKV", "hTk"], ["pb%d" % half])
                    pv = pb[:, :].rearrange("p (j m) -> p j m", j=4)
                    if i < NT:
                        outv = KT[:, half * 4:half * 4 + 4, i * 128:(i + 1) * 128]
                    else:
                        outv = KTs[:, half * 4:half * 4 + 4, :]
                    S.op("act", lambda e, outv=outv, pv=pv: e.copy(out=outv, in_=pv), reads=["pb%d" % half], writes=["KT"])
                for n in range(4):
                    pb = pbank[2 + n]
                    mm(pb[:, :], [(hT[:, k, :], wKV[:, k, n * 512:(n + 1) * 512]) for k in range(8)], ["wKV", "hTk"], ["pb%d" % (2 + n)])
                    S.op("act", lambda e, o=kvo[q][:, n * 512:(n + 1) * 512], pb=pb: e.copy(out=o, in_=pb[:, :]), reads=["pb%d" % (2 + n)], writes=["kvo%d" % q])
                if i < NT:
                    dma(k_p[i * 128:(i + 1) * 128, :], kvo[q][:, 0:1024], reads=["kvo%d" % q])
                    dma(v_p[i * 128:(i + 1) * 128, :], kvo[q][:, 1024:2048], reads=["kvo%d" % q])
                    cp("pool", Vb[:, i, :], kvo[q][:, 1024:2048], ["kvo%d" % q], ["Vb"])
                else:
                    dma(k_s[:, :], kvo[q][:, 0:1024], reads=["kvo%d" % q])
                    dma(v_s[:, :], kvo[q][:, 1024:2048], reads=["kvo%d" % q], writes=["v_s_dram"])
            A.release(m)
            phase_barrier()

        def sb_layer(j, first, src_nat, src_q, src_s, dst_q, dst_s):
            KT, Vb, KTs, sbc, ccore, cattn = ATT["KT"], ATT["Vb"], ATT["KTs"], ATT["sbc"], ATT["ccore"], ATT["cattn"]
            identf, onesrow = ATT["identf"], ATT["onesrow"]
            triL, onesb, M0, M1, mnew, zerosb = cattn[:, 0, :], cattn[:, 1, :], cattn[:, 2, :], cattn[:, 3, :], cattn[:, 4, :], cattn[:, 5, :]
            QG = cfg.QG
            NQ = QG * 128
            mL = A.mark()
            gpost = A.f32([1024])
            biasT = A.f32([16])
            biasrow = A.f32([128])
            dma(gpost, sbgpost[j], writes=["gpost"])
            dma(biasT, sbbias[j], writes=["biasT"])
            dma(biasrow, sbbiasrow[j], writes=["biasrow"])
            ebias = A.f32([128])
            act(ebias, biasrow, AF.Exp, ["biasrow"], ["ebias"])
            gcol = sbc[:, 8 + 8 * j:16 + 8 * j]
            xt = A.f32([1024]); hb = A.bf([1024]); hT = A.bf([8, 128]); ssc = A.f32([4]); junk = A.bf([1024])
            res = A.f32([1024]); ss2 = A.f32([4])
            off_qT = A.off; qT = A.bf([8, NQ]); gT = A.bf([8, NQ]); off_hT4 = A.off; hT4 = A.bf([8, NQ])

            def load_x(slot):
                if slot is None:
                    dma(xt, src_s, writes=["xtk"])
                elif first:
                    dma(xt, src_nat[(2 * slot) * 128:(2 * slot + 1) * 128, :], writes=["xtk"])
                    dma(res, src_nat[(2 * slot + 1) * 128:(2 * slot + 2) * 128, :], writes=["res"])
                    ts("dve", xt, xt, ccore[:, 0:1], None, ALU.mult, None, ["xtk", "ccore"], ["xtk"])
                    stt("dve", xt, res, ccore[:, 1:2], xt, ALU.mult, ALU.add, ["res", "xtk", "ccore"], ["xtk"])
                else:
                    dma(xt, src_q[slot * 128:(slot + 1) * 128, :], writes=["xtk"])

            def project(slots, ncols):
                mR = A.mark()
                wQh = A.bf([8, 1024])
                stage = [A.f32([1024]) for _ in range(2)]
                load_weight2(wQh, b_w_in[j][:, 0:1024], 1024, gcol, 0.125, stage, "wQh", "sbc")
                for t, slot in enumerate(slots):
                    load_x(slot)
                    norm_T(None, xt, hb, hT, "k", ssc, junk)
                    cp("dve", hT4[:, :, t * 128:(t + 1) * 128], hT, ["hTk"], ["hT4"])
                for a in range(8):
                    pb = pbank[a % 2]
                    mm(pb[:, 0:ncols], [(wQh[:, k, a * 128:(a + 1) * 128], hT4[:, k, 0:ncols]) for k in range(8)], ["wQh", "hT4"], ["pb%d" % (a % 2)])
                    S.op("act", lambda e, o=qT[:, a, 0:ncols], pb=pb: e.copy(out=o, in_=pb[:, 0:ncols]), reads=["pb%d" % (a % 2)], writes=["qT"])
                load_weight2(wQh, b_w_in[j][:, 1024:2048], 1024, gcol, 1.0, stage, "wQh", "sbc")
                for a in range(8):
                    pb = pbank[a % 2]
                    mm(pb[:, 0:ncols], [(wQh[:, k, a * 128:(a + 1) * 128], hT4[:, k, 0:ncols]) for k in range(8)], ["wQh", "hT4"], ["pb%d" % (a % 2)])
                    act(gT[:, a, 0:ncols], pb[:, 0:ncols], AF.Silu, ["pb%d" % (a % 2)], ["gT"])
                A.release(mR)
                phase_barrier()

            def out_proj(slots):
                mR = A.mark()
                wO = A.bf([8, 1024])
                stage = [A.f32([1024]) for _ in range(2)]
                load_weight(wO, b_w_out[j], 1024, None, stage, "wO")
                for t, slot in enumerate(slots):
                    load_x(slot)
                    for half in range(2):
                        pb = pbank[4 + half]
                        mm(pb[:, :], [(gT[:, k, t * 128:(t + 1) * 128], wO[:, k, half * 512:(half + 1) * 512]) for k in range(8)], ["gT", "wO"], ["pb%d" % (4 + half)])
                    dst = dst_s if slot is None else dst_q[slot * 128:(slot + 1) * 128, :]
                    post_norm_residual(pbank[4], pbank[5], "pb4", "pb5", xt, "xtk", gpost, res, "res", ss2, "ssb", junk, dst)
                A.release(mR)
                phase_barrier()

            for G in range(0 if DBG.get("noprompt") else cfg.NSLOT // QG):
                slots = [G * QG + t for t in range(QG)]
                project(slots, NQ)
                mR = A.mark()
                NBUF, LAG = 4, 2
                ez = [A.f32([NQ]) for _ in range(NBUF)]
                e1 = [A.f32([NQ]) for _ in range(2)]
                sp = [A.bf([NQ]) for _ in range(NBUF)]
                Ab = [A.bf([NQ]) for _ in range(2)]
                RSb = [A.bf([NQ]) for _ in range(NBUF)]
                RS32 = A.f32([NQ])
                KBmax = 2 * (G * QG + QG - 1) + 1
                for h in range(16):
                    a, hb_ = h // 2, (h % 2) * 64
                    po = pbank[4 + a % 2]
                    S.op("pool", lambda e: e.memset(RS32, 0.0), writes=["RS32"])
                    for b_ in range(NBUF):
                        S.op("pool", lambda e, b_=b_: e.memset(RSb[b_], 0.0), writes=["RSb%d" % b_])
                    units = list(range(KBmax, -1, -1))
                    n = len(units)
                    S.op("pe", lambda e: e.matmul(po[hb_:hb_ + 64, 0:NQ], lhsT=zerosb[:, 0:64], rhs=KT[:, a, 0:NQ], start=True, stop=False),
                         reads=["cattn", "KT"], writes=["po%d_%d" % (a % 2, h % 2)])

                    def geo(kb):
                        c0 = max(0, kb // 2 - G * QG) * 128
                        t = kb // 2 - G * QG
                        return c0, (t if t >= 0 else None), kb % 2

                    def s1(u):
                        kb = units[u]; c0, t, x = geo(kb); p = u % 2
                        mm(pbank[p][:, c0:NQ], [(KT[hb_:hb_ + 64, a, kb * 128:(kb + 1) * 128], qT[hb_:hb_ + 64, a, c0:NQ])], ["KT", "qT"], ["pb%d" % p])

                    def s2(u):
                        kb = units[u]; c0, t, x = geo(kb); p = u % 2; b = u % NBUF; bn = (u + 1) % NBUF
                        act(ez[b][:, c0:NQ], pbank[p][:, c0:NQ], AF.Exp, ["pb%d" % p, "biasT"], ["ez%d" % b], bias=biasT[:, h:h + 1])
                        act(sp[b][:, c0:NQ], ez[b][:, c0:NQ], AF.Ln, ["ez%d" % b], ["sp%d" % b], bias=1.0)
                        if t is not None:
                            M = M0 if x == 0 else M1
                            tt("dve", sp[b][:, t * 128:(t + 1) * 128], sp[b][:, t * 128:(t + 1) * 128], M, ALU.mult, ["sp%d" % b, "cattn"], ["sp%d" % b])
                        if kb > 0:
                            tt("pool", RS32[:, c0:NQ], RS32[:, c0:NQ], sp[b][:, c0:NQ], ALU.add, ["RS32", "sp%d" % b], ["RS32"])
                            cp("pool", RSb[bn][:, c0:NQ], RS32[:, c0:NQ], ["RS32"], ["RSb%d" % bn])

                    def s3(u):
                        kb = units[u]; c0, t, x = geo(kb); p = u % 2; b = u % NBUF
                        mm(pbank[2 + p][:, c0:NQ], [(triL, sp[b][:, c0:NQ]), (onesb, RSb[b][:, c0:NQ])], ["cattn", "sp%d" % b, "RSb%d" % b], ["pb%d" % (2 + p)])

                    def s4(u):
                        kb = units[u]; c0, t, x = geo(kb); p = u % 2; b = u % NBUF
                        act(e1[p][:, c0:NQ], pbank[2 + p][:, c0:NQ], AF.Exp, ["pb%d" % (2 + p)], ["e1%d" % p], scale=-1.0)
                        tt("dve", Ab[p][:, c0:NQ], ez[b][:, c0:NQ], e1[p][:, c0:NQ], ALU.mult, ["ez%d" % b, "e1%d" % p], ["Ab%d" % p])
                        if t is not None:
                            M = M0 if x == 0 else M1
                            tt("dve", Ab[p][:, t * 128:(t + 1) * 128], Ab[p][:, t * 128:(t + 1) * 128], M, ALU.mult, ["Ab%d" % p, "cattn"], ["Ab%d" % p])

                    def s5(u):
                        kb = units[u]; c0, t, x = geo(kb); p = u % 2

                        def fn(e, kb=kb, c0=c0, p=p):
                            ins = None
                            for tl_ in range(c0 // 128, QG):
                                sig = G * QG + tl_
                                ins = e.matmul(po[hb_:hb_ + 64, tl_ * 128:(tl_ + 1) * 128], lhsT=Vb[:, kb, h * 64:(h + 1) * 64],
                                               rhs=Ab[p][:, tl_ * 128:(tl_ + 1) * 128], start=False, stop=(kb == 0))
                            return ins
                        S.op("pe", fn, reads=["Vb", "Ab%d" % p], writes=["po%d_%d" % (a % 2, h % 2)])

                    for tstep in range(n + LAG + 1):
                        if tstep < n:
                            s1(tstep); s2(tstep)
                        if 0 <= tstep - LAG < n:
                            s3(tstep - LAG); s4(tstep - LAG)
                        if 0 <= tstep - LAG - 1 < n:
                            s5(tstep - LAG - 1)
                    if h % 2 == 1:
                        tt("dve", gT[:, a, :], po[:, 0:NQ], gT[:, a, :], ALU.mult, ["po%d_0" % (a % 2), "po%d_1" % (a % 2), "gT"], ["gT"])
                A.release(mR)
                phase_barrier()
                out_proj(slots)

            if cfg.sample:
                project([None], 128)
                mR = A.mark()
                NPG = cfg.NPG
                idx = idx_t; idf = A.f32([4 * NPG]); ptt = A.i32([4 * NPG])
                if cfg.sample_attn:
                    dma(ptt, ptab[:, :], writes=["ptt"])
                    ts("dve", idf, ptt, 128.0, ccore[:, 2:3], ALU.mult, ALU.add, ["ptt", "ccore"], ["idf"])
                    cp("dve", idx[:, :], idf, ["idf"], ["idx"])
                A2 = Arena(arena_t[:, off_hT4:off_hT4 + 4 * NQ], 4 * NQ)
                Kpg = [A2.f32([1024])] if 4 * NQ >= 2048 else [A.f32([1024])]
                Vpg = [A2.f32([1024])] if 4 * NQ >= 2048 else [A.f32([1024])]
                KTp = [A.bf([8, 128])]
                Vbp = [A.bf([1024])]
                Kb = [A.bf([1024])]
                ez = [A.f32([128]) for _ in range(2)]
                e1 = [A.f32([128]) for _ in range(2)]
                sp = [A.bf([128]) for _ in range(2)]
                Ab = [A.bf([128]) for _ in range(2)]
                RSb = [A.bf([128]) for _ in range(2)]
                RS32 = A.f32([128])
                fz = A.f32([8])
                qTm = A.bf([8, 2, 128])
                S.op("pool", lambda e: e.memset(flat(qTm), 0.0), writes=["qTm"])
                cp("dve", qTm[0:64, :, 0, :], qT[0:64, :, 0:128], ["qT", "qTm"], ["qTm"])
                cp("dve", qTm[64:128, :, 1, :], qT[64:128, :, 0:128], ["qT", "qTm"], ["qTm"])
                pos = pbank[4]
                phase_barrier()
                if 4 * NQ >= 2048:
                    A3 = Arena(arena_t[:, off_qT:off_qT + 4 * NQ], 4 * NQ)
                    Kpg.append(A3.f32([1024])); Vpg.append(A3.f32([1024]))
                else:
                    Kpg.append(A.f32([1024])); Vpg.append(A.f32([1024]))
                for i in range(4 if cfg.sample_attn else 0):
                    S.op("pool", lambda e: e.memset(RS32, 0.0), writes=["RS32"])
                    S.op("pool", lambda e: e.memset(RSb[0], 0.0), writes=["RSb0"])
                    S.op("pool", lambda e: e.memset(RSb[1], 0.0), writes=["RSb1"])
                    units = [None] + list(range(NPG - 1, -1, -1))
                    n = len(units)

                    def G(u):
                        pg = units[u]; pk = u % 2
                        col = i * NPG + pg
                        if DBG.get("nogather"):
                            dma(Kpg[pk][:, :], cache_k[pg * 128:(pg + 1) * 128, :], writes=["Kpg%d" % pk])
                            dma(Vpg[pk][:, :], cache_v[pg * 128:(pg + 1) * 128, :], writes=["Vpg%d" % pk])
                            return
                        S.op("pool", lambda e, pk=pk, col=col: e.indirect_dma_start(out=Kpg[pk][:, :], out_offset=None, in_=cache_k[:, :],
                             in_offset=bass.IndirectOffsetOnAxis(ap=idx[:, col:col + 1], axis=0)), reads=["idx"], writes=["Kpg%d" % pk], dma=True)
                        S.op("pool", lambda e, pk=pk, col=col: e.indirect_dma_start(out=Vpg[pk][:, :], out_offset=None, in_=cache_v[:, :],
                             in_offset=bass.IndirectOffsetOnAxis(ap=idx[:, col:col + 1], axis=0)), reads=["idx"], writes=["Vpg%d" % pk], dma=True)

                    def s0(u):
                        pg = units[u]; pk = u % 2
                        if pg is None:
                            dma(Vpg[pk][0:8, :], v_s[32 * i:32 * i + 8, :], reads=["v_s_dram"], writes=["Vpg%d" % pk])
                            cp("pool", Vbp[0][0:8, :], Vpg[pk][0:8, :], ["Vpg%d" % pk], ["Vbp0"])
                            return
                        cp("dve", Kb[0], Kpg[pk], ["Kpg%d" % pk], ["Kb0"])

                        def tp(e):
                            ins = None
                            for k in range(8):
                                ins = e.transpose(ptr[:, k, :], Kb[0][:, k * 128:(k + 1) * 128], ident)
                            return ins
                        S.op("pe", tp, reads=["Kb0", "ident", "fz"], writes=["ptr"])
                        S.op("act", lambda e, o=KTp[0]: e.copy(out=o, in_=ptr[:, :, :]), reads=["ptr"], writes=["KTp0"])

                    def cpV(u):
                        pk = u % 2
                        cp("pool", Vbp[0], Vpg[pk], ["Vpg%d" % pk], ["Vbp0"])

                    def s1(u):
                        pg = units[u]; p = u % 2
                        nk = 8 if pg is None else 128

                        if (DBG.get("noA") and pg is None) or (DBG.get("noB") and pg is not None):
                            return

                        def fn(e, p=p, pg=pg, nk=nk):
                            ins = None
                            for h in range(1 if DBG.get("noD") else 16):
                                a_, hb_ = h // 2, (h % 2) * 64
                                if DBG.get("noC"):
                                    hb_ = 0
                                if pg is None:
                                    kt = KTs[:, a_, 32 * i:32 * i + 8]
                                else:
                                    kt = KTp[0][:, a_, :]
                                ins = e.matmul(pbank[p][0:nk, h * 8:(h + 1) * 8], lhsT=kt, rhs=qTm[:, a_, h % 2, 32 * i:32 * i + 8], start=True, stop=True)
                            return ins
                        S.op("pe", fn, reads=["KTp0", "KTp0", "KT", "qTm"], writes=["pb%d" % p])

                    def s2(u):
                        pg = units[u]; p = u % 2
                        nk = 8 if pg is None else 128
                        act(ez[p][0:nk, :], pbank[p][0:nk, 0:128], AF.Exp, ["pb%d" % p], ["ez%d" % p])
                        tt("dve", ez[p][0:nk, :], ez[p][0:nk, :], ebias[0:nk, :], ALU.mult, ["ez%d" % p, "ebias"], ["ez%d" % p])
                        act(sp[p][0:nk, :], ez[p][0:nk, :], AF.Ln, ["ez%d" % p], ["sp%d" % p], bias=1.0)
                        if pg is None:
                            tt("dve", sp[p][0:8, :], sp[p][0:8, :], mnew[0:8, :], ALU.mult, ["sp%d" % p, "cattn"], ["sp%d" % p])
                        if u < n - 1:
                            tt("pool", RS32[0:nk, :], RS32[0:nk, :], sp[p][0:nk, :], ALU.add, ["RS32", "sp%d" % p], ["RS32"])
                            cp("pool", RSb[1 - p], RS32, ["RS32"], ["RSb%d" % (1 - p)])

                    def s3(u):
                        pg = units[u]; p = u % 2
                        nk = 8 if pg is None else 128
                        if pg is None:
                            mm(pbank[2 + p][0:8, 0:128], [(triL[0:8, 0:8], sp[p][0:8, :])], ["cattn", "sp%d" % p], ["pb%d" % (2 + p)])
                        else:
                            mm(pbank[2 + p][:, 0:128], [(triL, sp[p]), (onesb, RSb[p])], ["cattn", "sp%d" % p, "RSb%d" % p], ["pb%d" % (2 + p)])

                    def s4(u):
                        pg = units[u]; p = u % 2
                        nk = 8 if pg is None else 128
                        act(e1[p][0:nk, :], pbank[2 + p][0:nk, 0:128], AF.Exp, ["pb%d" % (2 + p)], ["e1%d" % p], scale=-1.0)
                        tt("dve", Ab[p][0:nk, :], ez[p][0:nk, :], e1[p][0:nk, :], ALU.mult, ["ez%d" % p, "e1%d" % p], ["Ab%d" % p])
                        if pg is None:
                            tt("dve", Ab[p][0:8, :], Ab[p][0:8, :], mnew[0:8, :], ALU.mult, ["Ab%d" % p, "cattn"], ["Ab%d" % p])

                    def s5(u):
                        pg = units[u]; p = u % 2
                        nk = 8 if pg is None else 128

                        def fn(e, p=p, nk=nk, u=u):
                            ins = None
                            for h in range(16):
                                a_, hb_ = h // 2, (h % 2) * 64
                                c_ = (i * 8 + a_) * 8
                                ins = e.matmul(pos[hb_:hb_ + 64, c_:c_ + 8], lhsT=Vbp[0][0:nk, h * 64:(h + 1) * 64], rhs=Ab[p][0:nk, h * 8:(h + 1) * 8],
                                               start=(u == 0 and h < 2), stop=(u == n - 1))
                            return ins
                        S.op("pe", fn, reads=["Vbp0", "Ab%d" % p], writes=["pos"])

                    s0(0); s1(0); s2(0); s3(0); s4(0); s5(0)
                    act(fz, pbank[2][:, 0:8], AF.Copy, ["pos", "pb2"], ["fz"])
                    if n > 1:
                        G(1)
                    for t in range(1, n + 1):
                        if t < n:
                            s0(t)
                        if t - 1 >= 1:
                            s3(t - 1); s4(t - 1); cpV(t - 1); s5(t - 1)
                        if t + 1 < n:
                            G(t + 1)
                        if t < n:
                            s1(t); s2(t)
                ov = pos[:, 0:256].rearrange("p (i a q) -> p i a q", i=4, a=8)
                gv = gT[:, :, 0:128].rearrange("p a (i r) -> p i a r", r=32)[:, :, :, 0:8]
                tt("dve", gv, ov, gv, ALU.mult, ["pos", "gT"], ["gT"])
                A.release(mR)
                phase_barrier()
                out_proj([None])
            A.release(mL)
            phase_barrier()

        XA_P, XA_S = xa[0:SEQ, :], xa[SEQ:SEQ + 128, :]
        XB_P, XB_S = xb[0:SEQ, :], xb[SEQ:SEQ + 128, :]
        XQ_P, XQ_S = xq[0:cfg.NSLOT * 128, :], xq[cfg.NSLOT * 128:cfg.NSLOT * 128 + 128, :]
        if cfg.stages == ("s5",):
            if cfg.n_s5 == 1:
                s5_layer(0, xp, xs, y_p, y_s)
            else:
                s5_layer(0, xp, xs, XA_P, XA_S)
                s5_layer(1, XA_P, XA_S, y_p, y_s)
        elif cfg.stages == ("kv", "sb"):
            kv_phase(xp, xs)
            if cfg.n_sb == 1:
                sb_layer(0, True, xp, None, xs, yq_p, y_s)
            else:
                sb_layer(0, True, xp, None, xs, XQ_P, XQ_S)
                sb_layer(1, False, None, XQ_P, XQ_S, yq_p, y_s)
        else:
            s5_layer(0, xp, xs, XA_P, XA_S)
            s5_layer(1, XA_P, XA_S, XB_P, XB_S)
            kv_phase(XB_P, XB_S)
            sb_layer(0, True, XB_P, None, XB_S, XQ_P, XQ_S)
            sb_layer(1, False, None, XQ_P, XQ_S, yq_p, y_s)
        S.finish()
        S.emit()
        print("ops", S.nops, "arena peak words", A.peak)
    return nc


def _bc(a, n=128):
    return np.broadcast_to(a[None], (n,) + a.shape)


def make_consts():
    c = {}
    c["c_ident"] = np.eye(128, dtype=np.float32).astype(ml_dtypes.bfloat16)
    rows = np.arange(128)
    colsi = np.arange(512)
    c["c_mask4"] = (((colsi[None, :] % 128) // 16) >= (rows[:, None] // 16)).astype(np.float32)
    P = np.zeros((128, 128), np.float32)
    P[rows, (rows + 64) % 128] = 1.0
    c["c_pswap"] = P
    cc = np.zeros((128, 4), np.float32)
    cc[:, 0] = 7 - rows // 16
    cc[:, 1] = np.where(rows < 64, 1.0, -1.0)
    cc[:, 2] = -cc[:, 1]
    c["c_cols"] = cc
    c["c_iota"] = np.ascontiguousarray(_bc(np.arange(512, dtype=np.float32)))
    ev = np.array([0, -1, -2, -3, -4, -5, -6, -7, 0, 1, 2, 3, 4, 5, 6, 7, 8], np.float32)
    c["c_evals"] = np.ascontiguousarray(_bc(np.repeat(ev, 64)))
    return c


def prep_core(inp, cfg, c):
    f = np.float32
    b, r = c // 2, c % 2
    SEQ = cfg.SEQ
    d = {}
    d["xp"] = np.ascontiguousarray(inp["x_prompt"][b, :SEQ])
    xs = np.zeros((128, 1024), f)
    for i in range(4):
        xs[32 * i:32 * i + 8] = inp["x_sample"][4 * c + i]
    d["xs"] = xs
    pA, pB, pAT, pBT, pCT, cols, gpost, h0 = [], [], [], [], [], [], [], []
    for l in range(2):
        Ar, Ai, ld = inp["a_A_re"][l], inp["a_A_im"][l], inp["a_log_dt"][l]
        pA.append(np.stack([_bc(Ar.reshape(-1)), _bc(Ai.reshape(-1)), _bc(np.repeat(ld, 64))], axis=1))
        Br, Bi = inp["a_B_re"][l], inp["a_B_im"][l]
        brw = np.tile(Br.transpose(2, 0, 1), (8, 1, 1)).reshape(128, 4096)
        biw = np.tile(Bi.transpose(2, 0, 1), (8, 1, 1)).reshape(128, 4096)
        pB.append(np.stack([brw, biw], axis=1))
        ArT = np.concatenate([Ar.T, Ar.T], 0)
        AiT = np.concatenate([Ai.T, Ai.T], 0)
        pAT.append(np.stack([ArT, AiT, _bc(ld)], axis=1))
        BrT, BiT = Br.transpose(1, 0, 2), Bi.transpose(1, 0, 2)
        B2 = np.concatenate([BrT, BiT], 0).reshape(128, 1024)
        B2s = np.concatenate([BiT, BrT], 0).reshape(128, 1024)
        pBT.append(np.stack([B2, B2s], axis=1))
        Cr, Ci = inp["a_C_re"][l].transpose(2, 0, 1), inp["a_C_im"][l].transpose(2, 0, 1)
        C2 = np.concatenate([Cr, Ci], 0).reshape(128, 1024)
        C2s = np.concatenate([Ci, Cr], 0).reshape(128, 1024)
        pCT.append(np.stack([C2, C2s], axis=1))
        cols.append(np.concatenate([inp["a_norm_pre"][l].reshape(8, 128).T, inp["a_D"][l].reshape(8, 128).T,
                                    inp["a_b_glu"][l].reshape(8, 128).T], axis=1))
        gpost.append(_bc(inp["a_norm_post"][l]))
        hr = inp["state_ssm_re"][l, 4 * c:4 * c + 4].transpose(2, 1, 0)
        hi = inp["state_ssm_im"][l, 4 * c:4 * c + 4].transpose(2, 1, 0)
        h0.append(np.stack([np.concatenate([hr, hi], 0).reshape(128, 256),
                            np.concatenate([hi, hr], 0).reshape(128, 256)], axis=1))
    d["s5pA"] = np.ascontiguousarray(np.stack(pA), f)
    d["s5pB"] = np.ascontiguousarray(np.stack(pB), f)
    d["s5pAT"] = np.ascontiguousarray(np.stack(pAT), f)
    d["s5pBT"] = np.ascontiguousarray(np.stack(pBT), f)
    d["s5pCT"] = np.ascontiguousarray(np.stack(pCT), f)
    d["s5cols"] = np.ascontiguousarray(np.stack(cols), f)
    d["s5gpost"] = np.ascontiguousarray(np.stack(gpost), f)
    d["s5h0"] = np.ascontiguousarray(np.stack(h0), f)
    for k in ("a_w_in", "a_w_glu", "a_w_out"):
        d[k] = np.ascontiguousarray(inp[k], f)
    return d


def prep_core_attn(inp, cfg, c, d):
    f = np.float32
    r = c % 2
    d["w_kv"] = np.ascontiguousarray(inp["w_kv"], f)
    d["b_w_in"] = np.ascontiguousarray(inp["b_w_in"], f)
    d["b_w_out"] = np.ascontiguousarray(inp["b_w_out"], f)
    d["sbcols"] = np.ascontiguousarray(np.concatenate([inp["kv_norm"].reshape(8, 128).T, inp["b_norm_pre"][0].reshape(8, 128).T,
                                                       inp["b_norm_pre"][1].reshape(8, 128).T], axis=1), f)
    d["sbgpost"] = np.ascontiguousarray(np.stack([_bc(inp["b_norm_post"][j]) for j in range(2)]), f)
    d["sbbias"] = np.ascontiguousarray(np.stack([_bc(inp["b_logit_bias"][j]) for j in range(2)]), f)
    d["sbbiasrow"] = np.ascontiguousarray(np.stack([_bc(np.repeat(inp["b_logit_bias"][j], 8)) for j in range(2)]), f)
    cc = np.zeros((128, 4), f)
    cc[:, 0] = 1.0 if r == 0 else 0.0
    cc[:, 1] = 0.0 if r == 0 else 1.0
    cc[:, 2] = np.arange(128)
    d["c_core"] = cc
    k = np.arange(128)
    triL = (k[:, None] >= k[None, :]).astype(f)
    ones = np.ones((128, 128), f)
    triT = (k[:, None] < k[None, :]).astype(f)
    zeros = np.zeros((128, 128), f)
    M0, M1 = (triT, zeros) if r == 0 else (ones, triT)
    mnew = np.zeros((128, 128), f)
    svals = np.tile(np.arange(8), 16)
    for j in range(8):
        mnew[j] = (j < svals)
    d["c_attn"] = np.ascontiguousarray(np.stack([triL, ones, M0, M1, mnew, zeros], axis=1)).astype(ml_dtypes.bfloat16)
    d["c_identf"] = np.eye(128, dtype=f)
    d["c_onesrow"] = np.ones((1, 128), f)
    if cfg.sample_attn:
        d["cache_k"] = inp["cache_k"].reshape(-1, 1024)
        d["cache_v"] = inp["cache_v"].reshape(-1, 1024)
        pt = inp["page_table"][4 * c:4 * c + 4, :cfg.NPG].reshape(-1).astype(np.int32)
        d["ptab"] = np.ascontiguousarray(_bc(pt))
    return d


_NC_CACHE = {}


def kernel(**inputs):
    cfg = Cfg()
    inp = {k: np.asarray(v) for k, v in inputs.items()}
    if "nc" not in _NC_CACHE:
        _NC_CACHE["nc"] = build(cfg)
    nc = _NC_CACHE["nc"]
    consts = make_consts()
    in_maps = []
    for c in range(8):
        d = prep_core(inp, cfg, c)
        prep_core_attn(inp, cfg, c, d)
        d.update(consts)
        in_maps.append(d)
    res = run_bass_kernel_spmd(nc, in_maps, core_ids=list(range(8)))
    R = res.results
    f = np.float32
    SEQ, NT = cfg.SEQ, cfg.NT
    y_prompt = np.empty((4, NT, 128, 1024), f)
    y_sample = np.empty((32, 8, 1024), f)
    re_p = np.empty((2, 4, 64, 64), f)
    im_p = np.empty((2, 4, 64, 64), f)
    k_prompt = np.empty((4, SEQ, 1024), f)
    v_prompt = np.empty((4, SEQ, 1024), f)
    re_s = np.empty((2, 32, 64, 64), f)
    im_s = np.empty((2, 32, 64, 64), f)
    k_sample = np.empty((32, 8, 1024), f)
    v_sample = np.empty((32, 8, 1024), f)
    for c in range(8):
        b, r = c // 2, c % 2
        o = R[c]
        y_prompt[b, r::2] = np.asarray(o["yq_p"]).reshape(cfg.NSLOT, 128, 1024)
        ys, ks, vs = np.asarray(o["y_s"]), np.asarray(o["k_s"]), np.asarray(o["v_s"])
        for i in range(4):
            y_sample[4 * c + i] = ys[32 * i:32 * i + 8]
            k_sample[4 * c + i] = ks[32 * i:32 * i + 8]
            v_sample[4 * c + i] = vs[32 * i:32 * i + 8]
        fs = np.asarray(o["fin_s"]).reshape(2, 128, 64, 4)
        for l in range(2):
            re_s[l, 4 * c:4 * c + 4] = fs[l, :64].transpose(2, 1, 0)
            im_s[l, 4 * c:4 * c + 4] = fs[l, 64:].transpose(2, 1, 0)
        if r == 0:
            fp = np.asarray(o["fin_p"])
            for l in range(2):
                re_p[l, b] = fp[l, :64].T
                im_p[l, b] = fp[l, 64:].T
            k_prompt[b] = np.asarray(o["k_p"])
            v_prompt[b] = np.asarray(o["v_p"])
    return (y_prompt.reshape(4, SEQ, 1024), y_sample, re_p, im_p,
            k_prompt.reshape(4, SEQ, 16, 64), v_prompt.reshape(4, SEQ, 16, 64),
            re_s, im_s, k_sample.reshape(32, 8, 16, 64), v_sample.reshape(32, 8, 16, 64))
```
